# Optimizing a Trainium2 kernel written in Bass

```python
import jax, jax.numpy as jnp
from jax import lax
import numpy as np

D_MODEL = 1024
BATCH = 2
SEQ = 8192
DEPTH = 2
DEC_BATCH = 32
DEC_SEQ = 1
PAST_LEN = 8192
PAGE_SIZE = 128

N_MIXERS = 2
N_FOX_LAYERS = (DEPTH + 1) // 2
N_GLA_LAYERS = DEPTH // 2
FOX_HEADS = 16
FOX_HEAD_DIM = D_MODEL // FOX_HEADS
FOX_Q_BLOCK = 128
FOX_FGATE_BIAS = 3.0
GLA_HEADS = 4
GLA_DK = D_MODEL // 2
GLA_DV = D_MODEL
GLA_DK_H = GLA_DK // GLA_HEADS
GLA_DV_H = GLA_DV // GLA_HEADS
GLA_RANK = 16
GLA_TAU = 16.0
GLA_CHUNK = 64
D_FF = ((8 * D_MODEL // 3 + 127) // 128) * 128
DN_ALPHA = (2 * DEPTH) ** 0.25
DN_BETA = (8 * DEPTH) ** -0.25
LN_EPS = 1e-5
MASK_VALUE = -1e30

kernel_name = "fox_gla_macaron_deepnorm_step"


def layer_norm(x, g, b):
    xf = x.astype(jnp.float32)
    mu = jnp.mean(xf, axis=-1, keepdims=True)
    var = jnp.mean(jnp.square(xf - mu), axis=-1, keepdims=True)
    y = (xf - mu) * lax.rsqrt(var + LN_EPS)
    return (y * g.astype(jnp.float32) + b.astype(jnp.float32)).astype(x.dtype)


def swiglu(x, w_in, w_out):
    g, u = jnp.split(x @ w_in, 2, axis=-1)
    return (jax.nn.silu(g) * u) @ w_out


def macaron_half(x, w_in, w_out, g, b):
    return layer_norm(DN_ALPHA * x + 0.5 * swiglu(x, w_in, w_out), g, b)


def fox_project(x, w_in, b_f):
    B, T, _ = x.shape
    q, k, v, f = jnp.split(x @ w_in, [D_MODEL, 2 * D_MODEL, 3 * D_MODEL], axis=-1)
    hs = (B, T, FOX_HEADS, FOX_HEAD_DIM)
    logf = jax.nn.log_sigmoid((f + b_f).astype(jnp.float32))
    return q.reshape(hs), k.reshape(hs), v.reshape(hs), logf


def fox_prompt(q, k, v, logf):
    B, T, H, HD = q.shape
    scale = HD ** -0.5
    c_k = jnp.cumsum(logf, axis=1).transpose(0, 2, 1)
    k_pos = jnp.arange(T)

    def block(i):
        start = i * FOX_Q_BLOCK
        q_blk = lax.dynamic_slice_in_dim(q, start, FOX_Q_BLOCK, axis=1)
        c_q = lax.dynamic_slice_in_dim(c_k, start, FOX_Q_BLOCK, axis=2)
        logits = (jnp.einsum('bqhd,bkhd->bhqk', q_blk, k).astype(jnp.float32) * scale
                  + (c_q[..., None] - c_k[:, :, None, :]))
        q_pos = start + jnp.arange(FOX_Q_BLOCK)
        logits = jnp.where(k_pos[None, :] <= q_pos[:, None], logits, MASK_VALUE)
        p = jax.nn.softmax(logits, axis=-1).astype(v.dtype)
        return jnp.einsum('bhqk,bkhd->bqhd', p, v)

    out = lax.map(block, jnp.arange(T // FOX_Q_BLOCK))
    return out.transpose(1, 0, 2, 3, 4).reshape(B, T, H * HD)


def fox_sample(q, k, v, logf, ck, cv, clogf, page_table):
    DB, S, H, HD = q.shape
    scale = HD ** -0.5
    past_k = ck[page_table].reshape(DB, -1, H, HD)
    past_v = cv[page_table].reshape(DB, -1, H, HD)
    past_lf = clogf[page_table].reshape(DB, -1, H).astype(jnp.float32)
    past_len = past_k.shape[1]
    keys = jnp.concatenate([past_k, k.astype(past_k.dtype)], axis=1)
    vals = jnp.concatenate([past_v, v.astype(past_v.dtype)], axis=1)
    lf = jnp.concatenate([past_lf, logf], axis=1)
    csum = jnp.cumsum(lf, axis=1).transpose(0, 2, 1)
    c_q = csum[:, :, past_len:]
    logits = (jnp.einsum('bqhd,bkhd->bhqk', q.astype(keys.dtype), keys).astype(jnp.float32) * scale
              + (c_q[..., None] - csum[:, :, None, :]))
    q_pos = past_len + jnp.arange(S)
    k_pos = jnp.arange(keys.shape[1])
    logits = jnp.where(k_pos[None, :] <= q_pos[:, None], logits, MASK_VALUE)
    p = jax.nn.softmax(logits, axis=-1).astype(vals.dtype)
    out = jnp.einsum('bhqk,bkhd->bqhd', p, vals)
    return out.reshape(DB, S, H * HD).astype(q.dtype)


def gla_project(x, w_in, w_a2, b_a):
    B, T, _ = x.shape
    q, k, v, r, a_lr = jnp.split(
        x @ w_in, [GLA_DK, 2 * GLA_DK, 2 * GLA_DK + GLA_DV, 2 * GLA_DK + 2 * GLA_DV], axis=-1)
    q = q.reshape(B, T, GLA_HEADS, GLA_DK_H) * (GLA_DK_H ** -0.5)
    k = k.reshape(B, T, GLA_HEADS, GLA_DK_H)
    v = v.reshape(B, T, GLA_HEADS, GLA_DV_H)
    log_a = jax.nn.log_sigmoid((a_lr @ w_a2 + b_a).astype(jnp.float32)) / GLA_TAU
    return q, k, v, r, log_a.reshape(B, T, GLA_HEADS, GLA_DK_H)


def gla_chunked(q, k, v, log_a, s0, chunk):
    B, T, H, DK = q.shape
    DV = v.shape[-1]
    N = T // chunk

    def to_chunks(a):
        return a.astype(jnp.float32).reshape(B, N, chunk, H, a.shape[-1]).transpose(1, 0, 3, 2, 4)

    qc, kc, vc = to_chunks(q), to_chunks(k), to_chunks(v)
    bc = jnp.cumsum(to_chunks(log_a), axis=3)
    b_last = bc[:, :, :, -1:, :]
    q_dec = qc * jnp.exp(bc)
    k_inv = kc * jnp.exp(-bc)
    k_end = kc * jnp.exp(b_last - bc)
    tril = jnp.tril(jnp.ones((chunk, chunk), dtype=bool))
    attn = jnp.where(tril, jnp.einsum('nbhid,nbhjd->nbhij', q_dec, k_inv), 0.0)
    o_intra = jnp.einsum('nbhij,nbhjv->nbhiv', attn, vc)

    def step(s, xs):
        qd, ke, vv, bl = xs
        o_inter = jnp.einsum('bhid,bhdv->bhiv', qd, s)
        s_new = jnp.exp(bl[:, :, 0, :])[..., None] * s + jnp.einsum('bhjd,bhjv->bhdv', ke, vv)
        return s_new, o_inter

    s_final, o_inter = lax.scan(step, s0.astype(jnp.float32), (q_dec, k_end, vc, b_last))
    o = (o_intra + o_inter).transpose(1, 0, 3, 2, 4).reshape(B, T, H, DV)
    return o, s_final


def gla_output(o, r, g, w_o):
    mu = jnp.mean(o, axis=-1, keepdims=True)
    var = jnp.mean(jnp.square(o - mu), axis=-1, keepdims=True)
    on = ((o - mu) * lax.rsqrt(var + LN_EPS)).reshape(o.shape[0], o.shape[1], GLA_DV)
    on = (on * g.astype(jnp.float32)).astype(r.dtype)
    return (on * jax.nn.silu(r)) @ w_o


def setup_inputs(seed: int = 0) -> dict:
    key = jax.random.key(seed)
    ks = jax.random.split(key, 20)
    n_pages = PAST_LEN // PAGE_SIZE
    n_pool = (DEC_BATCH * n_pages * 5) // 4
    f32 = jnp.float32
    nrm = lambda k, s: jax.random.normal(k, s, dtype=f32)

    x_prompt = nrm(ks[0], (BATCH, SEQ, D_MODEL))
    x_sample = nrm(ks[1], (DEC_BATCH, DEC_SEQ, D_MODEL))
    cache_fox_k = nrm(ks[2], (N_FOX_LAYERS, n_pool, PAGE_SIZE, FOX_HEADS, FOX_HEAD_DIM))
    cache_fox_v = DN_BETA * nrm(ks[3], (N_FOX_LAYERS, n_pool, PAGE_SIZE, FOX_HEADS, FOX_HEAD_DIM))
    cache_fox_logf = jax.nn.log_sigmoid(
        FOX_FGATE_BIAS + nrm(ks[4], (N_FOX_LAYERS, n_pool, PAGE_SIZE, FOX_HEADS)))
    state_gla = 0.5 * nrm(ks[5], (N_GLA_LAYERS, DEC_BATCH, GLA_HEADS, GLA_DK_H, GLA_DV_H))
    page_table = jax.random.permutation(ks[6], n_pool)[:DEC_BATCH * n_pages].reshape(
        DEC_BATCH, n_pages).astype(jnp.int32)

    ln_g = 1.0 + 0.02 * nrm(ks[7], (DEPTH, 3, D_MODEL))
    ln_b = 0.02 * nrm(ks[8], (DEPTH, 3, D_MODEL))
    ffn_w_in = nrm(ks[9], (DEPTH, 2, D_MODEL, 2 * D_FF)) * D_MODEL ** -0.5
    ffn_w_out = nrm(ks[10], (DEPTH, 2, D_FF, D_MODEL)) * (D_FF ** -0.5 * DN_BETA)

    fox_cols = jnp.concatenate([jnp.ones((2 * D_MODEL,), f32), jnp.full((D_MODEL,), DN_BETA, f32),
                                jnp.ones((FOX_HEADS,), f32)])
    fox_w_in = nrm(ks[11], (N_FOX_LAYERS, D_MODEL, 3 * D_MODEL + FOX_HEADS)) * D_MODEL ** -0.5 * fox_cols
    fox_b_f = FOX_FGATE_BIAS + 0.1 * nrm(ks[12], (N_FOX_LAYERS, FOX_HEADS))
    fox_w_o = nrm(ks[13], (N_FOX_LAYERS, D_MODEL, D_MODEL)) * (D_MODEL ** -0.5 * DN_BETA)

    gla_cols = jnp.concatenate([jnp.ones((2 * GLA_DK,), f32), jnp.full((GLA_DV,), DN_BETA, f32),
                                jnp.ones((GLA_DV + GLA_RANK,), f32)])
    gla_w_in = nrm(ks[14], (N_GLA_LAYERS, D_MODEL, 2 * GLA_DK + 2 * GLA_DV + GLA_RANK)) * D_MODEL ** -0.5 * gla_cols
    gla_w_a2 = nrm(ks[15], (N_GLA_LAYERS, GLA_RANK, GLA_DK)) * GLA_RANK ** -0.5
    gla_b_a = 0.1 * nrm(ks[16], (N_GLA_LAYERS, GLA_DK))
    gla_norm_g = 1.0 + 0.02 * nrm(ks[17], (N_GLA_LAYERS, GLA_DV))
    gla_w_o = nrm(ks[18], (N_GLA_LAYERS, GLA_DV, D_MODEL)) * (GLA_DV ** -0.5 * DN_BETA)

    return {"x_prompt": x_prompt, "x_sample": x_sample,
            "cache_fox_k": cache_fox_k, "cache_fox_v": cache_fox_v, "cache_fox_logf": cache_fox_logf,
            "state_gla": state_gla, "page_table": page_table,
            "ln_g": ln_g, "ln_b": ln_b, "ffn_w_in": ffn_w_in, "ffn_w_out": ffn_w_out,
            "fox_w_in": fox_w_in, "fox_b_f": fox_b_f, "fox_w_o": fox_w_o,
            "gla_w_in": gla_w_in, "gla_w_a2": gla_w_a2, "gla_b_a": gla_b_a,
            "gla_norm_g": gla_norm_g, "gla_w_o": gla_w_o}


def reference(x_prompt, x_sample, cache_fox_k, cache_fox_v, cache_fox_logf, state_gla, page_table,
              ln_g, ln_b, ffn_w_in, ffn_w_out, fox_w_in, fox_b_f, fox_w_o,
              gla_w_in, gla_w_a2, gla_b_a, gla_norm_g, gla_w_o):
    xp, xs = x_prompt, x_sample
    kp_l, vp_l, lfp_l, ks_l, vs_l, lfs_l = [], [], [], [], [], []
    sgp_l, sgs_l = [], []
    for i in range(DEPTH):
        j = i // N_MIXERS
        xp = macaron_half(xp, ffn_w_in[i, 0], ffn_w_out[i, 0], ln_g[i, 0], ln_b[i, 0])
        xs = macaron_half(xs, ffn_w_in[i, 0], ffn_w_out[i, 0], ln_g[i, 0], ln_b[i, 0])
        if i % N_MIXERS == 0:
            qp, kp, vp, lfp = fox_project(xp, fox_w_in[j], fox_b_f[j])
            qs, kss, vss, lfs = fox_project(xs, fox_w_in[j], fox_b_f[j])
            mp = fox_prompt(qp, kp, vp, lfp) @ fox_w_o[j]
            ms = fox_sample(qs, kss, vss, lfs, cache_fox_k[j], cache_fox_v[j], cache_fox_logf[j],
                            page_table) @ fox_w_o[j]
            kp_l.append(kp); vp_l.append(vp); lfp_l.append(lfp)
            ks_l.append(kss); vs_l.append(vss); lfs_l.append(lfs)
        else:
            qp, kp, vp, rp, lap = gla_project(xp, gla_w_in[j], gla_w_a2[j], gla_b_a[j])
            qs, kss, vss, rs, las = gla_project(xs, gla_w_in[j], gla_w_a2[j], gla_b_a[j])
            s0p = jnp.zeros((xp.shape[0], GLA_HEADS, GLA_DK_H, GLA_DV_H), jnp.float32)
            op, sp = gla_chunked(qp, kp, vp, lap, s0p, GLA_CHUNK)
            os_, ss = gla_chunked(qs, kss, vss, las, state_gla[j], xs.shape[1])
            mp = gla_output(op, rp, gla_norm_g[j], gla_w_o[j])
            ms = gla_output(os_, rs, gla_norm_g[j], gla_w_o[j])
            sgp_l.append(sp); sgs_l.append(ss)
        xp = layer_norm(DN_ALPHA * xp + mp, ln_g[i, 1], ln_b[i, 1])
        xs = layer_norm(DN_ALPHA * xs + ms, ln_g[i, 1], ln_b[i, 1])
        xp = macaron_half(xp, ffn_w_in[i, 1], ffn_w_out[i, 1], ln_g[i, 2], ln_b[i, 2])
        xs = macaron_half(xs, ffn_w_in[i, 1], ffn_w_out[i, 1], ln_g[i, 2], ln_b[i, 2])
    new_k_prompt = jnp.stack(kp_l, axis=0)
    new_v_prompt = jnp.stack(vp_l, axis=0)
    new_logf_prompt = jnp.stack(lfp_l, axis=0)
    new_k_sample = jnp.stack(ks_l, axis=0)
    new_v_sample = jnp.stack(vs_l, axis=0)
    new_logf_sample = jnp.stack(lfs_l, axis=0)
    state_gla_prompt = jnp.stack(sgp_l, axis=0)
    state_gla_sample = jnp.stack(sgs_l, axis=0)
    return (xp, xs, new_k_prompt, new_v_prompt, new_logf_prompt,
            new_k_sample, new_v_sample, new_logf_sample, state_gla_prompt, state_gla_sample)
```

```python
import contextlib
import os
import numpy as np
import concourse.bass as bass
import concourse.mybir as mybir
from concourse.bass_utils import run_bass_kernel_spmd

F32 = mybir.dt.float32
BF = mybir.dt.bfloat16
I32 = mybir.dt.int32
AF = mybir.ActivationFunctionType
ALU = mybir.AluOpType
AX = mybir.AxisListType

D = 1024
NP_ = 2048
NS = 4
NT = NP_ + NS
DFF = 2816
H = 16
HD = 64
ALPHA = 4.0 ** 0.25
EPS = 1e-5
GH = 4
GDK = 128
GDV = 256
GC = 64
NPOOL = 2560
PEN = -30000.0

PASSES = [
    dict(c0=0, groups=[(0, 512), (512, 512)], subs=[(i * 128, 128) for i in range(8)]),
    dict(c0=1024, groups=[(1024, 512), (1536, 512), (2048, 4)],
         subs=[(1024 + i * 128, 128) for i in range(8)] + [(2048, 4)]),
]


class _Stop(Exception):
    pass


_TRS = []


def chk(level):
    if int(os.environ.get('KSTOP', '0')) == level:
        _TRS[-1].finish()
        _TRS[-1].stopped = True


class TR:
    def __init__(self, nc, es):
        self.nc = nc
        self.es = es
        self.E = {'pe': nc.tensor, 'act': nc.scalar, 'dve': nc.vector, 'pool': nc.gpsimd, 'sp': nc.sync}
        self.esem = {}
        self.ecnt = {}
        self.nsem = 0
        for e in self.E:
            self._newsem(e)
        self.waited = {e: {} for e in self.E}
        self.lastw = {}
        self.readers = {}
        self.dsem = {}
        self.dcnt = {}
        self.allsems = []
        self.freed = []
        self.psi = 0
        self.stopped = False
        _TRS.append(self)

    def _newsem(self, e):
        self.nsem += 1
        s = self.es.enter_context(self.nc.semaphore("se_%s_%d" % (e, self.nsem)))
        self.esem[e] = s
        self.ecnt[e] = 0

    def _need(self, eng, toks):
        for (sem, val, owner) in toks:
            if eng == 'pe' and owner == 'pe':
                continue
            w = self.waited[eng]
            k = id(sem)
            if w.get(k, 0) >= val:
                continue
            self.E[eng].wait_ge(sem, val)
            w[k] = val

    def deps(self, eng, reads, writes):
        toks = []
        for k in reads:
            if k in self.lastw:
                toks.append(self.lastw[k])
        for k in writes:
            if k in self.lastw:
                toks.append(self.lastw[k])
            toks += list(self.readers.get(k, {}).values())
        self._need(eng, toks)

    def commit(self, tok, reads, writes):
        for k in reads:
            self.readers.setdefault(k, {})[id(tok[0])] = tok
        for k in writes:
            self.lastw[k] = tok
            self.readers[k] = {}

    def op(self, eng, fn, reads=(), writes=()):
        if self.stopped:
            return None
        self.deps(eng, reads, writes)
        ins = fn(self.E[eng])
        if self.ecnt[eng] >= 30000:
            self._newsem(eng)
        self.ecnt[eng] += 1
        ins.then_inc(self.esem[eng], 1)
        tok = (self.esem[eng], self.ecnt[eng], eng)
        self.commit(tok, reads, writes)
        return tok

    def _dsem(self, key):
        if key not in self.dsem:
            if self.freed:
                sem, cnt = self.freed.pop()
                self.dsem[key] = sem
                self.dcnt[key] = cnt
            else:
                self.nsem += 1
                self.dsem[key] = self.es.enter_context(self.nc.semaphore("sd_%d" % self.nsem))
                self.dcnt[key] = 0
        return self.dsem[key]

    def dma(self, q, out, in_, reads=(), writes=(), semkey=None, inc=16, fn=None, **kw):
        if self.stopped:
            return None
        self.deps(q, reads, writes)
        sem = self._dsem(semkey)
        if fn is None:
            ins = self.E[q].dma_start(out=out, in_=in_, **kw)
        else:
            ins = fn(self.E[q])
        self.dcnt[semkey] += inc
        ins.then_inc(sem, inc)
        tok = (sem, self.dcnt[semkey], 'dma')
        self.commit(tok, reads, writes)
        return tok

    def barrier(self):
        if self.stopped:
            return
        toks = [(self.esem[e], self.ecnt[e], e) for e in self.E if self.ecnt[e] > 0]
        toks += [(self.dsem[k], self.dcnt[k], 'dma') for k in self.dsem if self.dcnt[k] > 0]
        for e in self.E:
            self._need(e, [t for t in toks if not (t[2] == e and e != 'pe')] if e != 'pe' else [t for t in toks if t[2] != 'pe'])
        for k in list(self.dsem):
            self.freed.append((self.dsem[k], self.dcnt[k]))
            del self.dsem[k]
            del self.dcnt[k]

    def finish(self):
        if self.stopped:
            return
        toks = [(self.esem[e], self.ecnt[e], e) for e in self.E if self.ecnt[e] > 0 and e != 'sp']
        toks += [(self.dsem[k], self.dcnt[k], 'dma') for k in self.dsem if self.dcnt[k] > 0]
        toks += [(sm, c, 'dma') for (sm, c) in self.freed if c > 0]
        self._need('sp', toks)


def build(ncores=8, NPOOL=NPOOL):
    nc = bass.Bass("TRN2", target_bir_lowering=False, num_devices=ncores)
    dt = nc.dram_tensor

    def din(name, shape, dtype=F32):
        return dt(name, list(shape), dtype, kind="ExternalInput").ap()

    def dout(name, shape, dtype=F32):
        return dt(name, list(shape), dtype, kind="ExternalOutput").ap()

    def dint(name, shape, dtype=F32):
        return dt(name, list(shape), dtype, kind="Internal").ap()

    x_in = din("x_in", [NT, D])
    ck = din("cache_k", [NPOOL * 64, 2 * D])
    cv = din("cache_v", [NPOOL * 64, 2 * D])
    clf = din("cache_lf", [NPOOL * 2, 64 * H])
    sgla = din("state_gla", [NS, GH, GDK, GDV])
    ptab = din("page_table", [NS, 64], I32)
    ln_g = din("ln_g", [2, 3, D])
    ln_b = din("ln_b", [2, 3, D])
    ffn_w_in = din("ffn_w_in", [2, 2, D, 2 * DFF])
    ffn_w_out = din("ffn_w_out", [2, 2, DFF, D])
    fox_w_in = din("fox_w_in", [D, 3 * D + H])
    fox_b_f = din("fox_b_f", [1, H])
    fox_w_o = din("fox_w_o", [D, D])
    gla_w_in = din("gla_w_in", [D, 3088])
    gla_w_a2 = din("gla_w_a2", [16, 512])
    gla_b_a = din("gla_b_a", [1, 512])
    gla_norm_g = din("gla_norm_g", [1, D])
    gla_w_o = din("gla_w_o", [D, D])
    c_ident = din("c_ident", [128, 128])
    c_tri = din("c_tri", [128, 128])
    c_slow = din("c_slow", [128, 128])
    c_rank = din("c_rank", [128, 80])
    o_y = dout("o_y", [NT, D])
    o_k = dout("o_k", [NT, D])
    o_v = dout("o_v", [NT, D])
    o_lf = dout("o_lf", [NT, H])
    o_sp = dout("o_sp", [GH * GDK, GDV])
    o_ss = dout("o_ss", [NS * GH * GDK, GDV])
    xres = dint("xres", [NT, D])
    qT_d = dint("qT_d", [H, HD, NP_], BF)
    ktloc_c = [dint("ktloc%d" % i, [140, NP_], BF) for i in range(8)]
    ktg_c = [dint("ktg%d" % i, [560, NP_], BF) for i in range(8)]
    vloc_c = [dint("vloc%d" % i, [256, 1040], BF) for i in range(8)]
    vg_c = [dint("vg%d" % i, [1024, 1040], BF) for i in range(8)]
    totloc_f = dint("totloc", [1, 128])
    totloc = totloc_f[:, 0:H]
    totg_f = dint("totg", [4, 128])
    totg = totg_f[:, 0:H]
    qaug_d = dint("qaug_d", [5, H, 6, NP_], BF)
    srow_d = dint("srow_d", [NS, 3 * D + 2 * H])
    gq_d = dint("gq_d", [NT, 512])
    sloc = dint("sloc", [GH * GDK, GDV + 64])
    sg_ = dint("sgath", [4 * GH * GDK, GDV + 64])

    _uc = [0]

    def un(name):
        _uc[0] += 1
        return "%s_%d" % (name, _uc[0])

    es = contextlib.ExitStack()
    with es:
        T = TR(nc, es)
        sb = lambda name, shape, dtype=F32: es.enter_context(nc.sbuf_tensor(un(name), list(shape), dtype))
        PS = [es.enter_context(nc.psum_tensor("psb%d" % i, [128, 512], F32)) for i in range(8)]

        PSN = [6]

        def psum():
            T.psi = (T.psi + 1) % PSN[0]
            i = T.psi
            return i, PS[i], 'ps%d' % i

        def psum_r(i):
            return i, PS[i], 'ps%d' % i

        identf = sb("identf", [128, 128]); identb = sb("identb", [128, 128], BF)
        trib = sb("trib", [128, 128], BF); slowf = sb("slowf", [128, 128])
        onesf = sb("onesf", [128, 512]); onesb = sb("onesb", [128, 128], BF)
        crank = sb("crank", [128, 80])
        trif = sb("trif", [128, 128])
        T.dma('sp', identf[:], c_ident, writes=['identf'], semkey='c0')
        T.dma('sp', trif[:], c_tri, writes=['trif'], semkey='c1')
        T.dma('sp', slowf[:], c_slow, writes=['slowf'], semkey='c2')
        T.dma('sp', crank[:], c_rank, writes=['crank'], semkey='c3')
        T.op('dve', lambda e: e.tensor_copy(out=identb[:], in_=identf[:]), reads=['identf'], writes=['identb'])
        T.op('dve', lambda e: e.tensor_copy(out=trib[:], in_=trif[:]), reads=['trif'], writes=['trib'])
        T.op('dve', lambda e: e.memset(onesf[:], 1.0), writes=['onesf'])
        mbb = sb("mbb", [128, 128], BF)
        T.op('dve', lambda e: e.tensor_scalar(out=mbb[:], in0=slowf[:], scalar1=PEN, scalar2=None, op0=ALU.mult), reads=['slowf'], writes=['mbb'])
        T.op('dve', lambda e: e.memset(onesb[:], 1.0), writes=['onesb'])

        class _W:
            pass
        WKH = _W()

        def alloc_work(stack):
            a = lambda name, shape, dtype=F32: stack.enter_context(nc.sbuf_tensor(un(name), list(shape), dtype))
            WKH.xT = a("xT", [128, 8, 1028], BF)
            WKH.ld_f = [a("ld_f%d" % i, [128, D]) for i in range(2)]
            WKH.ld_b = [a("ld_b%d" % i, [128, D], BF) for i in range(2)]
            WKH.Gt = a("Gt", [128, D]); WKH.Bt = a("Bt", [128, D])
            WKH.zt = a("zt", [128, D]); WKH.xo = [a("xo%d" % i, [128, D]) for i in range(2)]
            WKH.junk = a("junk", [128, D])
            WKH.st_ = a("stats", [128, 16])
            T.barrier()
        cnt = dict(ld=0, xo=0, misc=0)

        def load_xT(pas, src):
            xT, ld_f, ld_b, Gt, Bt, zt, xo, junk, st_ = WKH.xT, WKH.ld_f, WKH.ld_b, WKH.Gt, WKH.Bt, WKH.zt, WKH.xo, WKH.junk, WKH.st_
            c0 = pas['c0']
            for (col, n) in pas['subs']:
                s = cnt['ld'] % 2
                cnt['ld'] += 1
                T.dma('sp', ld_f[s][:n, :], src[col:col + n, :], reads=['d:xres:%d' % col], writes=['ld_f%d' % s],
                      semkey='ld_f%d' % s)
                T.op('act', lambda e: e.copy(out=ld_b[s][:n, :], in_=ld_f[s][:n, :]), reads=['ld_f%d' % s],
                     writes=['ld_b%d' % s])
                pi, pt, pk = psum()
                ptb = pt[:].bitcast(BF)
                for dc in range(8):
                    T.op('pe', lambda e: e.transpose(out=ptb[:, dc * 128:dc * 128 + n], in_=ld_b[s][:n, dc * 128:(dc + 1) * 128],
                                                     identity=identb[:n, :n]),
                         reads=['ld_b%d' % s, 'identb'], writes=[pk])
                src_v = ptb[:, 0:1024].rearrange("p (c t) -> p c t", c=8)[:, :, 0:n]
                T.op('dve', lambda e: e.tensor_copy(out=xT[:, :, col - c0:col - c0 + n], in_=src_v), reads=[pk],
                     writes=['xT'])

        def ln_stage(col, n, ybanks, res_src, gi, gk, dst, gb_loaded):
            xT, ld_f, ld_b, Gt, Bt, zt, xo, junk, st_ = WKH.xT, WKH.ld_f, WKH.ld_b, WKH.Gt, WKH.Bt, WKH.zt, WKH.xo, WKH.junk, WKH.st_
            if not gb_loaded:
                T.dma('sp', Gt[:], ln_g[gi, gk:gk + 1, :].partition_broadcast(128), writes=['Gt'], semkey='Gt')
                T.dma('sp', Bt[:], ln_b[gi, gk:gk + 1, :].partition_broadcast(128), writes=['Bt'], semkey='Bt')
            s = cnt['ld'] % 2
            cnt['ld'] += 1
            xr = ld_f[s]
            T.dma('sp', xr[:n, :], res_src[col:col + n, :], reads=['d:xres:%d' % col], writes=['ld_f%d' % s],
                  semkey='ld_f%d' % s)
            for hf in range(2):
                (yp, yk) = ybanks[hf]
                T.op('dve', lambda e: e.scalar_tensor_tensor(out=zt[:n, hf * 512:(hf + 1) * 512], in0=xr[:n, hf * 512:(hf + 1) * 512],
                                                             scalar=ALPHA, in1=yp[:n, :], op0=ALU.mult, op1=ALU.add,
                                                             accum_out=st_[:n, hf:hf + 1]),
                     reads=['ld_f%d' % s, yk], writes=['zt', 'st'])
            T.op('act', lambda e: e.activation(out=junk[:n, :], in_=zt[:n, :], func=AF.Square, accum_out=st_[:n, 2:3]),
                 reads=['zt'], writes=['junk', 'st2'])
            T.op('dve', lambda e: e.tensor_tensor(out=st_[:n, 3:4], in0=st_[:n, 0:1], in1=st_[:n, 1:2], op=ALU.add),
                 reads=['st'], writes=['st'])
            T.op('dve', lambda e: e.tensor_scalar(out=st_[:n, 4:5], in0=st_[:n, 3:4], scalar1=1.0 / D, scalar2=None, op0=ALU.mult),
                 reads=['st'], writes=['st'])
            T.op('dve', lambda e: e.tensor_tensor(out=st_[:n, 5:6], in0=st_[:n, 4:5], in1=st_[:n, 4:5], op=ALU.mult),
                 reads=['st'], writes=['st'])
            T.op('dve', lambda e: e.scalar_tensor_tensor(out=st_[:n, 6:7], in0=st_[:n, 2:3], scalar=1.0 / D, in1=st_[:n, 5:6],
                                                         op0=ALU.mult, op1=ALU.subtract),
                 reads=['st', 'st2'], writes=['st'])
            T.op('dve', lambda e: e.tensor_scalar(out=st_[:n, 6:7], in0=st_[:n, 6:7], scalar1=EPS, scalar2=None, op0=ALU.add),
                 reads=['st'], writes=['st'])
            T.op('act', lambda e: e.activation(out=st_[:n, 7:8], in_=st_[:n, 6:7], func=AF.Sqrt), reads=['st'], writes=['st3'])
            T.op('dve', lambda e: e.reciprocal(out=st_[:n, 8:9], in_=st_[:n, 7:8]), reads=['st3'], writes=['st'])
            T.op('dve', lambda e: e.tensor_scalar(out=zt[:n, :], in0=zt[:n, :], scalar1=st_[:n, 4:5], scalar2=st_[:n, 8:9],
                                                  op0=ALU.subtract, op1=ALU.mult), reads=['zt', 'st'], writes=['zt'])
            T.op('dve', lambda e: e.tensor_tensor(out=zt[:n, :], in0=zt[:n, :], in1=Gt[:n, :], op=ALU.mult),
                 reads=['zt', 'Gt'], writes=['zt'])
            o = cnt['xo'] % 2
            cnt['xo'] += 1
            T.op('dve', lambda e: e.tensor_tensor(out=xo[o][:n, :], in0=zt[:n, :], in1=Bt[:n, :], op=ALU.add),
                 reads=['zt', 'Bt'], writes=['xo%d' % o])
            T.dma('sp', dst[col:col + n, :], xo[o][:n, :], reads=['xo%d' % o], writes=['d:xres:%d' % col], semkey='xo%d_st' % o)

        def ffn_stage(li, hi, gk, src, dst):
            xT, ld_f, ld_b, Gt, Bt, zt, xo, junk, st_ = WKH.xT, WKH.ld_f, WKH.ld_b, WKH.Gt, WKH.Bt, WKH.zt, WKH.xo, WKH.junk, WKH.st_
            with contextlib.ExitStack() as ls:
                lsb = lambda name, shape, dtype=F32: ls.enter_context(nc.sbuf_tensor(un(name), list(shape), dtype))
                aT = lsb("aT", [128, 22, 1028], BF)
                WOUT = lsb("WOUT", [128, 22, D], BF)
                WIN = [lsb("WIN%d" % i, [128, 8, 2, 256], BF) for i in range(2)]
                sg = [lsb("sg%d" % i, [128, 512]) for i in range(2)]
                w_in = ffn_w_in[li, hi].rearrange("(c p) f -> p c f", p=128)
                w_out = ffn_w_out[li, hi].rearrange("(c p) d -> p c d", p=128)
                first = True
                for pas in PASSES:
                    c0 = pas['c0']
                    load_xT(pas, src)

                    def load_w(jc):
                        s = jc % 2
                        for gu in range(2):
                            T.dma('pool', WIN[s][:, :, gu, :], w_in[:, :, gu * DFF + jc * 256: gu * DFF + (jc + 1) * 256],
                                  writes=['WIN%d_%d' % (s, gu)], semkey='WIN%d_%d' % (s, gu))
                        T.dma('pool', WOUT[:, 2 * jc:2 * jc + 2, :], w_out[:, 2 * jc:2 * jc + 2, :], writes=['WOUT%d' % jc],
                              semkey='WOUT%d' % jc)

                    load_w(0)
                    k = 0
                    for jc in range(11):
                        if jc + 1 < 11:
                            load_w(jc + 1)
                        s = jc % 2
                        for sub in range(2):
                            f = 2 * jc + sub
                            for (gc, gn) in pas['groups']:
                                lc = gc - c0
                                _, pg, pgk = psum()
                                _, pu, puk = psum()
                                for dc in range(8):
                                    T.op('pe', lambda e: e.matmul(pg[:, :gn], lhsT=WIN[s][:, dc, 0, sub * 128:(sub + 1) * 128],
                                                                  rhs=xT[:, dc, lc:lc + gn], start=(dc == 0), stop=(dc == 7)),
                                         reads=['WIN%d_0' % s, 'xT'], writes=[pgk])
                                for dc in range(8):
                                    T.op('pe', lambda e: e.matmul(pu[:, :gn], lhsT=WIN[s][:, dc, 1, sub * 128:(sub + 1) * 128],
                                                                  rhs=xT[:, dc, lc:lc + gn], start=(dc == 0), stop=(dc == 7)),
                                         reads=['WIN%d_1' % s, 'xT'], writes=[puk])
                                q = k % 2
                                k += 1
                                T.op('act', lambda e: e.activation(out=sg[q][:, :gn], in_=pg[:, :gn], func=AF.Silu),
                                     reads=[pgk], writes=['sg%d' % q])
                                T.op('dve', lambda e: e.scalar_tensor_tensor(out=aT[:, f, lc:lc + gn], in0=pu[:, :gn], scalar=0.5,
                                                                             in1=sg[q][:, :gn], op0=ALU.mult, op1=ALU.mult),
                                     reads=[puk, 'sg%d' % q], writes=['aT'])
                    for (col, n) in pas['subs']:
                        lc = col - c0
                        _, y0, y0k = psum()
                        _, y1, y1k = psum()
                        for f in range(22):
                            T.op('pe', lambda e: e.matmul(y0[:n, :], lhsT=aT[:, f, lc:lc + n], rhs=WOUT[:, f, 0:512],
                                                          start=(f == 0), stop=(f == 21)),
                                 reads=['aT', 'WOUT%d' % (f // 2)], writes=[y0k])
                            T.op('pe', lambda e: e.matmul(y1[:n, :], lhsT=aT[:, f, lc:lc + n], rhs=WOUT[:, f, 512:1024],
                                                          start=(f == 0), stop=(f == 21)),
                                 reads=['aT', 'WOUT%d' % (f // 2)], writes=[y1k])
                        ln_stage(col, n, [(y0, y0k), (y1, y1k)], src, li, gk, dst, not first)
                        first = False
                T.barrier()

        def wo_stage(OT, w_o_ap, li, gk):
            with contextlib.ExitStack() as ls:
                KP, NCH = OT.shape[0], OT.shape[1]
                WO = ls.enter_context(nc.sbuf_tensor(un("WO"), [KP, NCH, D], BF))
                hh = NCH // 2
                T.dma('pool', WO[:, 0:hh, :], w_o_ap.rearrange("(h p) d -> p h d", p=KP)[:, 0:hh, :], writes=['WO0'], semkey='WO0')
                T.dma('pool', WO[:, hh:NCH, :], w_o_ap.rearrange("(h p) d -> p h d", p=KP)[:, hh:NCH, :], writes=['WO1'], semkey='WO1')
                first = True
                for pas in PASSES:
                    for (col, n) in pas['subs']:
                        _, y0, y0k = psum()
                        _, y1, y1k = psum()
                        for h in range(NCH):
                            T.op('pe', lambda e: e.matmul(y0[:n, :], lhsT=OT[:, h, col:col + n], rhs=WO[:, h, 0:512],
                                                          start=(h == 0), stop=(h == NCH - 1)),
                                 reads=['OT', 'OTs', 'WO%d' % (h // hh)], writes=[y0k])
                            T.op('pe', lambda e: e.matmul(y1[:n, :], lhsT=OT[:, h, col:col + n], rhs=WO[:, h, 512:1024],
                                                          start=(h == 0), stop=(h == NCH - 1)),
                                 reads=['OT', 'OTs', 'WO%d' % (h // hh)], writes=[y1k])
                        ln_stage(col, n, [(y0, y0k), (y1, y1k)], xres, li, gk, xres, not first)
                        first = False
                T.barrier()

        def fox_proj(cT):
            xT, ld_f, ld_b, Gt, Bt, zt, xo, junk, st_ = WKH.xT, WKH.ld_f, WKH.ld_b, WKH.Gt, WKH.Bt, WKH.zt, WKH.xo, WKH.junk, WKH.st_
            with contextlib.ExitStack() as ls:
                lsb = lambda name, shape, dtype=F32: ls.enter_context(nc.sbuf_tensor(un(name), list(shape), dtype))
                WQ = lsb("WQ", [128, 8, D], BF); WK = lsb("WK", [128, 8, D], BF); WV = lsb("WV", [128, 8, D], BF)
                WF = lsb("WF", [128, 8, H], BF)
                QS = lsb("QS", [64, 16, 512], BF); KS = lsb("KS", [64, 16, 512], BF)
                ktm = [lsb("ktm%d" % i, [128, D]) for i in range(2)]
                vtm = [lsb("vtm%d" % i, [128, D]) for i in range(2)]
                qtm = lsb("qtm", [128, D])
                vau = [lsb("vau%d" % i, [128, 16, 65], BF) for i in range(2)]
                bfB = lsb("bfB", [128, H]); nbf = lsb("nbf", [16, 1]); bfc = lsb("bfc", [16, 1])
                lft = [lsb("lft%d" % i, [128, H]) for i in range(2)]
                lfT = lsb("lfT", [16, 512])
                wv = fox_w_in.rearrange("(c p) f -> p c f", p=128)
                for i, (W, nm) in enumerate([(WQ, 'WQ'), (WK, 'WK'), (WV, 'WV')]):
                    for hf in range(2):
                        T.dma('pool', W[:, :, hf * 512:(hf + 1) * 512], wv[:, :, i * D + hf * 512: i * D + (hf + 1) * 512],
                              writes=['%s%d' % (nm, hf)], semkey='%s%d' % (nm, hf))
                T.dma('pool', WF[:], wv[:, :, 3 * D:3 * D + H], writes=['WF'], semkey='WF')
                T.dma('sp', bfB[:], fox_b_f[0:1, :].partition_broadcast(128), writes=['bfB'], semkey='bfB')
                T.dma('sp', bfc[:], fox_b_f.rearrange("o h -> h o"), writes=['bfc'], semkey='bfc')
                T.op('dve', lambda e: e.tensor_scalar(out=nbf[:], in0=bfc[:], scalar1=-1.0, scalar2=None, op0=ALU.mult),
                     reads=['bfc'], writes=['nbf'])
                for i in range(2):
                    T.op('dve', lambda e: e.memset(vau[i][:], 1.0), writes=['vau%d' % i])
                k = 0
                for pas in PASSES:
                    c0 = pas['c0']
                    load_xT(pas, xres)
                    for (gc, gn) in pas['groups']:
                        if gn != 512:
                            continue
                        lc = gc - c0
                        for (W, nm, ST, scale) in [(WQ, 'WQ', QS, 0.125), (WK, 'WK', KS, 1.0)]:
                            for h in range(16):
                                _, pp, ppk = psum()
                                for dc in range(8):
                                    T.op('pe', lambda e: e.matmul(pp[:64, :], lhsT=W[:, dc, h * 64:(h + 1) * 64], rhs=xT[:, dc, lc:lc + 512],
                                                                  start=(dc == 0), stop=(dc == 7)),
                                         reads=['%s%d' % (nm, h // 8), 'xT'], writes=[ppk])
                                T.op('act', lambda e: e.activation(out=ST[:, h, :], in_=pp[:64, :], func=AF.Copy, scale=scale),
                                     reads=[ppk], writes=[nm + 'S'])
                        T.dma('sp', qT_d.rearrange("h d t -> d h t")[:, :, gc:gc + 512], QS[:], reads=['WQS'], writes=['d:qT'], semkey='QS_st')
                        for i8 in range(8):
                            T.dma('sp', ktloc_c[i8].rearrange("(h r) t -> r h t", r=70)[0:64, :, gc:gc + 512], KS[:, 2 * i8:2 * i8 + 2, :], reads=['WKS'],
                                  writes=['d:ktloc'], semkey='KS_st')
                        _, pf, pfk = psum()
                        for dc in range(8):
                            T.op('pe', lambda e: e.matmul(pf[:16, :], lhsT=WF[:, dc, :], rhs=xT[:, dc, lc:lc + 512], start=(dc == 0), stop=(dc == 7)),
                                 reads=['WF', 'xT'], writes=[pfk])
                        T.op('act', lambda e: e.activation(out=lfT[:], in_=pf[:16, :], func=AF.Exp, bias=nbf[:], scale=-1.0),
                             reads=[pfk, 'nbf'], writes=['lfT'])
                        T.op('act', lambda e: e.activation(out=lfT[:], in_=lfT[:], func=AF.Ln, bias=1.0, scale=1.0), reads=['lfT'], writes=['lfT'])
                        T.op('dve', lambda e: e.tensor_scalar(out=lfT[:], in0=lfT[:], scalar1=-1.0, scalar2=None, op0=ALU.mult),
                             reads=['lfT'], writes=['lfT'])
                        init = 0.0 if gc == 0 else cT[:, gc - 1:gc]
                        T.op('dve', lambda e: e.tensor_tensor_scan(out=cT[:, gc:gc + 512], data0=onesf[:16, :], data1=lfT[:], initial=init,
                                                                   op0=ALU.mult, op1=ALU.add), reads=['lfT', 'onesf', 'cT'], writes=['cT'])
                    for (col, n) in pas['subs']:
                        lc = col - c0
                        s = k % 2
                        k += 1
                        for (W, nm, tm) in [(WK, 'WK', ktm[s]), (WV, 'WV', vtm[s])] + ([(WQ, 'WQ', qtm)] if n == 4 else []):
                            key = tm.name if hasattr(tm, 'name') else nm
                            for hf in range(2):
                                _, pp, ppk = psum()
                                for dc in range(8):
                                    T.op('pe', lambda e: e.matmul(pp[:n, :], lhsT=xT[:, dc, lc:lc + n], rhs=W[:, dc, hf * 512:(hf + 1) * 512],
                                                                  start=(dc == 0), stop=(dc == 7)),
                                         reads=['%s%d' % (nm, hf), 'xT'], writes=[ppk])
                                T.op('act', lambda e: e.copy(out=tm[:n, hf * 512:(hf + 1) * 512], in_=pp[:n, :]), reads=[ppk],
                                     writes=['tm_%s_%d' % (nm, s)])
                        T.dma('sp', o_k[col:col + n, :], ktm[s][:n, :], reads=['tm_WK_%d' % s], semkey='ktm%d_st' % s)
                        T.dma('sp', o_v[col:col + n, :], vtm[s][:n, :], reads=['tm_WV_%d' % s], semkey='vtm%d_st' % s)
                        if n == 128:
                            T.op('dve', lambda e: e.tensor_copy(out=vau[s][:, :, 0:64], in_=vtm[s][:, :].rearrange("p (h d) -> p h d", h=16)),
                                 reads=['tm_WV_%d' % s], writes=['vau%d' % s])
                            for i8 in range(8):
                                T.dma('sp', vloc_c[i8].rearrange("a (b c) -> (a b) c", c=65).rearrange("(h t) c -> t h c", h=2)[col:col + 128, :, :],
                                      vau[s][:, 2 * i8:2 * i8 + 2, :], reads=['vau%d' % s], writes=['d:vloc'], semkey='vau%d_st' % s)
                        else:
                            T.dma('sp', srow_d[:, 0:D], qtm[:n, :], reads=['tm_WQ_%d' % s], writes=['d:srow'], semkey='qtm_st')
                            T.dma('sp', srow_d[:, D:2 * D], ktm[s][:n, :], reads=['tm_WK_%d' % s], writes=['d:srow'], semkey='ktm%d_st2' % s)
                            T.dma('sp', srow_d[:, 2 * D:3 * D], vtm[s][:n, :], reads=['tm_WV_%d' % s], writes=['d:srow'], semkey='vtm%d_st2' % s)
                        _, pf, pfk = psum()
                        for dc in range(8):
                            T.op('pe', lambda e: e.matmul(pf[:n, :16], lhsT=xT[:, dc, lc:lc + n], rhs=WF[:, dc, :], start=(dc == 0), stop=(dc == 7)),
                                 reads=['WF', 'xT'], writes=[pfk])
                        lf = lft[s]
                        T.op('dve', lambda e: e.tensor_tensor(out=lf[:n, :], in0=pf[:n, :16], in1=bfB[:n, :], op=ALU.add),
                             reads=[pfk, 'bfB'], writes=['lft%d' % s])
                        T.op('act', lambda e: e.activation(out=lf[:n, :], in_=lf[:n, :], func=AF.Exp, scale=-1.0), reads=['lft%d' % s], writes=['lft%d' % s])
                        T.op('act', lambda e: e.activation(out=lf[:n, :], in_=lf[:n, :], func=AF.Ln, bias=1.0, scale=1.0), reads=['lft%d' % s], writes=['lft%d' % s])
                        T.op('dve', lambda e: e.tensor_scalar(out=lf[:n, :], in0=lf[:n, :], scalar1=-1.0, scalar2=None, op0=ALU.mult),
                             reads=['lft%d' % s], writes=['lft%d' % s])
                        T.dma('sp', o_lf[col:col + n, :], lf[:n, :], reads=['lft%d' % s], semkey='lft%d_st' % s)
                        if n == 4:
                            T.dma('sp', srow_d[:, 3 * D:3 * D + H], lf[:n, :], reads=['lft%d' % s], writes=['d:srow'], semkey='lft%d_st2' % s)
                T.barrier()


        GROUPS4 = [[0, 1, 2, 3], [4, 5, 6, 7]]

        def allgather(src, dst, rk, wk, name):
            T.dma('pool', None, None, reads=[rk], writes=[wk], semkey=name, inc=1,
                  fn=lambda e: e.collective_compute("AllGather", ALU.bypass, replica_groups=GROUPS4, ins=[src], outs=[dst]))

        def fox_attn(cT, OT):
            with contextlib.ExitStack() as ls:
                lsb = lambda name, shape, dtype=F32: ls.enter_context(nc.sbuf_tensor(un(name), list(shape), dtype))
                pre = contextlib.ExitStack()
                psb = lambda name, shape, dtype=F32: pre.enter_context(nc.sbuf_tensor(un(name), list(shape), dtype))
                wk_ = psb("wk_", [16, NP_]); r1 = psb("r1", [16, NP_])
                hi = psb("hi", [16, NP_], BF); mid = psb("mid", [16, NP_], BF); lo = psb("lo", [16, NP_], BF)
                ones3 = psb("ones3", [16, 3, NP_], BF)
                TG = psb("TG", [16, 4]); Dp = psb("Dp", [16, 4]); tmp4 = psb("tmp4", [16, 4])
                T.op('dve', lambda e: e.memset(ones3[:], 1.0), writes=['ones3'])
                fox_sample_setup(psb)
                PSN[0] = 4

                def split3(srckey):
                    T.op('dve', lambda e: e.tensor_copy(out=hi[:], in_=wk_[:]), reads=[srckey], writes=['hi'])
                    T.op('dve', lambda e: e.tensor_tensor(out=r1[:], in0=wk_[:], in1=hi[:], op=ALU.subtract), reads=[srckey, 'hi'], writes=['r1'])
                    T.op('dve', lambda e: e.tensor_copy(out=mid[:], in_=r1[:]), reads=['r1'], writes=['mid'])
                    T.op('dve', lambda e: e.tensor_tensor(out=r1[:], in0=r1[:], in1=mid[:], op=ALU.subtract), reads=['r1', 'mid'], writes=['r1'])
                    T.op('dve', lambda e: e.tensor_copy(out=lo[:], in_=r1[:]), reads=['r1'], writes=['lo'])

                T.op('dve', lambda e: e.tensor_scalar(out=wk_[:], in0=cT[:], scalar1=-1.0, scalar2=None, op0=ALU.mult), reads=['cT'], writes=['wk_'])
                split3('wk_')
                for i8 in range(8):
                    ktv = ktloc_c[i8].rearrange("(h r) t -> h r t", r=70)
                    hs = slice(2 * i8, 2 * i8 + 2)
                    T.dma('sp', ktv[:, 64:67, :], ones3[hs], reads=['ones3'], writes=['d:kt_a'], semkey='ones3_st')
                    T.dma('sp', ktv[:, 67, :], hi[hs], reads=['hi'], writes=['d:kt_b'], semkey='hi_st')
                    T.dma('sp', ktv[:, 68, :], mid[hs], reads=['mid'], writes=['d:kt_c'], semkey='mid_st')
                    T.dma('sp', ktv[:, 69, :], lo[hs], reads=['lo'], writes=['d:kt_d'], semkey='lo_st')
                with nc.allow_non_contiguous_dma(reason="tiny"):
                    T.dma('sp', totloc.rearrange("o h -> h o"), cT[:, NP_ - 1:NP_], reads=['cT'], writes=['d:tot'], semkey='tot_st')
                T.barrier()
                chk(2)
                for i8 in range(8):
                    allgather(ktloc_c[i8], ktg_c[i8], 'd:kt_a', 'd:ktg', 'cc_kt%d' % i8)
                    allgather(vloc_c[i8], vg_c[i8], 'd:kt_a', 'd:vg', 'cc_v%d' % i8)
                allgather(totloc_f, totg_f, 'd:tot', 'd:totg', 'cc_tot')
                with nc.allow_non_contiguous_dma(reason="tiny"):
                    T.dma('sp', TG[:], totg.rearrange("r h -> h r"), reads=['d:totg'], writes=['TG'], semkey='TG')
                for rp in range(4):
                    T.op('dve', lambda e: e.tensor_tensor(out=tmp4[:], in0=TG[:], in1=crank[:16, rp * 4:rp * 4 + 4], op=ALU.mult),
                         reads=['TG', 'crank'], writes=['tmp4'])
                    T.op('dve', lambda e: e.tensor_reduce(out=Dp[:, rp:rp + 1], in_=tmp4[:], axis=AX.X, op=ALU.add), reads=['tmp4'], writes=['Dp'])
                T.op('dve', lambda e: e.tensor_tensor(out=Dp[:], in0=Dp[:], in1=crank[:16, 16:20], op=ALU.add), reads=['Dp', 'crank'], writes=['Dp'])
                for v in range(5):
                    if v == 0:
                        T.op('dve', lambda e: e.tensor_copy(out=wk_[:], in_=cT[:]), reads=['cT'], writes=['wk_'])
                    else:
                        T.op('dve', lambda e: e.tensor_scalar(out=wk_[:], in0=cT[:], scalar1=Dp[:, v - 1:v], scalar2=None, op0=ALU.add),
                             reads=['cT', 'Dp'], writes=['wk_'])
                    split3('wk_')
                    T.dma('sp', qaug_d[v, :, 0, :], hi[:], reads=['hi'], writes=['d:qa%d' % v], semkey='hi_st')
                    T.dma('sp', qaug_d[v, :, 1, :], mid[:], reads=['mid'], writes=['d:qb%d' % v], semkey='mid_st')
                    T.dma('sp', qaug_d[v, :, 2, :], lo[:], reads=['lo'], writes=['d:qc%d' % v], semkey='lo_st')
                    T.dma('sp', qaug_d[v, :, 3:6, :], ones3[:], reads=['ones3'], writes=['d:qd%d' % v], semkey='ones3_st')
                T.barrier()
                chk(3)
                pre.close()
                QA = lsb("QA", [70, 5, NP_], BF); KL = lsb("KL", [70, NP_], BF); KG = lsb("KG", [70, 4, NP_], BF)
                VL = lsb("VL", [128, 16, 65], BF); VG = lsb("VG", [128, 4, 16, 65], BF)
                PT = [lsb("PT%d" % i, [128, 512], BF) for i in range(3)]
                rdt = lsb("rdt", [65, 512]); bcs = lsb("bcs", [64, 512])
                sgen = fox_sample_gen(OT, lsb)
                next(sgen, None)
                next(sgen, None)
                ktgv = [ktg_c[i8].rearrange("(g h r) t -> h r g t", g=4, h=2) for i8 in range(8)]
                vlv = [vloc_c[i8].rearrange("a (b c) -> (a b) c", c=65).rearrange("(h kb p) c -> h p kb c", h=2, p=128) for i8 in range(8)]
                vgv = [vg_c[i8].rearrange("a (b c) -> (a b) c", c=65).rearrange("(g h p j) c -> h p g j c", g=4, h=2, p=128) for i8 in range(8)]
                KGv = KG[:].rearrange("r g (p j) -> r g j p", j=16)
                pk_ = 0
                for h in range(16):
                    T.dma('sp', QA[0:64, :, :], bass.AP(qT_d.tensor, h * 64 * NP_, [[NP_, 64], [0, 5], [1, NP_]]), writes=['QAq'], semkey='QAq')
                    T.dma('sp', QA[64:70, :, :], qaug_d[:, h, :, :].rearrange("v r t -> r v t"), writes=['QAa'], semkey='QAa')
                    T.dma('sp', KL[:], ktloc_c[h // 2][(h % 2) * 70:(h % 2 + 1) * 70, :], writes=['KL'], semkey='KL')
                    T.dma('sp', KG[:], ktgv[h // 2][h % 2], writes=['KG'], semkey='KG')
                    T.dma('sp', VL[:], vlv[h // 2][h % 2], writes=['VL'], semkey='VL')
                    T.dma('sp', VG[:], vgv[h // 2][h % 2], writes=['VG'], semkey='VG')
                    items = []
                    for qt in range(4):
                        blocks = [('L', kb, 0) for kb in range(4 * qt + 4)] + [('G', j, g) for g in range(4) for j in range(16)]
                        for bi, (kind, kb, g) in enumerate(blocks):
                            items.append((qt, kind, kb, g, bi == 0, bi == len(blocks) - 1))
                    st = {}

                    def stage_a(i):
                        (qt, kind, kb, g, first, last) = items[i]
                        qc = qt * 512
                        _, S, Sk = psum()
                        p = i % 3
                        if kind == 'L':
                            off = max(0, kb * 128 - qc)
                            n = 512 - off
                            dg = kb >= 4 * qt
                            T.op('pe', lambda e: e.matmul(S[:, :n], lhsT=KL[:, kb * 128:(kb + 1) * 128], rhs=QA[:, 0, qc + off:qc + 512],
                                                          start=True, stop=not dg), reads=['KL', 'QAq', 'QAa'], writes=[Sk])
                            if dg:
                                T.op('pe', lambda e: e.matmul(S[:, 0:128], lhsT=identb[:, :], rhs=mbb[:, :], start=False, stop=True),
                                     reads=['identb', 'mbb'], writes=[Sk])
                        else:
                            off = 0
                            n = 512
                            T.op('pe', lambda e: e.matmul(S[:, :n], lhsT=KGv[:, g, kb, :], rhs=QA[:, 1 + g, qc:qc + 512],
                                                          start=True, stop=True), reads=['KG', 'QAq', 'QAa'], writes=[Sk])
                        T.op('act', lambda e: e.activation(out=PT[p][:, :n], in_=S[:, :n], func=AF.Exp), reads=[Sk], writes=['PT%d' % p])
                        st[i] = (p, off, n)

                    def stage_b(i):
                        (qt, kind, kb, g, first, last) = items[i]
                        qc = qt * 512
                        (p, off, n) = st.pop(i)
                        _, O, Ok = psum_r(6 + (qt % 2))
                        lhsV = VL[:, kb, :] if kind == 'L' else VG[:, g, kb, :]
                        vkey = 'VL' if kind == 'L' else 'VG'
                        T.op('pe', lambda e: e.matmul(O[0:65, off:512], lhsT=lhsV, rhs=PT[p][:, :n], start=first, stop=last),
                             reads=[vkey, 'PT%d' % p], writes=[Ok])
                        if last:
                            T.op('dve', lambda e: e.reciprocal(out=rdt[64:65, :], in_=O[64:65, :]), reads=[Ok], writes=['rdt'])
                            _, bc, bck = psum()
                            T.op('pe', lambda e: e.matmul(bc[0:64, :], lhsT=onesf[64:65, 0:64], rhs=rdt[64:65, :], start=True, stop=True),
                                 reads=['rdt', 'onesf'], writes=[bck])
                            T.op('act', lambda e: e.copy(out=bcs[:], in_=bc[0:64, :]), reads=[bck], writes=['bcs'])
                            T.op('dve', lambda e: e.tensor_tensor(out=OT[:, h, qc:qc + 512], in0=O[0:64, :], in1=bcs[:], op=ALU.mult),
                                 reads=[Ok, 'bcs'], writes=['OT'])

                    LAH = 2
                    for i in range(len(items) + LAH):
                        if i < len(items):
                            stage_a(i)
                        if i >= LAH:
                            stage_b(i - LAH)
                        if i % 36 == 35:
                            next(sgen, None)
                for _ in sgen:
                    pass
                T.barrier()
                PSN[0] = 6

        def fox_sample_setup(psb):
            q4 = psb("q4", [4, D]); k4 = psb("k4", [4, D]); p4 = psb("p4", [4, D]); sn = psb("sn", [4, H]); pn = psb("pn", [4, H])
            T.dma('sp', q4[:], srow_d[:, 0:D], reads=['d:srow'], writes=['q4'], semkey='q4')
            T.dma('sp', k4[:], srow_d[:, D:2 * D], reads=['d:srow'], writes=['k4'], semkey='k4')
            T.op('dve', lambda e: e.tensor_tensor(out=p4[:], in0=q4[:], in1=k4[:], op=ALU.mult), reads=['q4', 'k4'], writes=['p4'])
            T.op('dve', lambda e: e.tensor_reduce(out=sn[:], in_=p4[:].rearrange("p (h d) -> p h d", h=16), axis=AX.X, op=ALU.add),
                 reads=['p4'], writes=['sn'])
            T.op('act', lambda e: e.activation(out=pn[:], in_=sn[:], func=AF.Exp, scale=0.125), reads=['sn'], writes=['pn'])
            T.dma('sp', srow_d[:, 3 * D + H:3 * D + 2 * H], pn[:], reads=['pn'], writes=['d:srow2'], semkey='pn_st')

        def fox_sample_gen(OT, lsb):
            RL = 3 * D + 2 * H
            NVB = 8
            rows = lsb("rows", [1, NS, 2 * H]); rowsb = lsb("rowsb", [1, NS, D], BF); pnb = lsb("pnb", [1, NS, H], BF)
            idx = lsb("idx", [128, 1], I32); idxf = lsb("idxf", [128, 1]); idxall = lsb("idxall", [128, 32], I32); idx2 = lsb("idx2", [128, 1], I32)
            LF = lsb("LF", [128, 64, H]); Bi = lsb("Bi", [128, 64, H]); Sc = lsb("Sc", [128, 64, H]); Pm = lsb("Pm", [128, 64, H], BF)
            KB = [lsb("KB%d" % i, [128, 2, D], BF) for i in range(2)]; VB = [lsb("VB%d" % i, [128, 2, D], BF) for i in range(NVB)]
            prod = lsb("prod", [128, 2, D], BF); qB = lsb("qB", [128, 2, D], BF)
            Tt = lsb("Tt", [128, H]); At = lsb("At", [128, H]); dpart = lsb("dpart", [128, H]); rden = lsb("rden", [16, 1])
            On = lsb("On", [16, D], BF)
            T.dma('sp', rows[:], bass.AP(srow_d.tensor, 3 * D, [[0, 1], [RL, NS], [1, 2 * H]]), reads=['d:srow2', 'd:srow'], writes=['rows'], semkey='rows')
            T.dma('pool', rowsb[:], bass.AP(srow_d.tensor, 2 * D, [[0, 1], [RL, NS], [1, D]]), reads=['d:srow'], writes=['rowsb'], semkey='rowsb')
            T.op('dve', lambda e: e.tensor_copy(out=pnb[:], in_=rows[:, :, H:2 * H]), reads=['rows'], writes=['pnb'])

            def vgather(j):
                b = j % NVB
                T.dma('pool', None, None, reads=['idxall'], writes=['VB%d' % b], semkey='VB%d' % b,
                      fn=lambda e: e.indirect_dma_start(out=VB[b][:].rearrange("p t d -> p (t d)"), out_offset=None, in_=cv,
                                                        in_offset=bass.IndirectOffsetOnAxis(ap=idxall[:, j:j + 1], axis=0)))

            for s_ in range(NS):
                with nc.allow_non_contiguous_dma(reason="tiny"):
                    T.dma('sp', idx[:], bass.AP(ptab.tensor, s_ * 64, [[1, 64], [0, 2], [1, 1]]), writes=['idx'], semkey='idx')
                T.op('dve', lambda e: e.tensor_scalar(out=idx2[:], in0=idx[:], scalar1=2.0, scalar2=crank[:, 40:41], op0=ALU.mult, op1=ALU.add),
                     reads=['idx', 'crank'], writes=['idx2'])
                T.op('dve', lambda e: e.tensor_scalar(out=idxf[:], in0=idx2[:], scalar1=32.0, scalar2=None, op0=ALU.mult), reads=['idx2'], writes=['idxf'])
                T.op('dve', lambda e: e.tensor_scalar(out=idxall[:], in0=crank[:, 48:80], scalar1=idxf[:, 0:1], scalar2=None, op0=ALU.add),
                     reads=['idxf', 'crank'], writes=['idxall'])
                T.dma('pool', None, None, reads=['idx2'], writes=['LF'], semkey='LF',
                      fn=lambda e: e.indirect_dma_start(out=LF[:].rearrange("p t h -> p (t h)"), out_offset=None, in_=clf,
                                                        in_offset=bass.IndirectOffsetOnAxis(ap=idx2[:, 0:1], axis=0)))
                for t in range(2):
                    T.dma('pool', qB[:, t, :], bass.AP(srow_d.tensor, s_ * RL, [[0, 128], [1, D]]), reads=['d:srow'], writes=['qB%d' % t], semkey='qB%d' % t)
                for j in range(32):
                    b = j % 2
                    T.dma('pool', None, None, reads=['idxall'], writes=['KB%d' % b], semkey='KB%d' % b,
                          fn=lambda e: e.indirect_dma_start(out=KB[b][:].rearrange("p t d -> p (t d)"), out_offset=None, in_=ck,
                                                            in_offset=bass.IndirectOffsetOnAxis(ap=idxall[:, j:j + 1], axis=0)))
                    T.op('dve', lambda e: e.tensor_tensor(out=prod[:], in0=KB[b][:], in1=qB[:], op=ALU.mult),
                         reads=['KB%d' % b, 'qB0', 'qB1'], writes=['prod'])
                    T.op('dve', lambda e: e.tensor_reduce(out=Sc[:, 2 * j:2 * j + 2, :].rearrange("p t h -> p (t h)"),
                                                          in_=prod[:].rearrange("p t (h d) -> p (t h) d", h=16), axis=AX.X, op=ALU.add),
                         reads=['prod'], writes=['Sc'])
                    if j % 2 == 1:
                        yield
                T.op('dve', lambda e: e.tensor_reduce(out=Tt[:], in_=LF[:].rearrange("p t h -> p h t"), axis=AX.X, op=ALU.add), reads=['LF'], writes=['Tt'])
                yield
                for j in range(NVB):
                    vgather(j)
                yield
                yield
                _, lp, lpk = psum()
                T.op('pe', lambda e: e.matmul(lp[:, 0:H], lhsT=slowf[:], rhs=Tt[:], start=True, stop=False), reads=['slowf', 'Tt'], writes=[lpk])
                T.op('pe', lambda e: e.matmul(lp[:, 0:H], lhsT=onesf[0:1, 0:128], rhs=rows[0:1, s_, 0:H], start=False, stop=True),
                     reads=['onesf', 'rows'], writes=[lpk])
                T.op('dve', lambda e: e.tensor_tensor(out=At[:], in0=lp[:, 0:H], in1=Tt[:], op=ALU.add), reads=[lpk, 'Tt'], writes=['At'])
                for h in range(16):
                    T.op('dve', lambda e: e.tensor_tensor_scan(out=Bi[:, :, h], data0=onesf[:, 0:64], data1=LF[:, :, h], initial=At[:, h:h + 1],
                                                               op0=ALU.mult, op1=ALU.subtract), reads=['LF', 'At', 'onesf'], writes=['Bi'])
                T.op('dve', lambda e: e.scalar_tensor_tensor(out=Sc[:].rearrange("p t h -> p (t h)"), in0=Sc[:].rearrange("p t h -> p (t h)"),
                                                             scalar=0.125, in1=Bi[:].rearrange("p t h -> p (t h)"), op0=ALU.mult, op1=ALU.add),
                     reads=['Sc', 'Bi'], writes=['Sc'])
                T.op('act', lambda e: e.activation(out=Pm[:].rearrange("p t h -> p (t h)"), in_=Sc[:].rearrange("p t h -> p (t h)"), func=AF.Exp),
                     reads=['Sc'], writes=['Pm'])
                T.op('dve', lambda e: e.tensor_reduce(out=dpart[:], in_=Pm[:].rearrange("p t h -> p h t"), axis=AX.X, op=ALU.add), reads=['Pm'], writes=['dpart'])
                yield
                yield
                _, dn, dnk = psum()
                T.op('pe', lambda e: e.matmul(dn[0:16, 0:1], lhsT=dpart[:], rhs=onesf[:, 0:1], start=True, stop=False), reads=['dpart', 'onesf'], writes=[dnk])
                T.op('pe', lambda e: e.matmul(dn[0:16, 0:1], lhsT=rows[0:1, s_, H:2 * H], rhs=onesf[0:1, 0:1], start=False, stop=True),
                     reads=['rows', 'onesf'], writes=[dnk])
                T.op('dve', lambda e: e.reciprocal(out=rden[:], in_=dn[0:16, 0:1]), reads=[dnk], writes=['rden'])
                _, O0, O0k = psum_r(4)
                _, O1, O1k = psum_r(5)
                for q in range(4):
                    for j in range(q * NVB, (q + 1) * NVB):
                        b = j % NVB
                        for t in range(2):
                            pos = 2 * j + t
                            T.op('pe', lambda e: e.matmul(O0[0:16, :], lhsT=Pm[:, pos, :], rhs=VB[b][:, t, 0:512], start=(pos == 0), stop=False),
                                 reads=['Pm', 'VB%d' % b], writes=[O0k])
                            T.op('pe', lambda e: e.matmul(O1[0:16, :], lhsT=Pm[:, pos, :], rhs=VB[b][:, t, 512:1024], start=(pos == 0), stop=False),
                                 reads=['Pm', 'VB%d' % b], writes=[O1k])
                    if q < 3:
                        for j in range((q + 1) * NVB, (q + 2) * NVB):
                            vgather(j)
                        yield
                        yield
                T.op('pe', lambda e: e.matmul(O0[0:16, :], lhsT=pnb[0:1, s_, :], rhs=rowsb[0:1, s_, 0:512], start=False, stop=True),
                     reads=['pnb', 'rowsb'], writes=[O0k])
                T.op('pe', lambda e: e.matmul(O1[0:16, :], lhsT=pnb[0:1, s_, :], rhs=rowsb[0:1, s_, 512:1024], start=False, stop=True),
                     reads=['pnb', 'rowsb'], writes=[O1k])
                T.op('dve', lambda e: e.tensor_scalar(out=On[:, 0:512], in0=O0[0:16, :], scalar1=rden[:, 0:1], scalar2=None, op0=ALU.mult),
                     reads=[O0k, 'rden'], writes=['On'])
                T.op('dve', lambda e: e.tensor_scalar(out=On[:, 512:1024], in0=O1[0:16, :], scalar1=rden[:, 0:1], scalar2=None, op0=ALU.mult),
                     reads=[O1k, 'rden'], writes=['On'])
                _, Z, Zk = psum()
                for h in range(16):
                    T.op('pe', lambda e: e.matmul(Z[0:64, h:h + 1], lhsT=On[:, h * 64:(h + 1) * 64], rhs=identb[0:16, h:h + 1], start=True, stop=True),
                         reads=['On', 'identb'], writes=[Zk])
                T.op('dve', lambda e: e.tensor_copy(out=OT[:, :, NP_ + s_], in_=Z[0:64, 0:16]), reads=[Zk], writes=['OTs'])
                yield

        gqT = dint("gqT", [GH, 128, NT], BF)
        gkT = dint("gkT", [GH, 128, NT], BF)
        glaT = dint("glaT", [GH, 128, NT])
        grT = dint("grT", [8, 128, NT], BF)
        gv = dint("gv", [NP_, D], BF)
        gsv = dint("gsv", [NS, D + 512])

        def gla_proj():
            xT, ld_f, ld_b, Gt, Bt, zt, xo, junk, st_ = WKH.xT, WKH.ld_f, WKH.ld_b, WKH.Gt, WKH.Bt, WKH.zt, WKH.xo, WKH.junk, WKH.st_
            with contextlib.ExitStack() as ls:
                lsb = lambda name, shape, dtype=F32: ls.enter_context(nc.sbuf_tensor(un(name), list(shape), dtype))
                WGQ = lsb("WGQ", [128, 8, 512], BF); WGK = lsb("WGK", [128, 8, 512], BF)
                WGV = lsb("WGV", [128, 8, D], BF); WGR = lsb("WGR", [128, 8, D], BF); WGA = lsb("WGA", [128, 8, 16], BF)
                WA2 = lsb("WA2", [16, 512], BF); nba = lsb("nba", [128, 4]); bac = lsb("bac", [128, 4])
                SQ = lsb("SQ", [128, 4, 512], BF); SK = lsb("SK", [128, 4, 512], BF); SL = lsb("SL", [128, 4, 512]); SR = lsb("SR", [128, 8, 512], BF)
                alr = lsb("alr", [16, 512], BF)
                vt = [lsb("vt%d" % i, [128, D], BF) for i in range(2)]
                vs_ = lsb("vs_", [4, D + 512])
                wv = gla_w_in.rearrange("(c p) f -> p c f", p=128)
                T.dma('pool', WGQ[:], wv[:, :, 0:512], writes=['WGQ'], semkey='WGQ')
                T.dma('pool', WGK[:], wv[:, :, 512:1024], writes=['WGK'], semkey='WGK')
                for hf in range(2):
                    T.dma('pool', WGV[:, :, hf * 512:(hf + 1) * 512], wv[:, :, 1024 + hf * 512:1024 + (hf + 1) * 512], writes=['WGV%d' % hf], semkey='WGV%d' % hf)
                    T.dma('pool', WGR[:, :, hf * 512:(hf + 1) * 512], wv[:, :, 2048 + hf * 512:2048 + (hf + 1) * 512], writes=['WGR%d' % hf], semkey='WGR%d' % hf)
                T.dma('pool', WGA[:], wv[:, :, 3072:3088], writes=['WGA'], semkey='WGA')
                T.dma('pool', WA2[:], gla_w_a2, writes=['WA2'], semkey='WA2')
                with nc.allow_non_contiguous_dma(reason="tiny"):
                    T.dma('sp', bac[:], gla_b_a.rearrange("o (h p) -> p (o h)", p=128), writes=['bac'], semkey='bac')
                T.op('dve', lambda e: e.tensor_scalar(out=nba[:], in0=bac[:], scalar1=-1.0, scalar2=None, op0=ALU.mult), reads=['bac'], writes=['nba'])
                k = 0
                for pas in PASSES:
                    c0 = pas['c0']
                    load_xT(pas, xres)
                    for (gc, gn) in pas['groups']:
                        lc = gc - c0
                        for (W, nm, ST, scale) in [(WGQ, 'WGQ', SQ, 128.0 ** -0.5), (WGK, 'WGK', SK, 1.0)]:
                            for h in range(4):
                                _, pp, ppk = psum()
                                for dc in range(8):
                                    T.op('pe', lambda e: e.matmul(pp[:, :gn], lhsT=W[:, dc, h * 128:(h + 1) * 128], rhs=xT[:, dc, lc:lc + gn],
                                                                  start=(dc == 0), stop=(dc == 7)), reads=[nm, 'xT'], writes=[ppk])
                                T.op('act', lambda e: e.activation(out=ST[:, h, :gn], in_=pp[:, :gn], func=AF.Copy, scale=scale), reads=[ppk], writes=[nm + 'S'])
                        T.dma('sp', gqT.rearrange("h p t -> p h t")[:, :, gc:gc + gn], SQ[:, :, :gn], reads=['WGQS'], writes=['d:gq'], semkey='SQ_st')
                        T.dma('sp', gkT.rearrange("h p t -> p h t")[:, :, gc:gc + gn], SK[:, :, :gn], reads=['WGKS'], writes=['d:gk'], semkey='SK_st')
                        _, pa, pak = psum()
                        for dc in range(8):
                            T.op('pe', lambda e: e.matmul(pa[:16, :gn], lhsT=WGA[:, dc, :], rhs=xT[:, dc, lc:lc + gn], start=(dc == 0), stop=(dc == 7)),
                                 reads=['WGA', 'xT'], writes=[pak])
                        T.op('act', lambda e: e.copy(out=alr[:, :gn], in_=pa[:16, :gn]), reads=[pak], writes=['alr'])
                        for h in range(4):
                            _, pp, ppk = psum()
                            T.op('pe', lambda e: e.matmul(pp[:, :gn], lhsT=WA2[:, h * 128:(h + 1) * 128], rhs=alr[:, :gn], start=True, stop=True),
                                 reads=['WA2', 'alr'], writes=[ppk])
                            T.op('act', lambda e: e.activation(out=SL[:, h, :gn], in_=pp[:, :gn], func=AF.Exp, bias=nba[:, h:h + 1], scale=-1.0),
                                 reads=[ppk, 'nba'], writes=['SL'])
                        T.op('act', lambda e: e.activation(out=SL[:, :, :gn], in_=SL[:, :, :gn], func=AF.Ln, bias=1.0, scale=1.0), reads=['SL'], writes=['SL'])
                        T.op('dve', lambda e: e.tensor_scalar(out=SL[:, :, :gn], in0=SL[:, :, :gn], scalar1=-1.0 / 16.0, scalar2=None, op0=ALU.mult),
                             reads=['SL'], writes=['SL'])
                        T.dma('sp', glaT.rearrange("h p t -> p h t")[:, :, gc:gc + gn], SL[:, :, :gn], reads=['SL'], writes=['d:gla'], semkey='SL_st')
                        for c in range(8):
                            _, pp, ppk = psum()
                            for dc in range(8):
                                T.op('pe', lambda e: e.matmul(pp[:, :gn], lhsT=WGR[:, dc, c * 128:(c + 1) * 128], rhs=xT[:, dc, lc:lc + gn],
                                                              start=(dc == 0), stop=(dc == 7)), reads=['WGR%d' % (c // 4), 'xT'], writes=[ppk])
                            T.op('act', lambda e: e.activation(out=SR[:, c, :gn], in_=pp[:, :gn], func=AF.Silu), reads=[ppk], writes=['SR'])
                        T.dma('sp', grT.rearrange("c p t -> p c t")[:, :, gc:gc + gn], SR[:, :, :gn], reads=['SR'], writes=['d:gr'], semkey='SR_st')
                    for (col, n) in pas['subs']:
                        lc = col - c0
                        s_ = k % 2
                        k += 1
                        dstt = vt[s_] if n == 128 else vs_
                        dk_ = ('vt%d' % s_) if n == 128 else 'vs_'
                        for hf in range(2):
                            _, pp, ppk = psum()
                            for dc in range(8):
                                T.op('pe', lambda e: e.matmul(pp[:n, :], lhsT=xT[:, dc, lc:lc + n], rhs=WGV[:, dc, hf * 512:(hf + 1) * 512],
                                                              start=(dc == 0), stop=(dc == 7)), reads=['WGV%d' % hf, 'xT'], writes=[ppk])
                            T.op('act', lambda e: e.copy(out=dstt[:n, hf * 512:(hf + 1) * 512], in_=pp[:n, :]), reads=[ppk], writes=[dk_])
                        if n == 128:
                            T.dma('sp', gv[col:col + n, :], vt[s_][:n, :], reads=[dk_], writes=['d:gv'], semkey='vt%d_st' % s_)
                        else:
                            _, pp, ppk = psum()
                            for dc in range(8):
                                T.op('pe', lambda e: e.matmul(pp[:n, :], lhsT=xT[:, dc, lc:lc + n], rhs=WGK[:, dc, :], start=(dc == 0), stop=(dc == 7)),
                                     reads=['WGK', 'xT'], writes=[ppk])
                            T.op('act', lambda e: e.copy(out=vs_[:n, D:D + 512], in_=pp[:n, :]), reads=[ppk], writes=['vs_'])
                            T.dma('sp', gsv[:, :], vs_[:, :], reads=['vs_'], writes=['d:gsv'], semkey='vs_st')
                T.barrier()

        def gla_rec(OTg):
            with contextlib.ExitStack() as ls:
                lsb = lambda name, shape, dtype=F32: ls.enter_context(nc.sbuf_tensor(un(name), list(shape), dtype))
                QG = lsb("QG", [128, 4, 512], BF); KGt = lsb("KGt", [128, 4, 512], BF); LA = lsb("LA", [128, 4, 512])
                VT = lsb("VT", [64, 8, D], BF); RG = lsb("RG", [128, 8, 512], BF); OF = lsb("OF", [128, 8, 512])
                Sf = lsb("Sf", [128, 4, 256]); Sb = lsb("Sb", [128, 4, 256], BF); Gs = lsb("Gs", [128, 4])
                SG = lsb("SG", [128, 4, 4, GDV + 64]); Lt = lsb("Lt", [128, 16]); Et = lsb("Et", [128, 16]); tm4 = lsb("tm4", [128, 4])
                BC = [lsb("BC%d" % i, [128, 64]) for i in range(8)]; E1 = [lsb("E1%d" % i, [128, 64]) for i in range(8)]
                E2 = [lsb("E2%d" % i, [128, 64]) for i in range(8)]; eb = [lsb("eb%d" % i, [128, 1]) for i in range(8)]
                qd = [lsb("qd%d" % i, [128, 64], BF) for i in range(8)]; ki = [lsb("ki%d" % i, [128, 64], BF) for i in range(8)]
                ke = [lsb("ke%d" % i, [128, 64], BF) for i in range(8)]; KE = [lsb("KE%d" % i, [64, 128], BF) for i in range(8)]
                at = [lsb("at%d" % i, [64, 64], BF) for i in range(8)]
                sq2 = lsb("sq2", [128, 512]); Mt = lsb("Mt", [128, 512]); Vt = lsb("Vt", [128, 512]); tt = lsb("tt", [128, 512])
                gng = lsb("gng", [128, 8])
                with nc.allow_non_contiguous_dma(reason="tiny"):
                    T.dma('sp', gng[:], gla_norm_g.rearrange("o (c p) -> p (o c)", p=128), writes=['gng'], semkey='gng')

                def ln_gate(n, cols_out, rg_ap, rgkey):
                    for h in range(4):
                        _, ps1, ps1k = psum()
                        _, ps2, ps2k = psum()
                        for c2 in range(2):
                            c = h * 2 + c2
                            T.op('act', lambda e: e.activation(out=sq2[:, :n], in_=OF[:, c, :n], func=AF.Square), reads=['OF'], writes=['sq2'])
                            T.op('pe', lambda e: e.matmul(ps1[:, :n], lhsT=onesf[:, 0:128], rhs=OF[:, c, :n], start=(c2 == 0), stop=(c2 == 1)),
                                 reads=['onesf', 'OF'], writes=[ps1k])
                            T.op('pe', lambda e: e.matmul(ps2[:, :n], lhsT=onesf[:, 0:128], rhs=sq2[:, :n], start=(c2 == 0), stop=(c2 == 1)),
                                 reads=['onesf', 'sq2'], writes=[ps2k])
                        T.op('act', lambda e: e.activation(out=Mt[:, :n], in_=ps1[:, :n], func=AF.Copy, scale=1.0 / GDV), reads=[ps1k], writes=['Mt'])
                        T.op('dve', lambda e: e.tensor_tensor(out=Vt[:, :n], in0=Mt[:, :n], in1=Mt[:, :n], op=ALU.mult), reads=['Mt'], writes=['Vt'])
                        T.op('dve', lambda e: e.scalar_tensor_tensor(out=Vt[:, :n], in0=ps2[:, :n], scalar=1.0 / GDV, in1=Vt[:, :n], op0=ALU.mult, op1=ALU.subtract),
                             reads=[ps2k, 'Vt'], writes=['Vt'])
                        T.op('dve', lambda e: e.tensor_scalar(out=Vt[:, :n], in0=Vt[:, :n], scalar1=EPS, scalar2=None, op0=ALU.add), reads=['Vt'], writes=['Vt'])
                        T.op('act', lambda e: e.activation(out=Vt[:, :n], in_=Vt[:, :n], func=AF.Sqrt), reads=['Vt'], writes=['Vt'])
                        T.op('dve', lambda e: e.reciprocal(out=Vt[:, :n], in_=Vt[:, :n]), reads=['Vt'], writes=['Vt'])
                        for c2 in range(2):
                            c = h * 2 + c2
                            T.op('dve', lambda e: e.tensor_tensor(out=tt[:, :n], in0=OF[:, c, :n], in1=Mt[:, :n], op=ALU.subtract), reads=['OF', 'Mt'], writes=['tt'])
                            T.op('dve', lambda e: e.tensor_tensor(out=tt[:, :n], in0=tt[:, :n], in1=Vt[:, :n], op=ALU.mult), reads=['tt', 'Vt'], writes=['tt'])
                            T.op('dve', lambda e: e.scalar_tensor_tensor(out=OTg[:, c, cols_out:cols_out + n], in0=tt[:, :n], scalar=gng[:, c:c + 1],
                                                                         in1=rg_ap[:, c, :n], op0=ALU.mult, op1=ALU.mult),
                                 reads=['tt', 'gng', rgkey], writes=['OTg'])

                kk = [0]

                def run(with_out):
                    for g4 in range(4):
                        gc = g4 * 512
                        T.dma('sp', KGt[:], gkT.rearrange("h p t -> p h t")[:, :, gc:gc + 512], reads=['d:gk'], writes=['KGt'], semkey='KGt')
                        T.dma('sp', LA[:], glaT.rearrange("h p t -> p h t")[:, :, gc:gc + 512], reads=['d:gla'], writes=['LA'], semkey='LA')
                        T.dma('sp', VT[:], gv[gc:gc + 512, :].rearrange("(c p) d -> p c d", p=64), reads=['d:gv'], writes=['VT'], semkey='VT')
                        if with_out:
                            T.dma('sp', QG[:], gqT.rearrange("h p t -> p h t")[:, :, gc:gc + 512], reads=['d:gq'], writes=['QG'], semkey='QG')
                            T.dma('sp', RG[:], grT.rearrange("c p t -> p c t")[:, :, gc:gc + 512], reads=['d:gr'], writes=['RG'], semkey='RG')
                        for ch in range(8):
                            cs = slice(ch * 64, ch * 64 + 64)
                            HB = [(h, (kk[0] % 2) * 4 + h) for h in range(4)]
                            kk[0] += 1
                            for (h, b) in HB:
                                T.op('dve', lambda e: e.tensor_tensor_scan(out=BC[b][:], data0=onesf[:, 0:64], data1=LA[:, h, cs], initial=0.0,
                                                                           op0=ALU.mult, op1=ALU.add), reads=['LA', 'onesf'], writes=['BC%d' % b])
                            for (h, b) in HB:
                                T.op('act', lambda e: e.activation(out=eb[b][:], in_=BC[b][:, 63:64], func=AF.Exp), reads=['BC%d' % b], writes=['eb%d' % b])
                                T.op('act', lambda e: e.activation(out=E2[b][:], in_=BC[b][:], func=AF.Exp, scale=-1.0), reads=['BC%d' % b], writes=['E2%d' % b])
                                if with_out:
                                    T.op('act', lambda e: e.activation(out=E1[b][:], in_=BC[b][:], func=AF.Exp), reads=['BC%d' % b], writes=['E1%d' % b])
                            for (h, b) in HB:
                                T.op('dve', lambda e: e.scalar_tensor_tensor(out=ke[b][:], in0=KGt[:, h, cs], scalar=eb[b][:, 0:1], in1=E2[b][:],
                                                                             op0=ALU.mult, op1=ALU.mult), reads=['KGt', 'eb%d' % b, 'E2%d' % b], writes=['ke%d' % b])
                                if with_out:
                                    T.op('dve', lambda e: e.tensor_tensor(out=qd[b][:], in0=QG[:, h, cs], in1=E1[b][:], op=ALU.mult), reads=['QG', 'E1%d' % b], writes=['qd%d' % b])
                                    T.op('dve', lambda e: e.tensor_tensor(out=ki[b][:], in0=KGt[:, h, cs], in1=E2[b][:], op=ALU.mult), reads=['KGt', 'E2%d' % b], writes=['ki%d' % b])
                            pts = {}
                            for (h, b) in HB:
                                _, pt_, ptk = psum()
                                ptb = pt_[:].bitcast(BF)
                                T.op('pe', lambda e: e.transpose(out=ptb[0:64, 0:128], in_=ke[b][:, :], identity=identb[:, :]), reads=['ke%d' % b, 'identb'], writes=[ptk])
                                T.op('act', lambda e: e.copy(out=KE[b][:], in_=ptb[0:64, 0:128]), reads=[ptk], writes=['KE%d' % b])
                            if with_out:
                                for (h, b) in HB:
                                    _, pa, pak = psum()
                                    T.op('pe', lambda e: e.matmul(pa[0:64, 0:64], lhsT=ki[b][:, :], rhs=qd[b][:, :], start=True, stop=True),
                                         reads=['ki%d' % b, 'qd%d' % b], writes=[pak])
                                    T.op('dve', lambda e: e.tensor_tensor(out=at[b][:], in0=pa[0:64, 0:64], in1=trif[0:64, 0:64], op=ALU.mult),
                                         reads=[pak, 'trif'], writes=['at%d' % b])
                                for (h, b) in HB:
                                    for c2 in range(2):
                                        _, po, pok = psum()
                                        T.op('pe', lambda e: e.matmul(po[:, 0:64], lhsT=VT[:, ch, h * 256 + c2 * 128:h * 256 + (c2 + 1) * 128], rhs=at[b][:, :],
                                                                      start=True, stop=False), reads=['VT', 'at%d' % b], writes=[pok])
                                        T.op('pe', lambda e: e.matmul(po[:, 0:64], lhsT=Sb[:, h, c2 * 128:(c2 + 1) * 128], rhs=qd[b][:, :],
                                                                      start=False, stop=True), reads=['Sb%d' % h, 'qd%d' % b], writes=[pok])
                                        T.op('act', lambda e: e.copy(out=OF[:, h * 2 + c2, cs], in_=po[:, 0:64]), reads=[pok], writes=['OF'])
                            for (h, b) in HB:
                                _, pst, pstk = psum()
                                T.op('pe', lambda e: e.matmul(pst[:, 0:256], lhsT=KE[b][:, :], rhs=VT[:, ch, h * 256:(h + 1) * 256], start=True, stop=True),
                                     reads=['KE%d' % b, 'VT'], writes=[pstk])
                                T.op('dve', lambda e: e.scalar_tensor_tensor(out=Sf[:, h, :], in0=Sf[:, h, :], scalar=eb[b][:, 0:1], in1=pst[:, 0:256],
                                                                             op0=ALU.mult, op1=ALU.add), reads=['Sf%d' % h, 'eb%d' % b, pstk], writes=['Sf%d' % h])
                                T.op('act', lambda e: e.copy(out=Sb[:, h, :], in_=Sf[:, h, :]), reads=['Sf%d' % h], writes=['Sb%d' % h])
                                if not with_out:
                                    T.op('dve', lambda e: e.tensor_tensor(out=Gs[:, h:h + 1], in0=Gs[:, h:h + 1], in1=BC[b][:, 63:64], op=ALU.add),
                                         reads=['Gs%d' % h, 'BC%d' % b], writes=['Gs%d' % h])
                        if with_out:
                            ln_gate(512, gc, RG, 'RG')

                T.op('dve', lambda e: e.memset(Sf[:], 0.0), writes=['Sf0', 'Sf1', 'Sf2', 'Sf3'])
                T.op('dve', lambda e: e.memset(Sb[:], 0.0), writes=['Sb0', 'Sb1', 'Sb2', 'Sb3'])
                T.op('dve', lambda e: e.memset(Gs[:], 0.0), writes=['Gs0', 'Gs1', 'Gs2', 'Gs3'])
                run(False)
                T.dma('sp', sloc.rearrange("(h p) c -> p h c", p=128)[:, :, 0:GDV], Sf[:], reads=['Sf0', 'Sf1', 'Sf2', 'Sf3'], writes=['d:sloc'], semkey='Sf_st')
                with nc.allow_non_contiguous_dma(reason="tiny"):
                    T.dma('sp', sloc.rearrange("(h p) c -> p h c", p=128)[:, :, GDV:GDV + 1], Gs[:].rearrange("p (h o) -> p h o", o=1), reads=['Gs0', 'Gs1', 'Gs2', 'Gs3'], writes=['d:sloc2'], semkey='Gs_st')
                T.barrier()
                allgather(sloc, sg_, 'd:sloc', 'd:sg', 'cc_s')
                T.dma('sp', SG[:], sg_.rearrange("(g h p) c -> p g h c", g=4, h=4), reads=['d:sg'], writes=['SG'], semkey='SG')
                for h in range(4):
                    for rp in range(4):
                        T.op('dve', lambda e: e.tensor_tensor(out=tm4[:], in0=SG[:, :, h, GDV], in1=crank[:, 20 + rp * 4:24 + rp * 4], op=ALU.mult),
                             reads=['SG', 'crank'], writes=['tm4'])
                        T.op('dve', lambda e: e.tensor_reduce(out=Lt[:, h * 4 + rp:h * 4 + rp + 1], in_=tm4[:], axis=AX.X, op=ALU.add), reads=['tm4'], writes=['Lt'])
                T.op('act', lambda e: e.activation(out=Et[:], in_=Lt[:], func=AF.Exp), reads=['Lt'], writes=['Et'])
                for h in range(4):
                    T.op('dve', lambda e: e.tensor_tensor(out=Et[:, h * 4:h * 4 + 4], in0=Et[:, h * 4:h * 4 + 4], in1=crank[:, 36:40], op=ALU.mult),
                         reads=['Et', 'crank'], writes=['Et'])
                for h in range(4):
                    T.op('dve', lambda e: e.tensor_scalar(out=Sf[:, h, :], in0=SG[:, 0, h, 0:GDV], scalar1=Et[:, h * 4:h * 4 + 1], scalar2=None, op0=ALU.mult),
                         reads=['SG', 'Et'], writes=['Sf%d' % h])
                    for rp in range(1, 4):
                        T.op('dve', lambda e: e.scalar_tensor_tensor(out=Sf[:, h, :], in0=SG[:, rp, h, 0:GDV], scalar=Et[:, h * 4 + rp:h * 4 + rp + 1],
                                                                     in1=Sf[:, h, :], op0=ALU.mult, op1=ALU.add), reads=['SG', 'Et', 'Sf%d' % h], writes=['Sf%d' % h])
                    T.op('act', lambda e: e.copy(out=Sb[:, h, :], in_=Sf[:, h, :]), reads=['Sf%d' % h], writes=['Sb%d' % h])
                run(True)
                T.dma('sp', o_sp.rearrange("(h p) c -> p h c", p=128), Sf[:], reads=['Sf0', 'Sf1', 'Sf2', 'Sf3'], semkey='Sf_st')
                T.barrier()
                QSs = lsb("QSs", [128, 4, NS], BF); LAs = lsb("LAs", [128, 4, NS]); ELs = lsb("ELs", [128, 4, NS]); RGs = lsb("RGs", [128, 8, NS], BF)
                KVb = lsb("KVb", [1, NS, D + 512], BF)
                S0 = [lsb("S0%d" % i, [128, 256]) for i in range(2)]; Snb = [lsb("Snb%d" % i, [128, 256], BF) for i in range(2)]
                with nc.allow_non_contiguous_dma(reason="tiny"):
                    T.dma('sp', QSs[:], gqT.rearrange("h p t -> p h t")[:, :, NP_:NT], reads=['d:gq'], writes=['QSs'], semkey='QSs')
                    T.dma('sp', LAs[:], glaT.rearrange("h p t -> p h t")[:, :, NP_:NT], reads=['d:gla'], writes=['LAs'], semkey='LAs')
                    T.dma('sp', RGs[:], grT.rearrange("c p t -> p c t")[:, :, NP_:NT], reads=['d:gr'], writes=['RGs'], semkey='RGs')
                T.dma('pool', KVb[:], bass.AP(gsv.tensor, 0, [[0, 1], [D + 512, NS], [1, D + 512]]), reads=['d:gsv'], writes=['KVb'], semkey='KVb')
                T.op('act', lambda e: e.activation(out=ELs[:], in_=LAs[:], func=AF.Exp), reads=['LAs'], writes=['ELs'])
                i_ = 0
                for s_ in range(NS):
                    for h in range(4):
                        b = i_ % 2
                        i_ += 1
                        T.dma('sp', S0[b][:], sgla[s_, h], writes=['S0%d' % b], semkey='S0%d' % b)
                        _, pkv, pkvk = psum()
                        T.op('pe', lambda e: e.matmul(pkv[:, 0:256], lhsT=KVb[0:1, s_, D + h * 128:D + (h + 1) * 128], rhs=KVb[0:1, s_, h * 256:(h + 1) * 256],
                                                      start=True, stop=True), reads=['KVb'], writes=[pkvk])
                        T.op('dve', lambda e: e.scalar_tensor_tensor(out=S0[b][:], in0=S0[b][:], scalar=ELs[:, h, s_:s_ + 1], in1=pkv[:, 0:256],
                                                                     op0=ALU.mult, op1=ALU.add), reads=['S0%d' % b, 'ELs', pkvk], writes=['S0%d' % b])
                        T.dma('sp', o_ss[(s_ * 4 + h) * 128:(s_ * 4 + h + 1) * 128, :], S0[b][:], reads=['S0%d' % b], semkey='S0%d_st' % b)
                        T.op('act', lambda e: e.copy(out=Snb[b][:], in_=S0[b][:]), reads=['S0%d' % b], writes=['Snb%d' % b])
                        for c2 in range(2):
                            _, po, pok = psum()
                            T.op('pe', lambda e: e.matmul(po[:, 0:1], lhsT=Snb[b][:, c2 * 128:(c2 + 1) * 128], rhs=QSs[:, h, s_:s_ + 1], start=True, stop=True),
                                 reads=['Snb%d' % b, 'QSs'], writes=[pok])
                            T.op('act', lambda e: e.copy(out=OF[:, h * 2 + c2, s_:s_ + 1], in_=po[:, 0:1]), reads=[pok], writes=['OF'])
                ln_gate(NS, NP_, RGs, 'RGs')
                T.barrier()

        with contextlib.ExitStack() as s1:
            cT = s1.enter_context(nc.sbuf_tensor("cT", [16, NP_], F32))
            with contextlib.ExitStack() as w1:
                alloc_work(w1)
                ffn_stage(0, 0, 0, x_in, xres)
                fox_proj(cT)
            chk(1)
            with contextlib.ExitStack() as s_ot:
                OT = s_ot.enter_context(nc.sbuf_tensor("OT", [64, 16, NT], BF))
                fox_attn(cT, OT)
                chk(5)
                with contextlib.ExitStack() as w2:
                    alloc_work(w2)
                    wo_stage(OT, fox_w_o, 0, 1)
                chk(6)
        with contextlib.ExitStack() as w3:
            alloc_work(w3)
            ffn_stage(0, 1, 2, xres, xres)
            ffn_stage(1, 0, 0, xres, xres)
            chk(7)
            gla_proj()
            chk(8)
            with contextlib.ExitStack() as s2:
                OTg = s2.enter_context(nc.sbuf_tensor("OTg", [128, 8, NT], BF))
                gla_rec(OTg)
                chk(9)
                wo_stage(OTg, gla_w_o, 1, 1)
            ffn_stage(1, 1, 2, xres, o_y)
        T.finish()
    return nc


_NC = None


def kernel(x_prompt, x_sample, cache_fox_k, cache_fox_v, cache_fox_logf, state_gla, page_table,
           ln_g, ln_b, ffn_w_in, ffn_w_out, fox_w_in, fox_b_f, fox_w_o,
           gla_w_in, gla_w_a2, gla_b_a, gla_norm_g, gla_w_o):
    global _NC
    f = lambda a: np.ascontiguousarray(np.asarray(a))
    NPOOL = int(np.asarray(cache_fox_k).shape[1])
    nc = build(8, NPOOL)
    ident = np.eye(128, dtype=np.float32)
    r_ = np.arange(128)
    tri = (r_[:, None] <= r_[None, :]).astype(np.float32)
    slow = (r_[:, None] > r_[None, :]).astype(np.float32)
    ck = f(cache_fox_k).reshape(NPOOL * 64, 2 * D)
    cv = f(cache_fox_v).reshape(NPOOL * 64, 2 * D)
    clf = f(cache_fox_logf).reshape(NPOOL * 2, 64 * H)
    shared = dict(cache_k=ck, cache_v=cv, cache_lf=clf, ln_g=f(ln_g), ln_b=f(ln_b), ffn_w_in=f(ffn_w_in),
                  ffn_w_out=f(ffn_w_out), fox_w_in=f(fox_w_in)[0], fox_b_f=f(fox_b_f), fox_w_o=f(fox_w_o)[0],
                  gla_w_in=f(gla_w_in)[0], gla_w_a2=f(gla_w_a2)[0], gla_b_a=f(gla_b_a), gla_norm_g=f(gla_norm_g),
                  gla_w_o=f(gla_w_o)[0], c_ident=ident, c_tri=tri, c_slow=slow)
    xp = f(x_prompt); xs = f(x_sample); sg = f(state_gla); pt = f(page_table)
    in_maps = []
    for c in range(8):
        b, r = c // 4, c % 4
        x_in = np.concatenate([xp[b, r * NP_:(r + 1) * NP_, :], xs[4 * c:4 * c + 4, 0, :]], axis=0)
        cr = np.zeros((128, 80), np.float32)
        cr[:, 40] = np.arange(128) % 2
        cr[:, 48:80] = np.arange(32)[None, :]
        for a in range(4):
            for bb in range(4):
                cr[:, a * 4 + bb] = 1.0 if (a <= bb < r) else 0.0
                cr[:, 20 + a * 4 + bb] = 1.0 if (a < bb < r) else 0.0
            cr[:, 16 + a] = 0.0 if a < r else PEN
            cr[:, 36 + a] = 1.0 if a < r else 0.0
        m = dict(shared)
        m.update(x_in=np.ascontiguousarray(x_in), state_gla=np.ascontiguousarray(sg[0, 4 * c:4 * c + 4]),
                 page_table=np.ascontiguousarray(pt[4 * c:4 * c + 4]), c_rank=cr)
        in_maps.append(m)
    res = run_bass_kernel_spmd(nc, in_maps, core_ids=list(range(8)))
    R = res.results
    y_p = np.zeros((2, 8192, D), np.float32); y_s = np.zeros((32, 1, D), np.float32)
    k_p = np.zeros((1, 2, 8192, H, HD), np.float32); v_p = np.zeros_like(k_p); lf_p = np.zeros((1, 2, 8192, H), np.float32)
    k_s = np.zeros((1, 32, 1, H, HD), np.float32); v_s = np.zeros_like(k_s); lf_s = np.zeros((1, 32, 1, H), np.float32)
    sg_p = np.zeros((1, 2, GH, GDK, GDV), np.float32); sg_s = np.zeros((1, 32, GH, GDK, GDV), np.float32)
    for c in range(8):
        b, r = c // 4, c % 4
        sl = slice(r * NP_, (r + 1) * NP_)
        y_p[b, sl] = R[c]["o_y"][:NP_]; y_s[4 * c:4 * c + 4, 0] = R[c]["o_y"][NP_:]
        k_p[0, b, sl] = R[c]["o_k"][:NP_].reshape(NP_, H, HD); k_s[0, 4 * c:4 * c + 4, 0] = R[c]["o_k"][NP_:].reshape(4, H, HD)
        v_p[0, b, sl] = R[c]["o_v"][:NP_].reshape(NP_, H, HD); v_s[0, 4 * c:4 * c + 4, 0] = R[c]["o_v"][NP_:].reshape(4, H, HD)
        lf_p[0, b, sl] = R[c]["o_lf"][:NP_]; lf_s[0, 4 * c:4 * c + 4, 0] = R[c]["o_lf"][NP_:]
        if r == 3:
            sg_p[0, b] = R[c]["o_sp"].reshape(GH, GDK, GDV)
        sg_s[0, 4 * c:4 * c + 4] = R[c]["o_ss"].reshape(4, GH, GDK, GDV)
    return (y_p, y_s, k_p, v_p, lf_p, k_s, v_s, lf_s, sg_p, sg_s)
```

```python
import contextlib
import os
import numpy as np
import concourse.bass as bass
import concourse.mybir as mybir
from concourse.bass_utils import run_bass_kernel_spmd

F32 = mybir.dt.float32
BF = mybir.dt.bfloat16
I32 = mybir.dt.int32
AF = mybir.ActivationFunctionType
ALU = mybir.AluOpType
AX = mybir.AxisListType

D = 1024
NP_ = 2048
NS = 4
NT = NP_ + NS
DFF = 2816
H = 16
HD = 64
ALPHA = 4.0 ** 0.25
EPS = 1e-5
GH = 4
GDK = 128
GDV = 256
GC = 64
NPOOL = 2560
PEN = -30000.0

PASSES = [
    dict(c0=0, groups=[(0, 512), (512, 512)], subs=[(i * 128, 128) for i in range(8)]),
    dict(c0=1024, groups=[(1024, 512), (1536, 512), (2048, 4)],
         subs=[(1024 + i * 128, 128) for i in range(8)] + [(2048, 4)]),
]


class _Stop(Exception):
    pass


_TRS = []


def chk(level):
    if int(os.environ.get('KSTOP', '0')) == level:
        _TRS[-1].finish()
        _TRS[-1].stopped = True


class TR:
    def __init__(self, nc, es):
        self.nc = nc
        self.es = es
        self.E = {'pe': nc.tensor, 'act': nc.scalar, 'dve': nc.vector, 'pool': nc.gpsimd, 'sp': nc.sync}
        self.esem = {}
        self.ecnt = {}
        self.nsem = 0
        for e in self.E:
            self._newsem(e)
        self.waited = {e: {} for e in self.E}
        self.lastw = {}
        self.readers = {}
        self.dsem = {}
        self.dcnt = {}
        self.allsems = []
        self.freed = []
        self.psi = 0
        self.stopped = False
        _TRS.append(self)

    def _newsem(self, e):
        self.nsem += 1
        s = self.es.enter_context(self.nc.semaphore("se_%s_%d" % (e, self.nsem)))
        self.esem[e] = s
        self.ecnt[e] = 0

    def _need(self, eng, toks):
        for (sem, val, owner) in toks:
            if eng == 'pe' and owner == 'pe':
                continue
            w = self.waited[eng]
            k = id(sem)
            if w.get(k, 0) >= val:
                continue
            self.E[eng].wait_ge(sem, val)
            w[k] = val

    def deps(self, eng, reads, writes):
        toks = []
        for k in reads:
            if k in self.lastw:
                toks.append(self.lastw[k])
        for k in writes:
            if k in self.lastw:
                toks.append(self.lastw[k])
            toks += list(self.readers.get(k, {}).values())
        self._need(eng, toks)

    def commit(self, tok, reads, writes):
        for k in reads:
            self.readers.setdefault(k, {})[id(tok[0])] = tok
        for k in writes:
            self.lastw[k] = tok
            self.readers[k] = {}

    def op(self, eng, fn, reads=(), writes=()):
        if self.stopped:
            return None
        self.deps(eng, reads, writes)
        ins = fn(self.E[eng])
        if self.ecnt[eng] >= 30000:
            self._newsem(eng)
        self.ecnt[eng] += 1
        ins.then_inc(self.esem[eng], 1)
        tok = (self.esem[eng], self.ecnt[eng], eng)
        self.commit(tok, reads, writes)
        return tok

    def _dsem(self, key):
        if key not in self.dsem:
            if self.freed:
                sem, cnt = self.freed.pop()
                self.dsem[key] = sem
                self.dcnt[key] = cnt
            else:
                self.nsem += 1
                self.dsem[key] = self.es.enter_context(self.nc.semaphore("sd_%d" % self.nsem))
                self.dcnt[key] = 0
        return self.dsem[key]

    def dma(self, q, out, in_, reads=(), writes=(), semkey=None, inc=16, fn=None, **kw):
        if self.stopped:
            return None
        self.deps(q, reads, writes)
        sem = self._dsem(semkey)
        if fn is None:
            ins = self.E[q].dma_start(out=out, in_=in_, **kw)
        else:
            ins = fn(self.E[q])
        self.dcnt[semkey] += inc
        ins.then_inc(sem, inc)
        tok = (sem, self.dcnt[semkey], 'dma')
        self.commit(tok, reads, writes)
        return tok

    def barrier(self):
        if self.stopped:
            return
        toks = [(self.esem[e], self.ecnt[e], e) for e in self.E if self.ecnt[e] > 0]
        toks += [(self.dsem[k], self.dcnt[k], 'dma') for k in self.dsem if self.dcnt[k] > 0]
        for e in self.E:
            self._need(e, [t for t in toks if not (t[2] == e and e != 'pe')] if e != 'pe' else [t for t in toks if t[2] != 'pe'])
        for k in list(self.dsem):
            self.freed.append((self.dsem[k], self.dcnt[k]))
            del self.dsem[k]
            del self.dcnt[k]

    def finish(self):
        if self.stopped:
            return
        toks = [(self.esem[e], self.ecnt[e], e) for e in self.E if self.ecnt[e] > 0 and e != 'sp']
        toks += [(self.dsem[k], self.dcnt[k], 'dma') for k in self.dsem if self.dcnt[k] > 0]
        toks += [(sm, c, 'dma') for (sm, c) in self.freed if c > 0]
        self._need('sp', toks)


def build(ncores=8, NPOOL=NPOOL):
    nc = bass.Bass("TRN2", target_bir_lowering=False, num_devices=ncores)
    dt = nc.dram_tensor

    def din(name, shape, dtype=F32):
        return dt(name, list(shape), dtype, kind="ExternalInput").ap()

    def dout(name, shape, dtype=F32):
        return dt(name, list(shape), dtype, kind="ExternalOutput").ap()

    def dint(name, shape, dtype=F32):
        return dt(name, list(shape), dtype, kind="Internal").ap()

    x_in = din("x_in", [NT, D])
    ck = din("cache_k", [NPOOL * 64, 2 * D])
    cv = din("cache_v", [NPOOL * 64, 2 * D])
    clf = din("cache_lf", [NPOOL * 2, 64 * H])
    sgla = din("state_gla", [NS, GH, GDK, GDV])
    ptab = din("page_table", [NS, 64], I32)
    ln_g = din("ln_g", [2, 3, D])
    ln_b = din("ln_b", [2, 3, D])
    ffn_w_in = din("ffn_w_in", [2, 2, D, 2 * DFF])
    ffn_w_out = din("ffn_w_out", [2, 2, DFF, D])
    fox_w_in = din("fox_w_in", [D, 3 * D + H])
    fox_b_f = din("fox_b_f", [1, H])
    fox_w_o = din("fox_w_o", [D, D])
    gla_w_in = din("gla_w_in", [D, 3088])
    gla_w_a2 = din("gla_w_a2", [16, 512])
    gla_b_a = din("gla_b_a", [1, 512])
    gla_norm_g = din("gla_norm_g", [1, D])
    gla_w_o = din("gla_w_o", [D, D])
    c_ident = din("c_ident", [128, 128])
    c_tri = din("c_tri", [128, 128])
    c_slow = din("c_slow", [128, 128])
    c_rank = din("c_rank", [128, 80])
    o_y = dout("o_y", [NT, D])
    o_k = dout("o_k", [NT, D])
    o_v = dout("o_v", [NT, D])
    o_lf = dout("o_lf", [NT, H])
    o_sp = dout("o_sp", [GH * GDK, GDV])
    o_ss = dout("o_ss", [NS * GH * GDK, GDV])
    xres = dint("xres", [NT, D])
    qT_d = dint("qT_d", [H, HD, NP_], BF)
    ktloc_c = [dint("ktloc%d" % i, [140, NP_], BF) for i in range(8)]
    ktg_c = [dint("ktg%d" % i, [560, NP_], BF) for i in range(8)]
    vloc_c = [dint("vloc%d" % i, [256, 1040], BF) for i in range(8)]
    vg_c = [dint("vg%d" % i, [1024, 1040], BF) for i in range(8)]
    totloc_f = dint("totloc", [1, 128])
    totloc = totloc_f[:, 0:H]
    totg_f = dint("totg", [4, 128])
    totg = totg_f[:, 0:H]
    qaug_d = dint("qaug_d", [5, H, 6, NP_], BF)
    srow_d = dint("srow_d", [NS, 3 * D + 2 * H])
    gq_d = dint("gq_d", [NT, 512])
    sloc = dint("sloc", [GH * GDK, GDV + 64])
    sg_ = dint("sgath", [4 * GH * GDK, GDV + 64])

    _uc = [0]

    def un(name):
        _uc[0] += 1
        return "%s_%d" % (name, _uc[0])

    es = contextlib.ExitStack()
    with es:
        T = TR(nc, es)
        sb = lambda name, shape, dtype=F32: es.enter_context(nc.sbuf_tensor(un(name), list(shape), dtype))
        PS = [es.enter_context(nc.psum_tensor("psb%d" % i, [128, 512], F32)) for i in range(8)]

        PSN = [6]

        def psum():
            T.psi = (T.psi + 1) % PSN[0]
            i = T.psi
            return i, PS[i], 'ps%d' % i

        def psum_r(i):
            return i, PS[i], 'ps%d' % i

        identf = sb("identf", [128, 128]); identb = sb("identb", [128, 128], BF)
        trib = sb("trib", [128, 128], BF); slowf = sb("slowf", [128, 128])
        onesf = sb("onesf", [128, 512]); onesb = sb("onesb", [128, 128], BF)
        crank = sb("crank", [128, 80])
        trif = sb("trif", [128, 128])
        T.dma('sp', identf[:], c_ident, writes=['identf'], semkey='c0')
        T.dma('sp', trif[:], c_tri, writes=['trif'], semkey='c1')
        T.dma('sp', slowf[:], c_slow, writes=['slowf'], semkey='c2')
        T.dma('sp', crank[:], c_rank, writes=['crank'], semkey='c3')
        T.op('dve', lambda e: e.tensor_copy(out=identb[:], in_=identf[:]), reads=['identf'], writes=['identb'])
        T.op('dve', lambda e: e.tensor_copy(out=trib[:], in_=trif[:]), reads=['trif'], writes=['trib'])
        T.op('dve', lambda e: e.memset(onesf[:], 1.0), writes=['onesf'])
        mbb = sb("mbb", [128, 128], BF)
        T.op('dve', lambda e: e.tensor_scalar(out=mbb[:], in0=slowf[:], scalar1=PEN, scalar2=None, op0=ALU.mult), reads=['slowf'], writes=['mbb'])
        T.op('dve', lambda e: e.memset(onesb[:], 1.0), writes=['onesb'])

        class _W:
            pass
        WKH = _W()

        def alloc_work(stack):
            a = lambda name, shape, dtype=F32: stack.enter_context(nc.sbuf_tensor(un(name), list(shape), dtype))
            WKH.xT = a("xT", [128, 8, 1028], BF)
            WKH.ld_f = [a("ld_f%d" % i, [128, D]) for i in range(2)]
            WKH.ld_b = [a("ld_b%d" % i, [128, D], BF) for i in range(2)]
            WKH.Gt = a("Gt", [128, D]); WKH.Bt = a("Bt", [128, D])
            WKH.zt = a("zt", [128, D]); WKH.xo = [a("xo%d" % i, [128, D]) for i in range(2)]
            WKH.junk = a("junk", [128, D])
            WKH.st_ = a("stats", [128, 16])
            T.barrier()
        cnt = dict(ld=0, xo=0, misc=0)

        def load_xT(pas, src):
            xT, ld_f, ld_b, Gt, Bt, zt, xo, junk, st_ = WKH.xT, WKH.ld_f, WKH.ld_b, WKH.Gt, WKH.Bt, WKH.zt, WKH.xo, WKH.junk, WKH.st_
            c0 = pas['c0']
            for (col, n) in pas['subs']:
                s = cnt['ld'] % 2
                cnt['ld'] += 1
                T.dma('sp', ld_f[s][:n, :], src[col:col + n, :], reads=['d:xres:%d' % col], writes=['ld_f%d' % s],
                      semkey='ld_f%d' % s)
                T.op('act', lambda e: e.copy(out=ld_b[s][:n, :], in_=ld_f[s][:n, :]), reads=['ld_f%d' % s],
                     writes=['ld_b%d' % s])
                pi, pt, pk = psum()
                ptb = pt[:].bitcast(BF)
                for dc in range(8):
                    T.op('pe', lambda e: e.transpose(out=ptb[:, dc * 128:dc * 128 + n], in_=ld_b[s][:n, dc * 128:(dc + 1) * 128],
                                                     identity=identb[:n, :n]),
                         reads=['ld_b%d' % s, 'identb'], writes=[pk])
                src_v = ptb[:, 0:1024].rearrange("p (c t) -> p c t", c=8)[:, :, 0:n]
                T.op('dve', lambda e: e.tensor_copy(out=xT[:, :, col - c0:col - c0 + n], in_=src_v), reads=[pk],
                     writes=['xT'])

        def ln_stage(col, n, ybanks, res_src, gi, gk, dst, gb_loaded):
            xT, ld_f, ld_b, Gt, Bt, zt, xo, junk, st_ = WKH.xT, WKH.ld_f, WKH.ld_b, WKH.Gt, WKH.Bt, WKH.zt, WKH.xo, WKH.junk, WKH.st_
            if not gb_loaded:
                T.dma('sp', Gt[:], ln_g[gi, gk:gk + 1, :].partition_broadcast(128), writes=['Gt'], semkey='Gt')
                T.dma('sp', Bt[:], ln_b[gi, gk:gk + 1, :].partition_broadcast(128), writes=['Bt'], semkey='Bt')
            s = cnt['ld'] % 2
            cnt['ld'] += 1
            xr = ld_f[s]
            T.dma('sp', xr[:n, :], res_src[col:col + n, :], reads=['d:xres:%d' % col], writes=['ld_f%d' % s],
                  semkey='ld_f%d' % s)
            for hf in range(2):
                (yp, yk) = ybanks[hf]
                T.op('dve', lambda e: e.scalar_tensor_tensor(out=zt[:n, hf * 512:(hf + 1) * 512], in0=xr[:n, hf * 512:(hf + 1) * 512],
                                                             scalar=ALPHA, in1=yp[:n, :], op0=ALU.mult, op1=ALU.add,
                                                             accum_out=st_[:n, hf:hf + 1]),
                     reads=['ld_f%d' % s, yk], writes=['zt', 'st'])
            T.op('act', lambda e: e.activation(out=junk[:n, :], in_=zt[:n, :], func=AF.Square, accum_out=st_[:n, 2:3]),
                 reads=['zt'], writes=['junk', 'st2'])
            T.op('dve', lambda e: e.tensor_tensor(out=st_[:n, 3:4], in0=st_[:n, 0:1], in1=st_[:n, 1:2], op=ALU.add),
                 reads=['st'], writes=['st'])
            T.op('dve', lambda e: e.tensor_scalar(out=st_[:n, 4:5], in0=st_[:n, 3:4], scalar1=1.0 / D, scalar2=None, op0=ALU.mult),
                 reads=['st'], writes=['st'])
            T.op('dve', lambda e: e.tensor_tensor(out=st_[:n, 5:6], in0=st_[:n, 4:5], in1=st_[:n, 4:5], op=ALU.mult),
                 reads=['st'], writes=['st'])
            T.op('dve', lambda e: e.scalar_tensor_tensor(out=st_[:n, 6:7], in0=st_[:n, 2:3], scalar=1.0 / D, in1=st_[:n, 5:6],
                                                         op0=ALU.mult, op1=ALU.subtract),
                 reads=['st', 'st2'], writes=['st'])
            T.op('dve', lambda e: e.tensor_scalar(out=st_[:n, 6:7], in0=st_[:n, 6:7], scalar1=EPS, scalar2=None, op0=ALU.add),
                 reads=['st'], writes=['st'])
            T.op('act', lambda e: e.activation(out=st_[:n, 7:8], in_=st_[:n, 6:7], func=AF.Sqrt), reads=['st'], writes=['st3'])
            T.op('dve', lambda e: e.reciprocal(out=st_[:n, 8:9], in_=st_[:n, 7:8]), reads=['st3'], writes=['st'])
            T.op('dve', lambda e: e.tensor_scalar(out=zt[:n, :], in0=zt[:n, :], scalar1=st_[:n, 4:5], scalar2=st_[:n, 8:9],
                                                  op0=ALU.subtract, op1=ALU.mult), reads=['zt', 'st'], writes=['zt'])
            T.op('dve', lambda e: e.tensor_tensor(out=zt[:n, :], in0=zt[:n, :], in1=Gt[:n, :], op=ALU.mult),
                 reads=['zt', 'Gt'], writes=['zt'])
            o = cnt['xo'] % 2
            cnt['xo'] += 1
            T.op('dve', lambda e: e.tensor_tensor(out=xo[o][:n, :], in0=zt[:n, :], in1=Bt[:n, :], op=ALU.add),
                 reads=['zt', 'Bt'], writes=['xo%d' % o])
            T.dma('sp', dst[col:col + n, :], xo[o][:n, :], reads=['xo%d' % o], writes=['d:xres:%d' % col], semkey='xo%d_st' % o)

        def ffn_stage(li, hi, gk, src, dst):
            xT, ld_f, ld_b, Gt, Bt, zt, xo, junk, st_ = WKH.xT, WKH.ld_f, WKH.ld_b, WKH.Gt, WKH.Bt, WKH.zt, WKH.xo, WKH.junk, WKH.st_
            with contextlib.ExitStack() as ls:
                lsb = lambda name, shape, dtype=F32: ls.enter_context(nc.sbuf_tensor(un(name), list(shape), dtype))
                aT = lsb("aT", [128, 22, 1028], BF)
                WOUT = lsb("WOUT", [128, 22, D], BF)
                WIN = [lsb("WIN%d" % i, [128, 8, 2, 256], BF) for i in range(2)]
                sg = [lsb("sg%d" % i, [128, 512]) for i in range(2)]
                w_in = ffn_w_in[li, hi].rearrange("(c p) f -> p c f", p=128)
                w_out = ffn_w_out[li, hi].rearrange("(c p) d -> p c d", p=128)
                first = True
                for pas in PASSES:
                    c0 = pas['c0']
                    load_xT(pas, src)

                    def load_w(jc):
                        s = jc % 2
                        for gu in range(2):
                            T.dma('pool', WIN[s][:, :, gu, :], w_in[:, :, gu * DFF + jc * 256: gu * DFF + (jc + 1) * 256],
                                  writes=['WIN%d_%d' % (s, gu)], semkey='WIN%d_%d' % (s, gu))
                        T.dma('pool', WOUT[:, 2 * jc:2 * jc + 2, :], w_out[:, 2 * jc:2 * jc + 2, :], writes=['WOUT%d' % jc],
                              semkey='WOUT%d' % jc)

                    load_w(0)
                    k = 0
                    for jc in range(11):
                        if jc + 1 < 11:
                            load_w(jc + 1)
                        s = jc % 2
                        for sub in range(2):
                            f = 2 * jc + sub
                            for (gc, gn) in pas['groups']:
                                lc = gc - c0
                                _, pg, pgk = psum()
                                _, pu, puk = psum()
                                for dc in range(8):
                                    T.op('pe', lambda e: e.matmul(pg[:, :gn], lhsT=WIN[s][:, dc, 0, sub * 128:(sub + 1) * 128],
                                                                  rhs=xT[:, dc, lc:lc + gn], start=(dc == 0), stop=(dc == 7)),
                                         reads=['WIN%d_0' % s, 'xT'], writes=[pgk])
                                for dc in range(8):
                                    T.op('pe', lambda e: e.matmul(pu[:, :gn], lhsT=WIN[s][:, dc, 1, sub * 128:(sub + 1) * 128],
                                                                  rhs=xT[:, dc, lc:lc + gn], start=(dc == 0), stop=(dc == 7)),
                                         reads=['WIN%d_1' % s, 'xT'], writes=[puk])
                                q = k % 2
                                k += 1
                                T.op('act', lambda e: e.activation(out=sg[q][:, :gn], in_=pg[:, :gn], func=AF.Silu),
                                     reads=[pgk], writes=['sg%d' % q])
                                T.op('dve', lambda e: e.scalar_tensor_tensor(out=aT[:, f, lc:lc + gn], in0=pu[:, :gn], scalar=0.5,
                                                                             in1=sg[q][:, :gn], op0=ALU.mult, op1=ALU.mult),
                                     reads=[puk, 'sg%d' % q], writes=['aT'])
                    for (col, n) in pas['subs']:
                        lc = col - c0
                        _, y0, y0k = psum()
                        _, y1, y1k = psum()
                        for f in range(22):
                            T.op('pe', lambda e: e.matmul(y0[:n, :], lhsT=aT[:, f, lc:lc + n], rhs=WOUT[:, f, 0:512],
                                                          start=(f == 0), stop=(f == 21)),
                                 reads=['aT', 'WOUT%d' % (f // 2)], writes=[y0k])
                            T.op('pe', lambda e: e.matmul(y1[:n, :], lhsT=aT[:, f, lc:lc + n], rhs=WOUT[:, f, 512:1024],
                                                          start=(f == 0), stop=(f == 21)),
                                 reads=['aT', 'WOUT%d' % (f // 2)], writes=[y1k])
                        ln_stage(col, n, [(y0, y0k), (y1, y1k)], src, li, gk, dst, not first)
                        first = False
                T.barrier()

        def wo_stage(OT, w_o_ap, li, gk):
            with contextlib.ExitStack() as ls:
                KP, NCH = OT.shape[0], OT.shape[1]
                WO = ls.enter_context(nc.sbuf_tensor(un("WO"), [KP, NCH, D], BF))
                hh = NCH // 2
                T.dma('pool', WO[:, 0:hh, :], w_o_ap.rearrange("(h p) d -> p h d", p=KP)[:, 0:hh, :], writes=['WO0'], semkey='WO0')
                T.dma('pool', WO[:, hh:NCH, :], w_o_ap.rearrange("(h p) d -> p h d", p=KP)[:, hh:NCH, :], writes=['WO1'], semkey='WO1')
                first = True
                for pas in PASSES:
                    for (col, n) in pas['subs']:
                        _, y0, y0k = psum()
                        _, y1, y1k = psum()
                        for h in range(NCH):
                            T.op('pe', lambda e: e.matmul(y0[:n, :], lhsT=OT[:, h, col:col + n], rhs=WO[:, h, 0:512],
                                                          start=(h == 0), stop=(h == NCH - 1)),
                                 reads=['OT', 'OTs', 'WO%d' % (h // hh)], writes=[y0k])
                            T.op('pe', lambda e: e.matmul(y1[:n, :], lhsT=OT[:, h, col:col + n], rhs=WO[:, h, 512:1024],
                                                          start=(h == 0), stop=(h == NCH - 1)),
                                 reads=['OT', 'OTs', 'WO%d' % (h // hh)], writes=[y1k])
                        ln_stage(col, n, [(y0, y0k), (y1, y1k)], xres, li, gk, xres, not first)
                        first = False
                T.barrier()

        def fox_proj(cT):
            xT, ld_f, ld_b, Gt, Bt, zt, xo, junk, st_ = WKH.xT, WKH.ld_f, WKH.ld_b, WKH.Gt, WKH.Bt, WKH.zt, WKH.xo, WKH.junk, WKH.st_
            with contextlib.ExitStack() as ls:
                lsb = lambda name, shape, dtype=F32: ls.enter_context(nc.sbuf_tensor(un(name), list(shape), dtype))
                WQ = lsb("WQ", [128, 8, D], BF); WK = lsb("WK", [128, 8, D], BF); WV = lsb("WV", [128, 8, D], BF)
                WF = lsb("WF", [128, 8, H], BF)
                QS = lsb("QS", [64, 16, 512], BF); KS = lsb("KS", [64, 16, 512], BF)
                ktm = [lsb("ktm%d" % i, [128, D]) for i in range(2)]
                vtm = [lsb("vtm%d" % i, [128, D]) for i in range(2)]
                qtm = lsb("qtm", [128, D])
                vau = [lsb("vau%d" % i, [128, 16, 65], BF) for i in range(2)]
                bfB = lsb("bfB", [128, H]); nbf = lsb("nbf", [16, 1]); bfc = lsb("bfc", [16, 1])
                lft = [lsb("lft%d" % i, [128, H]) for i in range(2)]
                lfT = lsb("lfT", [16, 512])
                wv = fox_w_in.rearrange("(c p) f -> p c f", p=128)
                for i, (W, nm) in enumerate([(WQ, 'WQ'), (WK, 'WK'), (WV, 'WV')]):
                    for hf in range(2):
                        T.dma('pool', W[:, :, hf * 512:(hf + 1) * 512], wv[:, :, i * D + hf * 512: i * D + (hf + 1) * 512],
                              writes=['%s%d' % (nm, hf)], semkey='%s%d' % (nm, hf))
                T.dma('pool', WF[:], wv[:, :, 3 * D:3 * D + H], writes=['WF'], semkey='WF')
                T.dma('sp', bfB[:], fox_b_f[0:1, :].partition_broadcast(128), writes=['bfB'], semkey='bfB')
                T.dma('sp', bfc[:], fox_b_f.rearrange("o h -> h o"), writes=['bfc'], semkey='bfc')
                T.op('dve', lambda e: e.tensor_scalar(out=nbf[:], in0=bfc[:], scalar1=-1.0, scalar2=None, op0=ALU.mult),
                     reads=['bfc'], writes=['nbf'])
                for i in range(2):
                    T.op('dve', lambda e: e.memset(vau[i][:], 1.0), writes=['vau%d' % i])
                k = 0
                for pas in PASSES:
                    c0 = pas['c0']
                    load_xT(pas, xres)
                    for (gc, gn) in pas['groups']:
                        if gn != 512:
                            continue
                        lc = gc - c0
                        for (W, nm, ST, scale) in [(WQ, 'WQ', QS, 0.125), (WK, 'WK', KS, 1.0)]:
                            for h in range(16):
                                _, pp, ppk = psum()
                                for dc in range(8):
                                    T.op('pe', lambda e: e.matmul(pp[:64, :], lhsT=W[:, dc, h * 64:(h + 1) * 64], rhs=xT[:, dc, lc:lc + 512],
                                                                  start=(dc == 0), stop=(dc == 7)),
                                         reads=['%s%d' % (nm, h // 8), 'xT'], writes=[ppk])
                                T.op('act', lambda e: e.activation(out=ST[:, h, :], in_=pp[:64, :], func=AF.Copy, scale=scale),
                                     reads=[ppk], writes=[nm + 'S'])
                        T.dma('sp', qT_d.rearrange("h d t -> d h t")[:, :, gc:gc + 512], QS[:], reads=['WQS'], writes=['d:qT'], semkey='QS_st')
                        for i8 in range(8):
                            T.dma('sp', ktloc_c[i8].rearrange("(h r) t -> r h t", r=70)[0:64, :, gc:gc + 512], KS[:, 2 * i8:2 * i8 + 2, :], reads=['WKS'],
                                  writes=['d:ktloc'], semkey='KS_st')
                        _, pf, pfk = psum()
                        for dc in range(8):
                            T.op('pe', lambda e: e.matmul(pf[:16, :], lhsT=WF[:, dc, :], rhs=xT[:, dc, lc:lc + 512], start=(dc == 0), stop=(dc == 7)),
                                 reads=['WF', 'xT'], writes=[pfk])
                        T.op('act', lambda e: e.activation(out=lfT[:], in_=pf[:16, :], func=AF.Exp, bias=nbf[:], scale=-1.0),
                             reads=[pfk, 'nbf'], writes=['lfT'])
                        T.op('act', lambda e: e.activation(out=lfT[:], in_=lfT[:], func=AF.Ln, bias=1.0, scale=1.0), reads=['lfT'], writes=['lfT'])
                        T.op('dve', lambda e: e.tensor_scalar(out=lfT[:], in0=lfT[:], scalar1=-1.0, scalar2=None, op0=ALU.mult),
                             reads=['lfT'], writes=['lfT'])
                        init = 0.0 if gc == 0 else cT[:, gc - 1:gc]
                        T.op('dve', lambda e: e.tensor_tensor_scan(out=cT[:, gc:gc + 512], data0=onesf[:16, :], data1=lfT[:], initial=init,
                                                                   op0=ALU.mult, op1=ALU.add), reads=['lfT', 'onesf', 'cT'], writes=['cT'])
                    for (col, n) in pas['subs']:
                        lc = col - c0
                        s = k % 2
                        k += 1
                        for (W, nm, tm) in [(WK, 'WK', ktm[s]), (WV, 'WV', vtm[s])] + ([(WQ, 'WQ', qtm)] if n == 4 else []):
                            key = tm.name if hasattr(tm, 'name') else nm
                            for hf in range(2):
                                _, pp, ppk = psum()
                                for dc in range(8):
                                    T.op('pe', lambda e: e.matmul(pp[:n, :], lhsT=xT[:, dc, lc:lc + n], rhs=W[:, dc, hf * 512:(hf + 1) * 512],
                                                                  start=(dc == 0), stop=(dc == 7)),
                                         reads=['%s%d' % (nm, hf), 'xT'], writes=[ppk])
                                T.op('act', lambda e: e.copy(out=tm[:n, hf * 512:(hf + 1) * 512], in_=pp[:n, :]), reads=[ppk],
                                     writes=['tm_%s_%d' % (nm, s)])
                        T.dma('sp', o_k[col:col + n, :], ktm[s][:n, :], reads=['tm_WK_%d' % s], semkey='ktm%d_st' % s)
                        T.dma('sp', o_v[col:col + n, :], vtm[s][:n, :], reads=['tm_WV_%d' % s], semkey='vtm%d_st' % s)
                        if n == 128:
                            T.op('dve', lambda e: e.tensor_copy(out=vau[s][:, :, 0:64], in_=vtm[s][:, :].rearrange("p (h d) -> p h d", h=16)),
                                 reads=['tm_WV_%d' % s], writes=['vau%d' % s])
                            for i8 in range(8):
                                T.dma('sp', vloc_c[i8].rearrange("a (b c) -> (a b) c", c=65).rearrange("(h t) c -> t h c", h=2)[col:col + 128, :, :],
                                      vau[s][:, 2 * i8:2 * i8 + 2, :], reads=['vau%d' % s], writes=['d:vloc'], semkey='vau%d_st' % s)
                        else:
                            T.dma('sp', srow_d[:, 0:D], qtm[:n, :], reads=['tm_WQ_%d' % s], writes=['d:srow'], semkey='qtm_st')
                            T.dma('sp', srow_d[:, D:2 * D], ktm[s][:n, :], reads=['tm_WK_%d' % s], writes=['d:srow'], semkey='ktm%d_st2' % s)
                            T.dma('sp', srow_d[:, 2 * D:3 * D], vtm[s][:n, :], reads=['tm_WV_%d' % s], writes=['d:srow'], semkey='vtm%d_st2' % s)
                        _, pf, pfk = psum()
                        for dc in range(8):
                            T.op('pe', lambda e: e.matmul(pf[:n, :16], lhsT=xT[:, dc, lc:lc + n], rhs=WF[:, dc, :], start=(dc == 0), stop=(dc == 7)),
                                 reads=['WF', 'xT'], writes=[pfk])
                        lf = lft[s]
                        T.op('dve', lambda e: e.tensor_tensor(out=lf[:n, :], in0=pf[:n, :16], in1=bfB[:n, :], op=ALU.add),
                             reads=[pfk, 'bfB'], writes=['lft%d' % s])
                        T.op('act', lambda e: e.activation(out=lf[:n, :], in_=lf[:n, :], func=AF.Exp, scale=-1.0), reads=['lft%d' % s], writes=['lft%d' % s])
                        T.op('act', lambda e: e.activation(out=lf[:n, :], in_=lf[:n, :], func=AF.Ln, bias=1.0, scale=1.0), reads=['lft%d' % s], writes=['lft%d' % s])
                        T.op('dve', lambda e: e.tensor_scalar(out=lf[:n, :], in0=lf[:n, :], scalar1=-1.0, scalar2=None, op0=ALU.mult),
                             reads=['lft%d' % s], writes=['lft%d' % s])
                        T.dma('sp', o_lf[col:col + n, :], lf[:n, :], reads=['lft%d' % s], semkey='lft%d_st' % s)
                        if n == 4:
                            T.dma('sp', srow_d[:, 3 * D:3 * D + H], lf[:n, :], reads=['lft%d' % s], writes=['d:srow'], semkey='lft%d_st2' % s)
                T.barrier()


        GROUPS4 = [[0, 1, 2, 3], [4, 5, 6, 7]]

        def allgather(src, dst, rk, wk, name):
            T.dma('pool', None, None, reads=[rk], writes=[wk], semkey=name, inc=1,
                  fn=lambda e: e.collective_compute("AllGather", ALU.bypass, replica_groups=GROUPS4, ins=[src], outs=[dst]))

        def fox_attn(cT, OT):
            with contextlib.ExitStack() as ls:
                lsb = lambda name, shape, dtype=F32: ls.enter_context(nc.sbuf_tensor(un(name), list(shape), dtype))
                pre = contextlib.ExitStack()
                psb = lambda name, shape, dtype=F32: pre.enter_context(nc.sbuf_tensor(un(name), list(shape), dtype))
                wk_ = psb("wk_", [16, NP_]); r1 = psb("r1", [16, NP_])
                hi = psb("hi", [16, NP_], BF); mid = psb("mid", [16, NP_], BF); lo = psb("lo", [16, NP_], BF)
                ones3 = psb("ones3", [16, 3, NP_], BF)
                TG = psb("TG", [16, 4]); Dp = psb("Dp", [16, 4]); tmp4 = psb("tmp4", [16, 4])
                T.op('dve', lambda e: e.memset(ones3[:], 1.0), writes=['ones3'])
                fox_sample_setup(psb)
                PSN[0] = 4

                def split3(srckey):
                    T.op('dve', lambda e: e.tensor_copy(out=hi[:], in_=wk_[:]), reads=[srckey], writes=['hi'])
                    T.op('dve', lambda e: e.tensor_tensor(out=r1[:], in0=wk_[:], in1=hi[:], op=ALU.subtract), reads=[srckey, 'hi'], writes=['r1'])
                    T.op('dve', lambda e: e.tensor_copy(out=mid[:], in_=r1[:]), reads=['r1'], writes=['mid'])
                    T.op('dve', lambda e: e.tensor_tensor(out=r1[:], in0=r1[:], in1=mid[:], op=ALU.subtract), reads=['r1', 'mid'], writes=['r1'])
                    T.op('dve', lambda e: e.tensor_copy(out=lo[:], in_=r1[:]), reads=['r1'], writes=['lo'])

                T.op('dve', lambda e: e.tensor_scalar(out=wk_[:], in0=cT[:], scalar1=-1.0, scalar2=None, op0=ALU.mult), reads=['cT'], writes=['wk_'])
                split3('wk_')
                for i8 in range(8):
                    ktv = ktloc_c[i8].rearrange("(h r) t -> h r t", r=70)
                    hs = slice(2 * i8, 2 * i8 + 2)
                    T.dma('sp', ktv[:, 64:67, :], ones3[hs], reads=['ones3'], writes=['d:kt_a'], semkey='ones3_st')
                    T.dma('sp', ktv[:, 67, :], hi[hs], reads=['hi'], writes=['d:kt_b'], semkey='hi_st')
                    T.dma('sp', ktv[:, 68, :], mid[hs], reads=['mid'], writes=['d:kt_c'], semkey='mid_st')
                    T.dma('sp', ktv[:, 69, :], lo[hs], reads=['lo'], writes=['d:kt_d'], semkey='lo_st')
                with nc.allow_non_contiguous_dma(reason="tiny"):
                    T.dma('sp', totloc.rearrange("o h -> h o"), cT[:, NP_ - 1:NP_], reads=['cT'], writes=['d:tot'], semkey='tot_st')
                T.barrier()
                chk(2)
                for i8 in range(8):
                    allgather(ktloc_c[i8], ktg_c[i8], 'd:kt_a', 'd:ktg', 'cc_kt%d' % i8)
                    allgather(vloc_c[i8], vg_c[i8], 'd:kt_a', 'd:vg', 'cc_v%d' % i8)
                allgather(totloc_f, totg_f, 'd:tot', 'd:totg', 'cc_tot')
                with nc.allow_non_contiguous_dma(reason="tiny"):
                    T.dma('sp', TG[:], totg.rearrange("r h -> h r"), reads=['d:totg'], writes=['TG'], semkey='TG')
                for rp in range(4):
                    T.op('dve', lambda e: e.tensor_tensor(out=tmp4[:], in0=TG[:], in1=crank[:16, rp * 4:rp * 4 + 4], op=ALU.mult),
                         reads=['TG', 'crank'], writes=['tmp4'])
                    T.op('dve', lambda e: e.tensor_reduce(out=Dp[:, rp:rp + 1], in_=tmp4[:], axis=AX.X, op=ALU.add), reads=['tmp4'], writes=['Dp'])
                T.op('dve', lambda e: e.tensor_tensor(out=Dp[:], in0=Dp[:], in1=crank[:16, 16:20], op=ALU.add), reads=['Dp', 'crank'], writes=['Dp'])
                for v in range(5):
                    if v == 0:
                        T.op('dve', lambda e: e.tensor_copy(out=wk_[:], in_=cT[:]), reads=['cT'], writes=['wk_'])
                    else:
                        T.op('dve', lambda e: e.tensor_scalar(out=wk_[:], in0=cT[:], scalar1=Dp[:, v - 1:v], scalar2=None, op0=ALU.add),
                             reads=['cT', 'Dp'], writes=['wk_'])
                    split3('wk_')
                    T.dma('sp', qaug_d[v, :, 0, :], hi[:], reads=['hi'], writes=['d:qa%d' % v], semkey='hi_st')
                    T.dma('sp', qaug_d[v, :, 1, :], mid[:], reads=['mid'], writes=['d:qb%d' % v], semkey='mid_st')
                    T.dma('sp', qaug_d[v, :, 2, :], lo[:], reads=['lo'], writes=['d:qc%d' % v], semkey='lo_st')
                    T.dma('sp', qaug_d[v, :, 3:6, :], ones3[:], reads=['ones3'], writes=['d:qd%d' % v], semkey='ones3_st')
                T.barrier()
                chk(3)
                pre.close()
                QA = lsb("QA", [70, 5, NP_], BF); KL = lsb("KL", [70, NP_], BF); KG = lsb("KG", [70, 4, NP_], BF)
                VL = lsb("VL", [128, 16, 65], BF); VG = lsb("VG", [128, 4, 16, 65], BF)
                PT = [lsb("PT%d" % i, [128, 512], BF) for i in range(3)]
                rdt = lsb("rdt", [65, 512]); bcs = lsb("bcs", [64, 512])
                sgen = fox_sample_gen(OT, lsb)
                next(sgen, None)
                next(sgen, None)
                ktgv = [ktg_c[i8].rearrange("(g h r) t -> h r g t", g=4, h=2) for i8 in range(8)]
                vlv = [vloc_c[i8].rearrange("a (b c) -> (a b) c", c=65).rearrange("(h kb p) c -> h p kb c", h=2, p=128) for i8 in range(8)]
                vgv = [vg_c[i8].rearrange("a (b c) -> (a b) c", c=65).rearrange("(g h p j) c -> h p g j c", g=4, h=2, p=128) for i8 in range(8)]
                KGv = KG[:].rearrange("r g (p j) -> r g j p", j=16)
                pk_ = 0
                for h in range(16):
                    T.dma('sp', QA[0:64, :, :], bass.AP(qT_d.tensor, h * 64 * NP_, [[NP_, 64], [0, 5], [1, NP_]]), writes=['QAq'], semkey='QAq')
                    T.dma('sp', QA[64:70, :, :], qaug_d[:, h, :, :].rearrange("v r t -> r v t"), writes=['QAa'], semkey='QAa')
                    T.dma('sp', KL[:], ktloc_c[h // 2][(h % 2) * 70:(h % 2 + 1) * 70, :], writes=['KL'], semkey='KL')
                    T.dma('sp', KG[:], ktgv[h // 2][h % 2], writes=['KG'], semkey='KG')
                    T.dma('sp', VL[:], vlv[h // 2][h % 2], writes=['VL'], semkey='VL')
                    T.dma('sp', VG[:], vgv[h // 2][h % 2], writes=['VG'], semkey='VG')
                    items = []
                    for qt in range(4):
                        blocks = [('L', kb, 0) for kb in range(4 * qt + 4)] + [('G', j, g) for g in range(4) for j in range(16)]
                        for bi, (kind, kb, g) in enumerate(blocks):
                            items.append((qt, kind, kb, g, bi == 0, bi == len(blocks) - 1))
                    st = {}

                    def stage_a(i):
                        (qt, kind, kb, g, first, last) = items[i]
                        qc = qt * 512
                        _, S, Sk = psum()
                        p = i % 3
                        if kind == 'L':
                            off = max(0, kb * 128 - qc)
                            n = 512 - off
                            dg = kb >= 4 * qt
                            T.op('pe', lambda e: e.matmul(S[:, :n], lhsT=KL[:, kb * 128:(kb + 1) * 128], rhs=QA[:, 0, qc + off:qc + 512],
                                                          start=True, stop=not dg), reads=['KL', 'QAq', 'QAa'], writes=[Sk])
                            if dg:
                                T.op('pe', lambda e: e.matmul(S[:, 0:128], lhsT=identb[:, :], rhs=mbb[:, :], start=False, stop=True),
                                     reads=['identb', 'mbb'], writes=[Sk])
                        else:
                            off = 0
                            n = 512
                            T.op('pe', lambda e: e.matmul(S[:, :n], lhsT=KGv[:, g, kb, :], rhs=QA[:, 1 + g, qc:qc + 512],
                                                          start=True, stop=True), reads=['KG', 'QAq', 'QAa'], writes=[Sk])
                        T.op('act', lambda e: e.activation(out=PT[p][:, :n], in_=S[:, :n], func=AF.Exp), reads=[Sk], writes=['PT%d' % p])
                        st[i] = (p, off, n)

                    def stage_b(i):
                        (qt, kind, kb, g, first, last) = items[i]
                        qc = qt * 512
                        (p, off, n) = st.pop(i)
                        _, O, Ok = psum_r(6 + (qt % 2))
                        lhsV = VL[:, kb, :] if kind == 'L' else VG[:, g, kb, :]
                        vkey = 'VL' if kind == 'L' else 'VG'
                        T.op('pe', lambda e: e.matmul(O[0:65, off:512], lhsT=lhsV, rhs=PT[p][:, :n], start=first, stop=last),
                             reads=[vkey, 'PT%d' % p], writes=[Ok])
                        if last:
                            T.op('dve', lambda e: e.reciprocal(out=rdt[64:65, :], in_=O[64:65, :]), reads=[Ok], writes=['rdt'])
                            _, bc, bck = psum()
                            T.op('pe', lambda e: e.matmul(bc[0:64, :], lhsT=onesf[64:65, 0:64], rhs=rdt[64:65, :], start=True, stop=True),
                                 reads=['rdt', 'onesf'], writes=[bck])
                            T.op('act', lambda e: e.copy(out=bcs[:], in_=bc[0:64, :]), reads=[bck], writes=['bcs'])
                            T.op('dve', lambda e: e.tensor_tensor(out=OT[:, h, qc:qc + 512], in0=O[0:64, :], in1=bcs[:], op=ALU.mult),
                                 reads=[Ok, 'bcs'], writes=['OT'])

                    LAH = 2
                    for i in range(len(items) + LAH):
                        if i < len(items):
                            stage_a(i)
                        if i >= LAH:
                            stage_b(i - LAH)
                        if i % 36 == 35:
                            next(sgen, None)
                for _ in sgen:
                    pass
                T.barrier()
                PSN[0] = 6

        def fox_sample_setup(psb):
            q4 = psb("q4", [4, D]); k4 = psb("k4", [4, D]); p4 = psb("p4", [4, D]); sn = psb("sn", [4, H]); pn = psb("pn", [4, H])
            T.dma('sp', q4[:], srow_d[:, 0:D], reads=['d:srow'], writes=['q4'], semkey='q4')
            T.dma('sp', k4[:], srow_d[:, D:2 * D], reads=['d:srow'], writes=['k4'], semkey='k4')
            T.op('dve', lambda e: e.tensor_tensor(out=p4[:], in0=q4[:], in1=k4[:], op=ALU.mult), reads=['q4', 'k4'], writes=['p4'])
            T.op('dve', lambda e: e.tensor_reduce(out=sn[:], in_=p4[:].rearrange("p (h d) -> p h d", h=16), axis=AX.X, op=ALU.add),
                 reads=['p4'], writes=['sn'])
            T.op('act', lambda e: e.activation(out=pn[:], in_=sn[:], func=AF.Exp, scale=0.125), reads=['sn'], writes=['pn'])
            T.dma('sp', srow_d[:, 3 * D + H:3 * D + 2 * H], pn[:], reads=['pn'], writes=['d:srow2'], semkey='pn_st')

        def fox_sample_gen(OT, lsb):
            RL = 3 * D + 2 * H
            NVB = 8
            rows = lsb("rows", [1, NS, 2 * H]); rowsb = lsb("rowsb", [1, NS, D], BF); pnb = lsb("pnb", [1, NS, H], BF)
            idx = lsb("idx", [128, 1], I32); idxf = lsb("idxf", [128, 1]); idxall = lsb("idxall", [128, 32], I32); idx2 = lsb("idx2", [128, 1], I32)
            LF = lsb("LF", [128, 64, H]); Bi = lsb("Bi", [128, 64, H]); Sc = lsb("Sc", [128, 64, H]); Pm = lsb("Pm", [128, 64, H], BF)
            KB = [lsb("KB%d" % i, [128, 2, D], BF) for i in range(2)]; VB = [lsb("VB%d" % i, [128, 2, D], BF) for i in range(NVB)]
            prod = lsb("prod", [128, 2, D], BF); qB = lsb("qB", [128, 2, D], BF)
            Tt = lsb("Tt", [128, H]); At = lsb("At", [128, H]); dpart = lsb("dpart", [128, H]); rden = lsb("rden", [16, 1])
            On = lsb("On", [16, D], BF)
            T.dma('sp', rows[:], bass.AP(srow_d.tensor, 3 * D, [[0, 1], [RL, NS], [1, 2 * H]]), reads=['d:srow2', 'd:srow'], writes=['rows'], semkey='rows')
            T.dma('pool', rowsb[:], bass.AP(srow_d.tensor, 2 * D, [[0, 1], [RL, NS], [1, D]]), reads=['d:srow'], writes=['rowsb'], semkey='rowsb')
            T.op('dve', lambda e: e.tensor_copy(out=pnb[:], in_=rows[:, :, H:2 * H]), reads=['rows'], writes=['pnb'])

            def vgather(j):
                b = j % NVB
                T.dma('pool', None, None, reads=['idxall'], writes=['VB%d' % b], semkey='VB%d' % b,
                      fn=lambda e: e.indirect_dma_start(out=VB[b][:].rearrange("p t d -> p (t d)"), out_offset=None, in_=cv,
                                                        in_offset=bass.IndirectOffsetOnAxis(ap=idxall[:, j:j + 1], axis=0)))

            for s_ in range(NS):
                with nc.allow_non_contiguous_dma(reason="tiny"):
                    T.dma('sp', idx[:], bass.AP(ptab.tensor, s_ * 64, [[1, 64], [0, 2], [1, 1]]), writes=['idx'], semkey='idx')
                T.op('dve', lambda e: e.tensor_scalar(out=idx2[:], in0=idx[:], scalar1=2.0, scalar2=crank[:, 40:41], op0=ALU.mult, op1=ALU.add),
                     reads=['idx', 'crank'], writes=['idx2'])
                T.op('dve', lambda e: e.tensor_scalar(out=idxf[:], in0=idx2[:], scalar1=32.0, scalar2=None, op0=ALU.mult), reads=['idx2'], writes=['idxf'])
                T.op('dve', lambda e: e.tensor_scalar(out=idxall[:], in0=crank[:, 48:80], scalar1=idxf[:, 0:1], scalar2=None, op0=ALU.add),
                     reads=['idxf', 'crank'], writes=['idxall'])
                T.dma('pool', None, None, reads=['idx2'], writes=['LF'], semkey='LF',
                      fn=lambda e: e.indirect_dma_start(out=LF[:].rearrange("p t h -> p (t h)"), out_offset=None, in_=clf,
                                                        in_offset=bass.IndirectOffsetOnAxis(ap=idx2[:, 0:1], axis=0)))
                for t in range(2):
                    T.dma('pool', qB[:, t, :], bass.AP(srow_d.tensor, s_ * RL, [[0, 128], [1, D]]), reads=['d:srow'], writes=['qB%d' % t], semkey='qB%d' % t)
                for j in range(32):
                    b = j % 2
                    T.dma('pool', None, None, reads=['idxall'], writes=['KB%d' % b], semkey='KB%d' % b,
                          fn=lambda e: e.indirect_dma_start(out=KB[b][:].rearrange("p t d -> p (t d)"), out_offset=None, in_=ck,
                                                            in_offset=bass.IndirectOffsetOnAxis(ap=idxall[:, j:j + 1], axis=0)))
                    T.op('dve', lambda e: e.tensor_tensor(out=prod[:], in0=KB[b][:], in1=qB[:], op=ALU.mult),
                         reads=['KB%d' % b, 'qB0', 'qB1'], writes=['prod'])
                    T.op('dve', lambda e: e.tensor_reduce(out=Sc[:, 2 * j:2 * j + 2, :].rearrange("p t h -> p (t h)"),
                                                          in_=prod[:].rearrange("p t (h d) -> p (t h) d", h=16), axis=AX.X, op=ALU.add),
                         reads=['prod'], writes=['Sc'])
                    if j % 2 == 1:
                        yield
                T.op('dve', lambda e: e.tensor_reduce(out=Tt[:], in_=LF[:].rearrange("p t h -> p h t"), axis=AX.X, op=ALU.add), reads=['LF'], writes=['Tt'])
                yield
                for j in range(NVB):
                    vgather(j)
                yield
                yield
                _, lp, lpk = psum()
                T.op('pe', lambda e: e.matmul(lp[:, 0:H], lhsT=slowf[:], rhs=Tt[:], start=True, stop=False), reads=['slowf', 'Tt'], writes=[lpk])
                T.op('pe', lambda e: e.matmul(lp[:, 0:H], lhsT=onesf[0:1, 0:128], rhs=rows[0:1, s_, 0:H], start=False, stop=True),
                     reads=['onesf', 'rows'], writes=[lpk])
                T.op('dve', lambda e: e.tensor_tensor(out=At[:], in0=lp[:, 0:H], in1=Tt[:], op=ALU.add), reads=[lpk, 'Tt'], writes=['At'])
                for h in range(16):
                    T.op('dve', lambda e: e.tensor_tensor_scan(out=Bi[:, :, h], data0=onesf[:, 0:64], data1=LF[:, :, h], initial=At[:, h:h + 1],
                                                               op0=ALU.mult, op1=ALU.subtract), reads=['LF', 'At', 'onesf'], writes=['Bi'])
                T.op('dve', lambda e: e.scalar_tensor_tensor(out=Sc[:].rearrange("p t h -> p (t h)"), in0=Sc[:].rearrange("p t h -> p (t h)"),
                                                             scalar=0.125, in1=Bi[:].rearrange("p t h -> p (t h)"), op0=ALU.mult, op1=ALU.add),
                     reads=['Sc', 'Bi'], writes=['Sc'])
                T.op('act', lambda e: e.activation(out=Pm[:].rearrange("p t h -> p (t h)"), in_=Sc[:].rearrange("p t h -> p (t h)"), func=AF.Exp),
                     reads=['Sc'], writes=['Pm'])
                T.op('dve', lambda e: e.tensor_reduce(out=dpart[:], in_=Pm[:].rearrange("p t h -> p h t"), axis=AX.X, op=ALU.add), reads=['Pm'], writes=['dpart'])
                yield
                yield
                _, dn, dnk = psum()
                T.op('pe', lambda e: e.matmul(dn[0:16, 0:1], lhsT=dpart[:], rhs=onesf[:, 0:1], start=True, stop=False), reads=['dpart', 'onesf'], writes=[dnk])
                T.op('pe', lambda e: e.matmul(dn[0:16, 0:1], lhsT=rows[0:1, s_, H:2 * H], rhs=onesf[0:1, 0:1], start=False, stop=True),
                     reads=['rows', 'onesf'], writes=[dnk])
                T.op('dve', lambda e: e.reciprocal(out=rden[:], in_=dn[0:16, 0:1]), reads=[dnk], writes=['rden'])
                _, O0, O0k = psum_r(4)
                _, O1, O1k = psum_r(5)
                for q in range(4):
                    for j in range(q * NVB, (q + 1) * NVB):
                        b = j % NVB
                        for t in range(2):
                            pos = 2 * j + t
                            T.op('pe', lambda e: e.matmul(O0[0:16, :], lhsT=Pm[:, pos, :], rhs=VB[b][:, t, 0:512], start=(pos == 0), stop=False),
                                 reads=['Pm', 'VB%d' % b], writes=[O0k])
                            T.op('pe', lambda e: e.matmul(O1[0:16, :], lhsT=Pm[:, pos, :], rhs=VB[b][:, t, 512:1024], start=(pos == 0), stop=False),
                                 reads=['Pm', 'VB%d' % b], writes=[O1k])
                    if q < 3:
                        for j in range((q + 1) * NVB, (q + 2) * NVB):
                            vgather(j)
                        yield
                        yield
                T.op('pe', lambda e: e.matmul(O0[0:16, :], lhsT=pnb[0:1, s_, :], rhs=rowsb[0:1, s_, 0:512], start=False, stop=True),
                     reads=['pnb', 'rowsb'], writes=[O0k])
                T.op('pe', lambda e: e.matmul(O1[0:16, :], lhsT=pnb[0:1, s_, :], rhs=rowsb[0:1, s_, 512:1024], start=False, stop=True),
                     reads=['pnb', 'rowsb'], writes=[O1k])
                T.op('dve', lambda e: e.tensor_scalar(out=On[:, 0:512], in0=O0[0:16, :], scalar1=rden[:, 0:1], scalar2=None, op0=ALU.mult),
                     reads=[O0k, 'rden'], writes=['On'])
                T.op('dve', lambda e: e.tensor_scalar(out=On[:, 512:1024], in0=O1[0:16, :], scalar1=rden[:, 0:1], scalar2=None, op0=ALU.mult),
                     reads=[O1k, 'rden'], writes=['On'])
                _, Z, Zk = psum()
                for h in range(16):
                    T.op('pe', lambda e: e.matmul(Z[0:64, h:h + 1], lhsT=On[:, h * 64:(h + 1) * 64], rhs=identb[0:16, h:h + 1], start=True, stop=True),
                         reads=['On', 'identb'], writes=[Zk])
                T.op('dve', lambda e: e.tensor_copy(out=OT[:, :, NP_ + s_], in_=Z[0:64, 0:16]), reads=[Zk], writes=['OTs'])
                yield

        gqT = dint("gqT", [GH, 128, NT], BF)
        gkT = dint("gkT", [GH, 128, NT], BF)
        glaT = dint("glaT", [GH, 128, NT])
        grT = dint("grT", [8, 128, NT], BF)
        gv = dint("gv", [NP_, D], BF)
        gsv = dint("gsv", [NS, D + 512])

        def gla_proj():
            xT, ld_f, ld_b, Gt, Bt, zt, xo, junk, st_ = WKH.xT, WKH.ld_f, WKH.ld_b, WKH.Gt, WKH.Bt, WKH.zt, WKH.xo, WKH.junk, WKH.st_
            with contextlib.ExitStack() as ls:
                lsb = lambda name, shape, dtype=F32: ls.enter_context(nc.sbuf_tensor(un(name), list(shape), dtype))
                WGQ = lsb("WGQ", [128, 8, 512], BF); WGK = lsb("WGK", [128, 8, 512], BF)
                WGV = lsb("WGV", [128, 8, D], BF); WGR = lsb("WGR", [128, 8, D], BF); WGA = lsb("WGA", [128, 8, 16], BF)
                WA2 = lsb("WA2", [16, 512], BF); nba = lsb("nba", [128, 4]); bac = lsb("bac", [128, 4])
                SQ = lsb("SQ", [128, 4, 512], BF); SK = lsb("SK", [128, 4, 512], BF); SL = lsb("SL", [128, 4, 512]); SR = lsb("SR", [128, 8, 512], BF)
                alr = lsb("alr", [16, 512], BF)
                vt = [lsb("vt%d" % i, [128, D], BF) for i in range(2)]
                vs_ = lsb("vs_", [4, D + 512])
                wv = gla_w_in.rearrange("(c p) f -> p c f", p=128)
                T.dma('pool', WGQ[:], wv[:, :, 0:512], writes=['WGQ'], semkey='WGQ')
                T.dma('pool', WGK[:], wv[:, :, 512:1024], writes=['WGK'], semkey='WGK')
                for hf in range(2):
                    T.dma('pool', WGV[:, :, hf * 512:(hf + 1) * 512], wv[:, :, 1024 + hf * 512:1024 + (hf + 1) * 512], writes=['WGV%d' % hf], semkey='WGV%d' % hf)
                    T.dma('pool', WGR[:, :, hf * 512:(hf + 1) * 512], wv[:, :, 2048 + hf * 512:2048 + (hf + 1) * 512], writes=['WGR%d' % hf], semkey='WGR%d' % hf)
                T.dma('pool', WGA[:], wv[:, :, 3072:3088], writes=['WGA'], semkey='WGA')
                T.dma('pool', WA2[:], gla_w_a2, writes=['WA2'], semkey='WA2')
                with nc.allow_non_contiguous_dma(reason="tiny"):
                    T.dma('sp', bac[:], gla_b_a.rearrange("o (h p) -> p (o h)", p=128), writes=['bac'], semkey='bac')
                T.op('dve', lambda e: e.tensor_scalar(out=nba[:], in0=bac[:], scalar1=-1.0, scalar2=None, op0=ALU.mult), reads=['bac'], writes=['nba'])
                k = 0
                for pas in PASSES:
                    c0 = pas['c0']
                    load_xT(pas, xres)
                    for (gc, gn) in pas['groups']:
                        lc = gc - c0
                        for (W, nm, ST, scale) in [(WGQ, 'WGQ', SQ, 128.0 ** -0.5), (WGK, 'WGK', SK, 1.0)]:
                            for h in range(4):
                                _, pp, ppk = psum()
                                for dc in range(8):
                                    T.op('pe', lambda e: e.matmul(pp[:, :gn], lhsT=W[:, dc, h * 128:(h + 1) * 128], rhs=xT[:, dc, lc:lc + gn],
                                                                  start=(dc == 0), stop=(dc == 7)), reads=[nm, 'xT'], writes=[ppk])
                                T.op('act', lambda e: e.activation(out=ST[:, h, :gn], in_=pp[:, :gn], func=AF.Copy, scale=scale), reads=[ppk], writes=[nm + 'S'])
                        T.dma('sp', gqT.rearrange("h p t -> p h t")[:, :, gc:gc + gn], SQ[:, :, :gn], reads=['WGQS'], writes=['d:gq'], semkey='SQ_st')
                        T.dma('sp', gkT.rearrange("h p t -> p h t")[:, :, gc:gc + gn], SK[:, :, :gn], reads=['WGKS'], writes=['d:gk'], semkey='SK_st')
                        _, pa, pak = psum()
                        for dc in range(8):
                            T.op('pe', lambda e: e.matmul(pa[:16, :gn], lhsT=WGA[:, dc, :], rhs=xT[:, dc, lc:lc + gn], start=(dc == 0), stop=(dc == 7)),
                                 reads=['WGA', 'xT'], writes=[pak])
                        T.op('act', lambda e: e.copy(out=alr[:, :gn], in_=pa[:16, :gn]), reads=[pak], writes=['alr'])
                        for h in range(4):
                            _, pp, ppk = psum()
                            T.op('pe', lambda e: e.matmul(pp[:, :gn], lhsT=WA2[:, h * 128:(h + 1) * 128], rhs=alr[:, :gn], start=True, stop=True),
                                 reads=['WA2', 'alr'], writes=[ppk])
                            T.op('act', lambda e: e.activation(out=SL[:, h, :gn], in_=pp[:, :gn], func=AF.Exp, bias=nba[:, h:h + 1], scale=-1.0),
                                 reads=[ppk, 'nba'], writes=['SL'])
                        T.op('act', lambda e: e.activation(out=SL[:, :, :gn], in_=SL[:, :, :gn], func=AF.Ln, bias=1.0, scale=1.0), reads=['SL'], writes=['SL'])
                        T.op('dve', lambda e: e.tensor_scalar(out=SL[:, :, :gn], in0=SL[:, :, :gn], scalar1=-1.0 / 16.0, scalar2=None, op0=ALU.mult),
                             reads=['SL'], writes=['SL'])
                        T.dma('sp', glaT.rearrange("h p t -> p h t")[:, :, gc:gc + gn], SL[:, :, :gn], reads=['SL'], writes=['d:gla'], semkey='SL_st')
                        for c in range(8):
                            _, pp, ppk = psum()
                            for dc in range(8):
                                T.op('pe', lambda e: e.matmul(pp[:, :gn], lhsT=WGR[:, dc, c * 128:(c + 1) * 128], rhs=xT[:, dc, lc:lc + gn],
                                                              start=(dc == 0), stop=(dc == 7)), reads=['WGR%d' % (c // 4), 'xT'], writes=[ppk])
                            T.op('act', lambda e: e.activation(out=SR[:, c, :gn], in_=pp[:, :gn], func=AF.Silu), reads=[ppk], writes=['SR'])
                        T.dma('sp', grT.rearrange("c p t -> p c t")[:, :, gc:gc + gn], SR[:, :, :gn], reads=['SR'], writes=['d:gr'], semkey='SR_st')
                    for (col, n) in pas['subs']:
                        lc = col - c0
                        s_ = k % 2
                        k += 1
                        dstt = vt[s_] if n == 128 else vs_
                        dk_ = ('vt%d' % s_) if n == 128 else 'vs_'
                        for hf in range(2):
                            _, pp, ppk = psum()
                            for dc in range(8):
                                T.op('pe', lambda e: e.matmul(pp[:n, :], lhsT=xT[:, dc, lc:lc + n], rhs=WGV[:, dc, hf * 512:(hf + 1) * 512],
                                                              start=(dc == 0), stop=(dc == 7)), reads=['WGV%d' % hf, 'xT'], writes=[ppk])
                            T.op('act', lambda e: e.copy(out=dstt[:n, hf * 512:(hf + 1) * 512], in_=pp[:n, :]), reads=[ppk], writes=[dk_])
                        if n == 128:
                            T.dma('sp', gv[col:col + n, :], vt[s_][:n, :], reads=[dk_], writes=['d:gv'], semkey='vt%d_st' % s_)
                        else:
                            _, pp, ppk = psum()
                            for dc in range(8):
                                T.op('pe', lambda e: e.matmul(pp[:n, :], lhsT=xT[:, dc, lc:lc + n], rhs=WGK[:, dc, :], start=(dc == 0), stop=(dc == 7)),
                                     reads=['WGK', 'xT'], writes=[ppk])
                            T.op('act', lambda e: e.copy(out=vs_[:n, D:D + 512], in_=pp[:n, :]), reads=[ppk], writes=['vs_'])
                            T.dma('sp', gsv[:, :], vs_[:, :], reads=['vs_'], writes=['d:gsv'], semkey='vs_st')
                T.barrier()

        def gla_rec(OTg):
            with contextlib.ExitStack() as ls:
                lsb = lambda name, shape, dtype=F32: ls.enter_context(nc.sbuf_tensor(un(name), list(shape), dtype))
                QG = lsb("QG", [128, 4, 512], BF); KGt = lsb("KGt", [128, 4, 512], BF); LA = lsb("LA", [128, 4, 512])
                VT = lsb("VT", [128, 4, D], BF); RG = lsb("RG", [128, 8, 512], BF); OF = lsb("OF", [128, 8, 512])
                Sf = lsb("Sf", [128, 4, 256]); Sb = lsb("Sb", [128, 4, 256], BF); Gs = lsb("Gs", [128, 4])
                SG = lsb("SG", [128, 4, 4, GDV + 64]); Lt = lsb("Lt", [128, 16]); Et = lsb("Et", [128, 16]); tm4 = lsb("tm4", [128, 4])
                BC = [lsb("BC%d" % i, [128, 128]) for i in range(4)]; E1 = [lsb("E1%d" % i, [128, 128]) for i in range(4)]
                E2 = [lsb("E2%d" % i, [128, 128]) for i in range(4)]; eb = [lsb("eb%d" % i, [128, 1]) for i in range(4)]
                qd = [lsb("qd%d" % i, [128, 128], BF) for i in range(4)]; ki = [lsb("ki%d" % i, [128, 128], BF) for i in range(4)]
                ke = [lsb("ke%d" % i, [128, 128], BF) for i in range(4)]; KE = [lsb("KE%d" % i, [128, 128], BF) for i in range(4)]
                at = [lsb("at%d" % i, [128, 128], BF) for i in range(4)]
                sq2 = lsb("sq2", [128, 512]); Mt = lsb("Mt", [128, 512]); Vt = lsb("Vt", [128, 512]); tt = lsb("tt", [128, 512])
                gng = lsb("gng", [128, 8])
                with nc.allow_non_contiguous_dma(reason="tiny"):
                    T.dma('sp', gng[:], gla_norm_g.rearrange("o (c p) -> p (o c)", p=128), writes=['gng'], semkey='gng')

                def ln_gate(n, cols_out, rg_ap, rgkey):
                    for h in range(4):
                        _, ps1, ps1k = psum()
                        _, ps2, ps2k = psum()
                        for c2 in range(2):
                            c = h * 2 + c2
                            T.op('act', lambda e: e.activation(out=sq2[:, :n], in_=OF[:, c, :n], func=AF.Square), reads=['OF'], writes=['sq2'])
                            T.op('pe', lambda e: e.matmul(ps1[:, :n], lhsT=onesf[:, 0:128], rhs=OF[:, c, :n], start=(c2 == 0), stop=(c2 == 1)),
                                 reads=['onesf', 'OF'], writes=[ps1k])
                            T.op('pe', lambda e: e.matmul(ps2[:, :n], lhsT=onesf[:, 0:128], rhs=sq2[:, :n], start=(c2 == 0), stop=(c2 == 1)),
                                 reads=['onesf', 'sq2'], writes=[ps2k])
                        T.op('act', lambda e: e.activation(out=Mt[:, :n], in_=ps1[:, :n], func=AF.Copy, scale=1.0 / GDV), reads=[ps1k], writes=['Mt'])
                        T.op('dve', lambda e: e.tensor_tensor(out=Vt[:, :n], in0=Mt[:, :n], in1=Mt[:, :n], op=ALU.mult), reads=['Mt'], writes=['Vt'])
                        T.op('dve', lambda e: e.scalar_tensor_tensor(out=Vt[:, :n], in0=ps2[:, :n], scalar=1.0 / GDV, in1=Vt[:, :n], op0=ALU.mult, op1=ALU.subtract),
                             reads=[ps2k, 'Vt'], writes=['Vt'])
                        T.op('dve', lambda e: e.tensor_scalar(out=Vt[:, :n], in0=Vt[:, :n], scalar1=EPS, scalar2=None, op0=ALU.add), reads=['Vt'], writes=['Vt'])
                        T.op('act', lambda e: e.activation(out=Vt[:, :n], in_=Vt[:, :n], func=AF.Sqrt), reads=['Vt'], writes=['Vt'])
                        T.op('dve', lambda e: e.reciprocal(out=Vt[:, :n], in_=Vt[:, :n]), reads=['Vt'], writes=['Vt'])
                        for c2 in range(2):
                            c = h * 2 + c2
                            T.op('dve', lambda e: e.tensor_tensor(out=tt[:, :n], in0=OF[:, c, :n], in1=Mt[:, :n], op=ALU.subtract), reads=['OF', 'Mt'], writes=['tt'])
                            T.op('dve', lambda e: e.tensor_tensor(out=tt[:, :n], in0=tt[:, :n], in1=Vt[:, :n], op=ALU.mult), reads=['tt', 'Vt'], writes=['tt'])
                            T.op('dve', lambda e: e.scalar_tensor_tensor(out=OTg[:, c, cols_out:cols_out + n], in0=tt[:, :n], scalar=gng[:, c:c + 1],
                                                                         in1=rg_ap[:, c, :n], op0=ALU.mult, op1=ALU.mult),
                                 reads=['tt', 'gng', rgkey], writes=['OTg'])

                kk = [0]

                def run(with_out):
                    for g4 in range(4):
                        gc = g4 * 512
                        T.dma('sp', KGt[:], gkT.rearrange("h p t -> p h t")[:, :, gc:gc + 512], reads=['d:gk'], writes=['KGt'], semkey='KGt')
                        T.dma('sp', LA[:], glaT.rearrange("h p t -> p h t")[:, :, gc:gc + 512], reads=['d:gla'], writes=['LA'], semkey='LA')
                        T.dma('sp', VT[:], gv[gc:gc + 512, :].rearrange("(c p) d -> p c d", p=128), reads=['d:gv'], writes=['VT'], semkey='VT')
                        if with_out:
                            T.dma('sp', QG[:], gqT.rearrange("h p t -> p h t")[:, :, gc:gc + 512], reads=['d:gq'], writes=['QG'], semkey='QG')
                            T.dma('sp', RG[:], grT.rearrange("c p t -> p c t")[:, :, gc:gc + 512], reads=['d:gr'], writes=['RG'], semkey='RG')
                        for ch in range(4):
                            cs = slice(ch * 128, ch * 128 + 128)
                            HB = [(h, h) for h in range(4)]
                            kk[0] += 1
                            for (h, b) in HB:
                                T.op('dve', lambda e: e.tensor_tensor_scan(out=BC[b][:], data0=onesf[:, 0:128], data1=LA[:, h, cs], initial=0.0,
                                                                           op0=ALU.mult, op1=ALU.add), reads=['LA', 'onesf'], writes=['BC%d' % b])
                            for (h, b) in HB:
                                T.op('act', lambda e: e.activation(out=eb[b][:], in_=BC[b][:, 127:128], func=AF.Exp), reads=['BC%d' % b], writes=['eb%d' % b])
                                T.op('act', lambda e: e.activation(out=E2[b][:], in_=BC[b][:], func=AF.Exp, scale=-1.0), reads=['BC%d' % b], writes=['E2%d' % b])
                                if with_out:
                                    T.op('act', lambda e: e.activation(out=E1[b][:], in_=BC[b][:], func=AF.Exp), reads=['BC%d' % b], writes=['E1%d' % b])
                            for (h, b) in HB:
                                T.op('dve', lambda e: e.scalar_tensor_tensor(out=ke[b][:], in0=KGt[:, h, cs], scalar=eb[b][:, 0:1], in1=E2[b][:],
                                                                             op0=ALU.mult, op1=ALU.mult), reads=['KGt', 'eb%d' % b, 'E2%d' % b], writes=['ke%d' % b])
                                if with_out:
                                    T.op('dve', lambda e: e.tensor_tensor(out=qd[b][:], in0=QG[:, h, cs], in1=E1[b][:], op=ALU.mult), reads=['QG', 'E1%d' % b], writes=['qd%d' % b])
                                    T.op('dve', lambda e: e.tensor_tensor(out=ki[b][:], in0=KGt[:, h, cs], in1=E2[b][:], op=ALU.mult), reads=['KGt', 'E2%d' % b], writes=['ki%d' % b])
                            pts = {}
                            for (h, b) in HB:
                                _, pt_, ptk = psum()
                                ptb = pt_[:].bitcast(BF)
                                T.op('pe', lambda e: e.transpose(out=ptb[0:128, 0:128], in_=ke[b][:, :], identity=identb[:, :]), reads=['ke%d' % b, 'identb'], writes=[ptk])
                                T.op('act', lambda e: e.copy(out=KE[b][:], in_=ptb[0:128, 0:128]), reads=[ptk], writes=['KE%d' % b])
                            if with_out:
                                for (h, b) in HB:
                                    _, pa, pak = psum()
                                    T.op('pe', lambda e: e.matmul(pa[0:128, 0:128], lhsT=ki[b][:, :], rhs=qd[b][:, :], start=True, stop=True),
                                         reads=['ki%d' % b, 'qd%d' % b], writes=[pak])
                                    T.op('dve', lambda e: e.tensor_tensor(out=at[b][:], in0=pa[0:128, 0:128], in1=trif[0:128, 0:128], op=ALU.mult),
                                         reads=[pak, 'trif'], writes=['at%d' % b])
                                for (h, b) in HB:
                                    for c2 in range(2):
                                        _, po, pok = psum()
                                        T.op('pe', lambda e: e.matmul(po[:, 0:128], lhsT=VT[:, ch, h * 256 + c2 * 128:h * 256 + (c2 + 1) * 128], rhs=at[b][:, :],
                                                                      start=True, stop=False), reads=['VT', 'at%d' % b], writes=[pok])
                                        T.op('pe', lambda e: e.matmul(po[:, 0:128], lhsT=Sb[:, h, c2 * 128:(c2 + 1) * 128], rhs=qd[b][:, :],
                                                                      start=False, stop=True), reads=['Sb%d' % h, 'qd%d' % b], writes=[pok])
                                        T.op('act', lambda e: e.copy(out=OF[:, h * 2 + c2, cs], in_=po[:, 0:128]), reads=[pok], writes=['OF'])
                            for (h, b) in HB:
                                _, pst, pstk = psum()
                                T.op('pe', lambda e: e.matmul(pst[:, 0:256], lhsT=KE[b][:, :], rhs=VT[:, ch, h * 256:(h + 1) * 256], start=True, stop=True),
                                     reads=['KE%d' % b, 'VT'], writes=[pstk])
                                T.op('dve', lambda e: e.scalar_tensor_tensor(out=Sf[:, h, :], in0=Sf[:, h, :], scalar=eb[b][:, 0:1], in1=pst[:, 0:256],
                                                                             op0=ALU.mult, op1=ALU.add), reads=['Sf%d' % h, 'eb%d' % b, pstk], writes=['Sf%d' % h])
                                T.op('act', lambda e: e.copy(out=Sb[:, h, :], in_=Sf[:, h, :]), reads=['Sf%d' % h], writes=['Sb%d' % h])
                                if not with_out:
                                    T.op('dve', lambda e: e.tensor_tensor(out=Gs[:, h:h + 1], in0=Gs[:, h:h + 1], in1=BC[b][:, 127:128], op=ALU.add),
                                         reads=['Gs%d' % h, 'BC%d' % b], writes=['Gs%d' % h])
                        if with_out:
                            ln_gate(512, gc, RG, 'RG')

                T.op('dve', lambda e: e.memset(Sf[:], 0.0), writes=['Sf0', 'Sf1', 'Sf2', 'Sf3'])
                T.op('dve', lambda e: e.memset(Sb[:], 0.0), writes=['Sb0', 'Sb1', 'Sb2', 'Sb3'])
                T.op('dve', lambda e: e.memset(Gs[:], 0.0), writes=['Gs0', 'Gs1', 'Gs2', 'Gs3'])
                run(False)
                T.dma('sp', sloc.rearrange("(h p) c -> p h c", p=128)[:, :, 0:GDV], Sf[:], reads=['Sf0', 'Sf1', 'Sf2', 'Sf3'], writes=['d:sloc'], semkey='Sf_st')
                with nc.allow_non_contiguous_dma(reason="tiny"):
                    T.dma('sp', sloc.rearrange("(h p) c -> p h c", p=128)[:, :, GDV:GDV + 1], Gs[:].rearrange("p (h o) -> p h o", o=1), reads=['Gs0', 'Gs1', 'Gs2', 'Gs3'], writes=['d:sloc2'], semkey='Gs_st')
                T.barrier()
                allgather(sloc, sg_, 'd:sloc', 'd:sg', 'cc_s')
                T.dma('sp', SG[:], sg_.rearrange("(g h p) c -> p g h c", g=4, h=4), reads=['d:sg'], writes=['SG'], semkey='SG')
                for h in range(4):
                    for rp in range(4):
                        T.op('dve', lambda e: e.tensor_tensor(out=tm4[:], in0=SG[:, :, h, GDV], in1=crank[:, 20 + rp * 4:24 + rp * 4], op=ALU.mult),
                             reads=['SG', 'crank'], writes=['tm4'])
                        T.op('dve', lambda e: e.tensor_reduce(out=Lt[:, h * 4 + rp:h * 4 + rp + 1], in_=tm4[:], axis=AX.X, op=ALU.add), reads=['tm4'], writes=['Lt'])
                T.op('act', lambda e: e.activation(out=Et[:], in_=Lt[:], func=AF.Exp), reads=['Lt'], writes=['Et'])
                for h in range(4):
                    T.op('dve', lambda e: e.tensor_tensor(out=Et[:, h * 4:h * 4 + 4], in0=Et[:, h * 4:h * 4 + 4], in1=crank[:, 36:40], op=ALU.mult),
                         reads=['Et', 'crank'], writes=['Et'])
                for h in range(4):
                    T.op('dve', lambda e: e.tensor_scalar(out=Sf[:, h, :], in0=SG[:, 0, h, 0:GDV], scalar1=Et[:, h * 4:h * 4 + 1], scalar2=None, op0=ALU.mult),
                         reads=['SG', 'Et'], writes=['Sf%d' % h])
                    for rp in range(1, 4):
                        T.op('dve', lambda e: e.scalar_tensor_tensor(out=Sf[:, h, :], in0=SG[:, rp, h, 0:GDV], scalar=Et[:, h * 4 + rp:h * 4 + rp + 1],
                                                                     in1=Sf[:, h, :], op0=ALU.mult, op1=ALU.add), reads=['SG', 'Et', 'Sf%d' % h], writes=['Sf%d' % h])
                    T.op('act', lambda e: e.copy(out=Sb[:, h, :], in_=Sf[:, h, :]), reads=['Sf%d' % h], writes=['Sb%d' % h])
                run(True)
                T.dma('sp', o_sp.rearrange("(h p) c -> p h c", p=128), Sf[:], reads=['Sf0', 'Sf1', 'Sf2', 'Sf3'], semkey='Sf_st')
                T.barrier()
                QSs = lsb("QSs", [128, 4, NS], BF); LAs = lsb("LAs", [128, 4, NS]); ELs = lsb("ELs", [128, 4, NS]); RGs = lsb("RGs", [128, 8, NS], BF)
                KVb = lsb("KVb", [1, NS, D + 512], BF)
                S0 = [lsb("S0%d" % i, [128, 256]) for i in range(2)]; Snb = [lsb("Snb%d" % i, [128, 256], BF) for i in range(2)]
                with nc.allow_non_contiguous_dma(reason="tiny"):
                    T.dma('sp', QSs[:], gqT.rearrange("h p t -> p h t")[:, :, NP_:NT], reads=['d:gq'], writes=['QSs'], semkey='QSs')
                    T.dma('sp', LAs[:], glaT.rearrange("h p t -> p h t")[:, :, NP_:NT], reads=['d:gla'], writes=['LAs'], semkey='LAs')
                    T.dma('sp', RGs[:], grT.rearrange("c p t -> p c t")[:, :, NP_:NT], reads=['d:gr'], writes=['RGs'], semkey='RGs')
                T.dma('pool', KVb[:], bass.AP(gsv.tensor, 0, [[0, 1], [D + 512, NS], [1, D + 512]]), reads=['d:gsv'], writes=['KVb'], semkey='KVb')
                T.op('act', lambda e: e.activation(out=ELs[:], in_=LAs[:], func=AF.Exp), reads=['LAs'], writes=['ELs'])
                i_ = 0
                for s_ in range(NS):
                    for h in range(4):
                        b = i_ % 2
                        i_ += 1
                        T.dma('sp', S0[b][:], sgla[s_, h], writes=['S0%d' % b], semkey='S0%d' % b)
                        _, pkv, pkvk = psum()
                        T.op('pe', lambda e: e.matmul(pkv[:, 0:256], lhsT=KVb[0:1, s_, D + h * 128:D + (h + 1) * 128], rhs=KVb[0:1, s_, h * 256:(h + 1) * 256],
                                                      start=True, stop=True), reads=['KVb'], writes=[pkvk])
                        T.op('dve', lambda e: e.scalar_tensor_tensor(out=S0[b][:], in0=S0[b][:], scalar=ELs[:, h, s_:s_ + 1], in1=pkv[:, 0:256],
                                                                     op0=ALU.mult, op1=ALU.add), reads=['S0%d' % b, 'ELs', pkvk], writes=['S0%d' % b])
                        T.dma('sp', o_ss[(s_ * 4 + h) * 128:(s_ * 4 + h + 1) * 128, :], S0[b][:], reads=['S0%d' % b], semkey='S0%d_st' % b)
                        T.op('act', lambda e: e.copy(out=Snb[b][:], in_=S0[b][:]), reads=['S0%d' % b], writes=['Snb%d' % b])
                        for c2 in range(2):
                            _, po, pok = psum()
                            T.op('pe', lambda e: e.matmul(po[:, 0:1], lhsT=Snb[b][:, c2 * 128:(c2 + 1) * 128], rhs=QSs[:, h, s_:s_ + 1], start=True, stop=True),
                                 reads=['Snb%d' % b, 'QSs'], writes=[pok])
                            T.op('act', lambda e: e.copy(out=OF[:, h * 2 + c2, s_:s_ + 1], in_=po[:, 0:1]), reads=[pok], writes=['OF'])
                ln_gate(NS, NP_, RGs, 'RGs')
                T.barrier()

        with contextlib.ExitStack() as s1:
            cT = s1.enter_context(nc.sbuf_tensor("cT", [16, NP_], F32))
            with contextlib.ExitStack() as w1:
                alloc_work(w1)
                ffn_stage(0, 0, 0, x_in, xres)
                fox_proj(cT)
            chk(1)
            with contextlib.ExitStack() as s_ot:
                OT = s_ot.enter_context(nc.sbuf_tensor("OT", [64, 16, NT], BF))
                fox_attn(cT, OT)
                chk(5)
                with contextlib.ExitStack() as w2:
                    alloc_work(w2)
                    wo_stage(OT, fox_w_o, 0, 1)
                chk(6)
        with contextlib.ExitStack() as w3:
            alloc_work(w3)
            ffn_stage(0, 1, 2, xres, xres)
            ffn_stage(1, 0, 0, xres, xres)
            chk(7)
            gla_proj()
            chk(8)
            with contextlib.ExitStack() as s2:
                OTg = s2.enter_context(nc.sbuf_tensor("OTg", [128, 8, NT], BF))
                gla_rec(OTg)
                chk(9)
                wo_stage(OTg, gla_w_o, 1, 1)
            ffn_stage(1, 1, 2, xres, o_y)
        T.finish()
    return nc


_NC = None


def kernel(x_prompt, x_sample, cache_fox_k, cache_fox_v, cache_fox_logf, state_gla, page_table,
           ln_g, ln_b, ffn_w_in, ffn_w_out, fox_w_in, fox_b_f, fox_w_o,
           gla_w_in, gla_w_a2, gla_b_a, gla_norm_g, gla_w_o):
    global _NC
    f = lambda a: np.ascontiguousarray(np.asarray(a))
    NPOOL = int(np.asarray(cache_fox_k).shape[1])
    nc = build(8, NPOOL)
    ident = np.eye(128, dtype=np.float32)
    r_ = np.arange(128)
    tri = (r_[:, None] <= r_[None, :]).astype(np.float32)
    slow = (r_[:, None] > r_[None, :]).astype(np.float32)
    ck = f(cache_fox_k).reshape(NPOOL * 64, 2 * D)
    cv = f(cache_fox_v).reshape(NPOOL * 64, 2 * D)
    clf = f(cache_fox_logf).reshape(NPOOL * 2, 64 * H)
    shared = dict(cache_k=ck, cache_v=cv, cache_lf=clf, ln_g=f(ln_g), ln_b=f(ln_b), ffn_w_in=f(ffn_w_in),
                  ffn_w_out=f(ffn_w_out), fox_w_in=f(fox_w_in)[0], fox_b_f=f(fox_b_f), fox_w_o=f(fox_w_o)[0],
                  gla_w_in=f(gla_w_in)[0], gla_w_a2=f(gla_w_a2)[0], gla_b_a=f(gla_b_a), gla_norm_g=f(gla_norm_g),
                  gla_w_o=f(gla_w_o)[0], c_ident=ident, c_tri=tri, c_slow=slow)
    xp = f(x_prompt); xs = f(x_sample); sg = f(state_gla); pt = f(page_table)
    in_maps = []
    for c in range(8):
        b, r = c // 4, c % 4
        x_in = np.concatenate([xp[b, r * NP_:(r + 1) * NP_, :], xs[4 * c:4 * c + 4, 0, :]], axis=0)
        cr = np.zeros((128, 80), np.float32)
        cr[:, 40] = np.arange(128) % 2
        cr[:, 48:80] = np.arange(32)[None, :]
        for a in range(4):
            for bb in range(4):
                cr[:, a * 4 + bb] = 1.0 if (a <= bb < r) else 0.0
                cr[:, 20 + a * 4 + bb] = 1.0 if (a < bb < r) else 0.0
            cr[:, 16 + a] = 0.0 if a < r else PEN
            cr[:, 36 + a] = 1.0 if a < r else 0.0
        m = dict(shared)
        m.update(x_in=np.ascontiguousarray(x_in), state_gla=np.ascontiguousarray(sg[0, 4 * c:4 * c + 4]),
                 page_table=np.ascontiguousarray(pt[4 * c:4 * c + 4]), c_rank=cr)
        in_maps.append(m)
    res = run_bass_kernel_spmd(nc, in_maps, core_ids=list(range(8)))
    R = res.results
    y_p = np.zeros((2, 8192, D), np.float32); y_s = np.zeros((32, 1, D), np.float32)
    k_p = np.zeros((1, 2, 8192, H, HD), np.float32); v_p = np.zeros_like(k_p); lf_p = np.zeros((1, 2, 8192, H), np.float32)
    k_s = np.zeros((1, 32, 1, H, HD), np.float32); v_s = np.zeros_like(k_s); lf_s = np.zeros((1, 32, 1, H), np.float32)
    sg_p = np.zeros((1, 2, GH, GDK, GDV), np.float32); sg_s = np.zeros((1, 32, GH, GDK, GDV), np.float32)
    for c in range(8):
        b, r = c // 4, c % 4
        sl = slice(r * NP_, (r + 1) * NP_)
        y_p[b, sl] = R[c]["o_y"][:NP_]; y_s[4 * c:4 * c + 4, 0] = R[c]["o_y"][NP_:]
        k_p[0, b, sl] = R[c]["o_k"][:NP_].reshape(NP_, H, HD); k_s[0, 4 * c:4 * c + 4, 0] = R[c]["o_k"][NP_:].reshape(4, H, HD)
        v_p[0, b, sl] = R[c]["o_v"][:NP_].reshape(NP_, H, HD); v_s[0, 4 * c:4 * c + 4, 0] = R[c]["o_v"][NP_:].reshape(4, H, HD)
        lf_p[0, b, sl] = R[c]["o_lf"][:NP_]; lf_s[0, 4 * c:4 * c + 4, 0] = R[c]["o_lf"][NP_:]
        if r == 3:
            sg_p[0, b] = R[c]["o_sp"].reshape(GH, GDK, GDV)
        sg_s[0, 4 * c:4 * c + 4] = R[c]["o_ss"].reshape(4, GH, GDK, GDV)
    return (y_p, y_s, k_p, v_p, lf_p, k_s, v_s, lf_s, sg_p, sg_s)
```

```python
import contextlib
import os
import numpy as np
import concourse.bass as bass
import concourse.mybir as mybir
from concourse.bass_utils import run_bass_kernel_spmd

F32 = mybir.dt.float32
BF = mybir.dt.bfloat16
I32 = mybir.dt.int32
AF = mybir.ActivationFunctionType
ALU = mybir.AluOpType
AX = mybir.AxisListType

D = 1024
NP_ = 2048
NS = 4
NT = NP_ + NS
DFF = 2816
H = 16
HD = 64
ALPHA = 4.0 ** 0.25
EPS = 1e-5
GH = 4
GDK = 128
GDV = 256
GC = 64
NPOOL = 2560
PEN = -30000.0

PASSES = [
    dict(c0=0, groups=[(0, 512), (512, 512)], subs=[(i * 128, 128) for i in range(8)]),
    dict(c0=1024, groups=[(1024, 512), (1536, 512), (2048, 4)],
         subs=[(1024 + i * 128, 128) for i in range(8)] + [(2048, 4)]),
]


class _Stop(Exception):
    pass


_TRS = []


def chk(level):
    if int(os.environ.get('KSTOP', '0')) == level:
        _TRS[-1].finish()
        _TRS[-1].stopped = True


class TR:
    def __init__(self, nc, es):
        self.nc = nc
        self.es = es
        self.E = {'pe': nc.tensor, 'act': nc.scalar, 'dve': nc.vector, 'pool': nc.gpsimd, 'sp': nc.sync}
        self.esem = {}
        self.ecnt = {}
        self.nsem = 0
        for e in self.E:
            self._newsem(e)
        self.waited = {e: {} for e in self.E}
        self.lastw = {}
        self.readers = {}
        self.dsem = {}
        self.dcnt = {}
        self.allsems = []
        self.freed = []
        self.psi = 0
        self.stopped = False
        _TRS.append(self)

    def _newsem(self, e):
        self.nsem += 1
        s = self.es.enter_context(self.nc.semaphore("se_%s_%d" % (e, self.nsem)))
        self.esem[e] = s
        self.ecnt[e] = 0

    def _need(self, eng, toks):
        for (sem, val, owner) in toks:
            if eng == 'pe' and owner == 'pe':
                continue
            w = self.waited[eng]
            k = id(sem)
            if w.get(k, 0) >= val:
                continue
            self.E[eng].wait_ge(sem, val)
            w[k] = val

    def deps(self, eng, reads, writes):
        toks = []
        for k in reads:
            if k in self.lastw:
                toks.append(self.lastw[k])
        for k in writes:
            if k in self.lastw:
                toks.append(self.lastw[k])
            toks += list(self.readers.get(k, {}).values())
        self._need(eng, toks)

    def commit(self, tok, reads, writes):
        for k in reads:
            self.readers.setdefault(k, {})[id(tok[0])] = tok
        for k in writes:
            self.lastw[k] = tok
            self.readers[k] = {}

    def op(self, eng, fn, reads=(), writes=()):
        if self.stopped:
            return None
        self.deps(eng, reads, writes)
        ins = fn(self.E[eng])
        if self.ecnt[eng] >= 30000:
            self._newsem(eng)
        self.ecnt[eng] += 1
        ins.then_inc(self.esem[eng], 1)
        tok = (self.esem[eng], self.ecnt[eng], eng)
        self.commit(tok, reads, writes)
        return tok

    def _dsem(self, key):
        if key not in self.dsem:
            if self.freed:
                sem, cnt = self.freed.pop()
                self.dsem[key] = sem
                self.dcnt[key] = cnt
            else:
                self.nsem += 1
                self.dsem[key] = self.es.enter_context(self.nc.semaphore("sd_%d" % self.nsem))
                self.dcnt[key] = 0
        return self.dsem[key]

    def dma(self, q, out, in_, reads=(), writes=(), semkey=None, inc=16, fn=None, **kw):
        if self.stopped:
            return None
        self.deps(q, reads, writes)
        sem = self._dsem(semkey)
        if fn is None:
            ins = self.E[q].dma_start(out=out, in_=in_, **kw)
        else:
            ins = fn(self.E[q])
        self.dcnt[semkey] += inc
        ins.then_inc(sem, inc)
        tok = (sem, self.dcnt[semkey], 'dma')
        self.commit(tok, reads, writes)
        return tok

    def barrier(self):
        if self.stopped:
            return
        toks = [(self.esem[e], self.ecnt[e], e) for e in self.E if self.ecnt[e] > 0]
        toks += [(self.dsem[k], self.dcnt[k], 'dma') for k in self.dsem if self.dcnt[k] > 0]
        for e in self.E:
            self._need(e, [t for t in toks if not (t[2] == e and e != 'pe')] if e != 'pe' else [t for t in toks if t[2] != 'pe'])
        for k in list(self.dsem):
            self.freed.append((self.dsem[k], self.dcnt[k]))
            del self.dsem[k]
            del self.dcnt[k]

    def finish(self):
        if self.stopped:
            return
        toks = [(self.esem[e], self.ecnt[e], e) for e in self.E if self.ecnt[e] > 0 and e != 'sp']
        toks += [(self.dsem[k], self.dcnt[k], 'dma') for k in self.dsem if self.dcnt[k] > 0]
        toks += [(sm, c, 'dma') for (sm, c) in self.freed if c > 0]
        self._need('sp', toks)


def build(ncores=8, NPOOL=NPOOL):
    nc = bass.Bass("TRN2", target_bir_lowering=False, num_devices=ncores)
    dt = nc.dram_tensor

    def din(name, shape, dtype=F32):
        return dt(name, list(shape), dtype, kind="ExternalInput").ap()

    def dout(name, shape, dtype=F32):
        return dt(name, list(shape), dtype, kind="ExternalOutput").ap()

    def dint(name, shape, dtype=F32):
        return dt(name, list(shape), dtype, kind="Internal").ap()

    x_in = din("x_in", [NT, D])
    ck = din("cache_k", [NPOOL * 64, 2 * D])
    cv = din("cache_v", [NPOOL * 64, 2 * D])
    clf = din("cache_lf", [NPOOL * 2, 64 * H])
    sgla = din("state_gla", [NS, GH, GDK, GDV])
    ptab = din("page_table", [NS, 64], I32)
    ln_g = din("ln_g", [2, 3, D])
    ln_b = din("ln_b", [2, 3, D])
    ffn_w_in = din("ffn_w_in", [2, 2, D, 2 * DFF])
    ffn_w_out = din("ffn_w_out", [2, 2, DFF, D])
    fox_w_in = din("fox_w_in", [D, 3 * D + H])
    fox_b_f = din("fox_b_f", [1, H])
    fox_w_o = din("fox_w_o", [D, D])
    gla_w_in = din("gla_w_in", [D, 3088])
    gla_w_a2 = din("gla_w_a2", [16, 512])
    gla_b_a = din("gla_b_a", [1, 512])
    gla_norm_g = din("gla_norm_g", [1, D])
    gla_w_o = din("gla_w_o", [D, D])
    c_ident = din("c_ident", [128, 128])
    c_tri = din("c_tri", [128, 128])
    c_slow = din("c_slow", [128, 128])
    c_rank = din("c_rank", [128, 80])
    o_y = dout("o_y", [NT, D])
    o_k = dout("o_k", [NT, D])
    o_v = dout("o_v", [NT, D])
    o_lf = dout("o_lf", [NT, H])
    o_sp = dout("o_sp", [GH * GDK, GDV])
    o_ss = dout("o_ss", [NS * GH * GDK, GDV])
    xres = dint("xres", [NT, D])
    qT_d = dint("qT_d", [H, HD, NP_], BF)
    ktloc_c = [dint("ktloc%d" % i, [140, NP_], BF) for i in range(8)]
    ktg_c = [dint("ktg%d" % i, [560, NP_], BF) for i in range(8)]
    vloc_c = [dint("vloc%d" % i, [256, 1040], BF) for i in range(8)]
    vg_c = [dint("vg%d" % i, [1024, 1040], BF) for i in range(8)]
    totloc_f = dint("totloc", [1, 128])
    totloc = totloc_f[:, 0:H]
    totg_f = dint("totg", [4, 128])
    totg = totg_f[:, 0:H]
    qaug_d = dint("qaug_d", [5, H, 6, NP_], BF)
    srow_d = dint("srow_d", [NS, 3 * D + 2 * H])
    gq_d = dint("gq_d", [NT, 512])
    sloc = dint("sloc", [GH * GDK, GDV + 64])
    sg_ = dint("sgath", [4 * GH * GDK, GDV + 64])

    _uc = [0]

    def un(name):
        _uc[0] += 1
        return "%s_%d" % (name, _uc[0])

    es = contextlib.ExitStack()
    with es:
        T = TR(nc, es)
        sb = lambda name, shape, dtype=F32: es.enter_context(nc.sbuf_tensor(un(name), list(shape), dtype))
        PS = [es.enter_context(nc.psum_tensor("psb%d" % i, [128, 512], F32)) for i in range(8)]

        PSN = [6]

        def psum():
            T.psi = (T.psi + 1) % PSN[0]
            i = T.psi
            return i, PS[i], 'ps%d' % i

        def psum_r(i):
            return i, PS[i], 'ps%d' % i

        identf = sb("identf", [128, 128]); identb = sb("identb", [128, 128], BF)
        trib = sb("trib", [128, 128], BF); slowf = sb("slowf", [128, 128])
        onesf = sb("onesf", [128, 512]); onesb = sb("onesb", [128, 128], BF)
        crank = sb("crank", [128, 80])
        trif = sb("trif", [128, 128])
        T.dma('sp', identf[:], c_ident, writes=['identf'], semkey='c0')
        T.dma('sp', trif[:], c_tri, writes=['trif'], semkey='c1')
        T.dma('sp', slowf[:], c_slow, writes=['slowf'], semkey='c2')
        T.dma('sp', crank[:], c_rank, writes=['crank'], semkey='c3')
        T.op('dve', lambda e: e.tensor_copy(out=identb[:], in_=identf[:]), reads=['identf'], writes=['identb'])
        T.op('dve', lambda e: e.tensor_copy(out=trib[:], in_=trif[:]), reads=['trif'], writes=['trib'])
        T.op('dve', lambda e: e.memset(onesf[:], 1.0), writes=['onesf'])
        mbb = sb("mbb", [128, 128], BF)
        T.op('dve', lambda e: e.tensor_scalar(out=mbb[:], in0=slowf[:], scalar1=PEN, scalar2=None, op0=ALU.mult), reads=['slowf'], writes=['mbb'])
        T.op('dve', lambda e: e.memset(onesb[:], 1.0), writes=['onesb'])

        class _W:
            pass
        WKH = _W()

        def alloc_work(stack):
            a = lambda name, shape, dtype=F32: stack.enter_context(nc.sbuf_tensor(un(name), list(shape), dtype))
            WKH.xT = a("xT", [128, 8, 1028], BF)
            WKH.ld_f = [a("ld_f%d" % i, [128, D]) for i in range(2)]
            WKH.ld_b = [a("ld_b%d" % i, [128, D], BF) for i in range(2)]
            WKH.Gt = a("Gt", [128, D]); WKH.Bt = a("Bt", [128, D])
            WKH.zt = a("zt", [128, D]); WKH.xo = [a("xo%d" % i, [128, D]) for i in range(2)]
            WKH.junk = a("junk", [128, D])
            WKH.st_ = a("stats", [128, 16])
            T.barrier()
        cnt = dict(ld=0, xo=0, misc=0)

        def load_xT(pas, src):
            xT, ld_f, ld_b, Gt, Bt, zt, xo, junk, st_ = WKH.xT, WKH.ld_f, WKH.ld_b, WKH.Gt, WKH.Bt, WKH.zt, WKH.xo, WKH.junk, WKH.st_
            c0 = pas['c0']
            for (col, n) in pas['subs']:
                s = cnt['ld'] % 2
                cnt['ld'] += 1
                T.dma('sp', ld_f[s][:n, :], src[col:col + n, :], reads=['d:xres:%d' % col], writes=['ld_f%d' % s],
                      semkey='ld_f%d' % s)
                T.op('act', lambda e: e.copy(out=ld_b[s][:n, :], in_=ld_f[s][:n, :]), reads=['ld_f%d' % s],
                     writes=['ld_b%d' % s])
                pi, pt, pk = psum()
                ptb = pt[:].bitcast(BF)
                for dc in range(8):
                    T.op('pe', lambda e: e.transpose(out=ptb[:, dc * 128:dc * 128 + n], in_=ld_b[s][:n, dc * 128:(dc + 1) * 128],
                                                     identity=identb[:n, :n]),
                         reads=['ld_b%d' % s, 'identb'], writes=[pk])
                src_v = ptb[:, 0:1024].rearrange("p (c t) -> p c t", c=8)[:, :, 0:n]
                T.op('dve', lambda e: e.tensor_copy(out=xT[:, :, col - c0:col - c0 + n], in_=src_v), reads=[pk],
                     writes=['xT'])

        def ln_stage(col, n, ybanks, res_src, gi, gk, dst, gb_loaded):
            xT, ld_f, ld_b, Gt, Bt, zt, xo, junk, st_ = WKH.xT, WKH.ld_f, WKH.ld_b, WKH.Gt, WKH.Bt, WKH.zt, WKH.xo, WKH.junk, WKH.st_
            if not gb_loaded:
                T.dma('sp', Gt[:], ln_g[gi, gk:gk + 1, :].partition_broadcast(128), writes=['Gt'], semkey='Gt')
                T.dma('sp', Bt[:], ln_b[gi, gk:gk + 1, :].partition_broadcast(128), writes=['Bt'], semkey='Bt')
            s = cnt['ld'] % 2
            cnt['ld'] += 1
            xr = ld_f[s]
            T.dma('sp', xr[:n, :], res_src[col:col + n, :], reads=['d:xres:%d' % col], writes=['ld_f%d' % s],
                  semkey='ld_f%d' % s)
            for hf in range(2):
                (yp, yk) = ybanks[hf]
                T.op('dve', lambda e: e.scalar_tensor_tensor(out=zt[:n, hf * 512:(hf + 1) * 512], in0=xr[:n, hf * 512:(hf + 1) * 512],
                                                             scalar=ALPHA, in1=yp[:n, :], op0=ALU.mult, op1=ALU.add,
                                                             accum_out=st_[:n, hf:hf + 1]),
                     reads=['ld_f%d' % s, yk], writes=['zt', 'st'])
            T.op('act', lambda e: e.activation(out=junk[:n, :], in_=zt[:n, :], func=AF.Square, accum_out=st_[:n, 2:3]),
                 reads=['zt'], writes=['junk', 'st2'])
            T.op('dve', lambda e: e.tensor_tensor(out=st_[:n, 3:4], in0=st_[:n, 0:1], in1=st_[:n, 1:2], op=ALU.add),
                 reads=['st'], writes=['st'])
            T.op('dve', lambda e: e.tensor_scalar(out=st_[:n, 4:5], in0=st_[:n, 3:4], scalar1=1.0 / D, scalar2=None, op0=ALU.mult),
                 reads=['st'], writes=['st'])
            T.op('dve', lambda e: e.tensor_tensor(out=st_[:n, 5:6], in0=st_[:n, 4:5], in1=st_[:n, 4:5], op=ALU.mult),
                 reads=['st'], writes=['st'])
            T.op('dve', lambda e: e.scalar_tensor_tensor(out=st_[:n, 6:7], in0=st_[:n, 2:3], scalar=1.0 / D, in1=st_[:n, 5:6],
                                                         op0=ALU.mult, op1=ALU.subtract),
                 reads=['st', 'st2'], writes=['st'])
            T.op('dve', lambda e: e.tensor_scalar(out=st_[:n, 6:7], in0=st_[:n, 6:7], scalar1=EPS, scalar2=None, op0=ALU.add),
                 reads=['st'], writes=['st'])
            T.op('act', lambda e: e.activation(out=st_[:n, 7:8], in_=st_[:n, 6:7], func=AF.Sqrt), reads=['st'], writes=['st3'])
            T.op('dve', lambda e: e.reciprocal(out=st_[:n, 8:9], in_=st_[:n, 7:8]), reads=['st3'], writes=['st'])
            T.op('dve', lambda e: e.tensor_scalar(out=zt[:n, :], in0=zt[:n, :], scalar1=st_[:n, 4:5], scalar2=st_[:n, 8:9],
                                                  op0=ALU.subtract, op1=ALU.mult), reads=['zt', 'st'], writes=['zt'])
            T.op('dve', lambda e: e.tensor_tensor(out=zt[:n, :], in0=zt[:n, :], in1=Gt[:n, :], op=ALU.mult),
                 reads=['zt', 'Gt'], writes=['zt'])
            o = cnt['xo'] % 2
            cnt['xo'] += 1
            T.op('dve', lambda e: e.tensor_tensor(out=xo[o][:n, :], in0=zt[:n, :], in1=Bt[:n, :], op=ALU.add),
                 reads=['zt', 'Bt'], writes=['xo%d' % o])
            T.dma('sp', dst[col:col + n, :], xo[o][:n, :], reads=['xo%d' % o], writes=['d:xres:%d' % col], semkey='xo%d_st' % o)

        def ffn_stage(li, hi, gk, src, dst):
            xT, ld_f, ld_b, Gt, Bt, zt, xo, junk, st_ = WKH.xT, WKH.ld_f, WKH.ld_b, WKH.Gt, WKH.Bt, WKH.zt, WKH.xo, WKH.junk, WKH.st_
            with contextlib.ExitStack() as ls:
                lsb = lambda name, shape, dtype=F32: ls.enter_context(nc.sbuf_tensor(un(name), list(shape), dtype))
                aT = lsb("aT", [128, 22, 1028], BF)
                WOUT = lsb("WOUT", [128, 22, D], BF)
                WIN = [lsb("WIN%d" % i, [128, 8, 2, 256], BF) for i in range(2)]
                sg = [lsb("sg%d" % i, [128, 512]) for i in range(2)]
                w_in = ffn_w_in[li, hi].rearrange("(c p) f -> p c f", p=128)
                w_out = ffn_w_out[li, hi].rearrange("(c p) d -> p c d", p=128)
                first = True
                for pas in PASSES:
                    c0 = pas['c0']
                    load_xT(pas, src)

                    def load_w(jc):
                        s = jc % 2
                        for gu in range(2):
                            T.dma('pool', WIN[s][:, :, gu, :], w_in[:, :, gu * DFF + jc * 256: gu * DFF + (jc + 1) * 256],
                                  writes=['WIN%d_%d' % (s, gu)], semkey='WIN%d_%d' % (s, gu))
                        T.dma('pool', WOUT[:, 2 * jc:2 * jc + 2, :], w_out[:, 2 * jc:2 * jc + 2, :], writes=['WOUT%d' % jc],
                              semkey='WOUT%d' % jc)

                    load_w(0)
                    k = 0
                    for jc in range(11):
                        if jc + 1 < 11:
                            load_w(jc + 1)
                        s = jc % 2
                        for sub in range(2):
                            f = 2 * jc + sub
                            for (gc, gn) in pas['groups']:
                                lc = gc - c0
                                _, pg, pgk = psum()
                                _, pu, puk = psum()
                                for dc in range(8):
                                    T.op('pe', lambda e: e.matmul(pg[:, :gn], lhsT=WIN[s][:, dc, 0, sub * 128:(sub + 1) * 128],
                                                                  rhs=xT[:, dc, lc:lc + gn], start=(dc == 0), stop=(dc == 7)),
                                         reads=['WIN%d_0' % s, 'xT'], writes=[pgk])
                                for dc in range(8):
                                    T.op('pe', lambda e: e.matmul(pu[:, :gn], lhsT=WIN[s][:, dc, 1, sub * 128:(sub + 1) * 128],
                                                                  rhs=xT[:, dc, lc:lc + gn], start=(dc == 0), stop=(dc == 7)),
                                         reads=['WIN%d_1' % s, 'xT'], writes=[puk])
                                q = k % 2
                                k += 1
                                T.op('act', lambda e: e.activation(out=sg[q][:, :gn], in_=pg[:, :gn], func=AF.Silu),
                                     reads=[pgk], writes=['sg%d' % q])
                                T.op('dve', lambda e: e.scalar_tensor_tensor(out=aT[:, f, lc:lc + gn], in0=pu[:, :gn], scalar=0.5,
                                                                             in1=sg[q][:, :gn], op0=ALU.mult, op1=ALU.mult),
                                     reads=[puk, 'sg%d' % q], writes=['aT'])
                    for (col, n) in pas['subs']:
                        lc = col - c0
                        _, y0, y0k = psum()
                        _, y1, y1k = psum()
                        for f in range(22):
                            T.op('pe', lambda e: e.matmul(y0[:n, :], lhsT=aT[:, f, lc:lc + n], rhs=WOUT[:, f, 0:512],
                                                          start=(f == 0), stop=(f == 21)),
                                 reads=['aT', 'WOUT%d' % (f // 2)], writes=[y0k])
                            T.op('pe', lambda e: e.matmul(y1[:n, :], lhsT=aT[:, f, lc:lc + n], rhs=WOUT[:, f, 512:1024],
                                                          start=(f == 0), stop=(f == 21)),
                                 reads=['aT', 'WOUT%d' % (f // 2)], writes=[y1k])
                        ln_stage(col, n, [(y0, y0k), (y1, y1k)], src, li, gk, dst, not first)
                        first = False
                T.barrier()

        def wo_stage(OT, w_o_ap, li, gk):
            with contextlib.ExitStack() as ls:
                KP, NCH = OT.shape[0], OT.shape[1]
                WO = ls.enter_context(nc.sbuf_tensor(un("WO"), [KP, NCH, D], BF))
                hh = NCH // 2
                T.dma('pool', WO[:, 0:hh, :], w_o_ap.rearrange("(h p) d -> p h d", p=KP)[:, 0:hh, :], writes=['WO0'], semkey='WO0')
                T.dma('pool', WO[:, hh:NCH, :], w_o_ap.rearrange("(h p) d -> p h d", p=KP)[:, hh:NCH, :], writes=['WO1'], semkey='WO1')
                first = True
                for pas in PASSES:
                    for (col, n) in pas['subs']:
                        _, y0, y0k = psum()
                        _, y1, y1k = psum()
                        for h in range(NCH):
                            T.op('pe', lambda e: e.matmul(y0[:n, :], lhsT=OT[:, h, col:col + n], rhs=WO[:, h, 0:512],
                                                          start=(h == 0), stop=(h == NCH - 1)),
                                 reads=['OT', 'OTs', 'WO%d' % (h // hh)], writes=[y0k])
                            T.op('pe', lambda e: e.matmul(y1[:n, :], lhsT=OT[:, h, col:col + n], rhs=WO[:, h, 512:1024],
                                                          start=(h == 0), stop=(h == NCH - 1)),
                                 reads=['OT', 'OTs', 'WO%d' % (h // hh)], writes=[y1k])
                        ln_stage(col, n, [(y0, y0k), (y1, y1k)], xres, li, gk, xres, not first)
                        first = False
                T.barrier()

        def fox_proj(cT):
            xT, ld_f, ld_b, Gt, Bt, zt, xo, junk, st_ = WKH.xT, WKH.ld_f, WKH.ld_b, WKH.Gt, WKH.Bt, WKH.zt, WKH.xo, WKH.junk, WKH.st_
            with contextlib.ExitStack() as ls:
                lsb = lambda name, shape, dtype=F32: ls.enter_context(nc.sbuf_tensor(un(name), list(shape), dtype))
                WQ = lsb("WQ", [128, 8, D], BF); WK = lsb("WK", [128, 8, D], BF); WV = lsb("WV", [128, 8, D], BF)
                WF = lsb("WF", [128, 8, H], BF)
                QS = lsb("QS", [64, 16, 512], BF); KS = lsb("KS", [64, 16, 512], BF)
                ktm = [lsb("ktm%d" % i, [128, D]) for i in range(2)]
                vtm = [lsb("vtm%d" % i, [128, D]) for i in range(2)]
                qtm = lsb("qtm", [128, D])
                vau = [lsb("vau%d" % i, [128, 16, 65], BF) for i in range(2)]
                bfB = lsb("bfB", [128, H]); nbf = lsb("nbf", [16, 1]); bfc = lsb("bfc", [16, 1])
                lft = [lsb("lft%d" % i, [128, H]) for i in range(2)]
                lfT = lsb("lfT", [16, 512])
                wv = fox_w_in.rearrange("(c p) f -> p c f", p=128)
                for i, (W, nm) in enumerate([(WQ, 'WQ'), (WK, 'WK'), (WV, 'WV')]):
                    for hf in range(2):
                        T.dma('pool', W[:, :, hf * 512:(hf + 1) * 512], wv[:, :, i * D + hf * 512: i * D + (hf + 1) * 512],
                              writes=['%s%d' % (nm, hf)], semkey='%s%d' % (nm, hf))
                T.dma('pool', WF[:], wv[:, :, 3 * D:3 * D + H], writes=['WF'], semkey='WF')
                T.dma('sp', bfB[:], fox_b_f[0:1, :].partition_broadcast(128), writes=['bfB'], semkey='bfB')
                T.dma('sp', bfc[:], fox_b_f.rearrange("o h -> h o"), writes=['bfc'], semkey='bfc')
                T.op('dve', lambda e: e.tensor_scalar(out=nbf[:], in0=bfc[:], scalar1=-1.0, scalar2=None, op0=ALU.mult),
                     reads=['bfc'], writes=['nbf'])
                for i in range(2):
                    T.op('dve', lambda e: e.memset(vau[i][:], 1.0), writes=['vau%d' % i])
                k = 0
                for pas in PASSES:
                    c0 = pas['c0']
                    load_xT(pas, xres)
                    for (gc, gn) in pas['groups']:
                        if gn != 512:
                            continue
                        lc = gc - c0
                        for (W, nm, ST, scale) in [(WQ, 'WQ', QS, 0.125), (WK, 'WK', KS, 1.0)]:
                            for h in range(16):
                                _, pp, ppk = psum()
                                for dc in range(8):
                                    T.op('pe', lambda e: e.matmul(pp[:64, :], lhsT=W[:, dc, h * 64:(h + 1) * 64], rhs=xT[:, dc, lc:lc + 512],
                                                                  start=(dc == 0), stop=(dc == 7)),
                                         reads=['%s%d' % (nm, h // 8), 'xT'], writes=[ppk])
                                T.op('act', lambda e: e.activation(out=ST[:, h, :], in_=pp[:64, :], func=AF.Copy, scale=scale),
                                     reads=[ppk], writes=[nm + 'S'])
                        T.dma('sp', qT_d.rearrange("h d t -> d h t")[:, :, gc:gc + 512], QS[:], reads=['WQS'], writes=['d:qT'], semkey='QS_st')
                        for i8 in range(8):
                            T.dma('sp', ktloc_c[i8].rearrange("(h r) t -> r h t", r=70)[0:64, :, gc:gc + 512], KS[:, 2 * i8:2 * i8 + 2, :], reads=['WKS'],
                                  writes=['d:ktloc'], semkey='KS_st')
                        _, pf, pfk = psum()
                        for dc in range(8):
                            T.op('pe', lambda e: e.matmul(pf[:16, :], lhsT=WF[:, dc, :], rhs=xT[:, dc, lc:lc + 512], start=(dc == 0), stop=(dc == 7)),
                                 reads=['WF', 'xT'], writes=[pfk])
                        T.op('act', lambda e: e.activation(out=lfT[:], in_=pf[:16, :], func=AF.Exp, bias=nbf[:], scale=-1.0),
                             reads=[pfk, 'nbf'], writes=['lfT'])
                        T.op('act', lambda e: e.activation(out=lfT[:], in_=lfT[:], func=AF.Ln, bias=1.0, scale=1.0), reads=['lfT'], writes=['lfT'])
                        T.op('dve', lambda e: e.tensor_scalar(out=lfT[:], in0=lfT[:], scalar1=-1.0, scalar2=None, op0=ALU.mult),
                             reads=['lfT'], writes=['lfT'])
                        init = 0.0 if gc == 0 else cT[:, gc - 1:gc]
                        T.op('dve', lambda e: e.tensor_tensor_scan(out=cT[:, gc:gc + 512], data0=onesf[:16, :], data1=lfT[:], initial=init,
                                                                   op0=ALU.mult, op1=ALU.add), reads=['lfT', 'onesf', 'cT'], writes=['cT'])
                    for (col, n) in pas['subs']:
                        lc = col - c0
                        s = k % 2
                        k += 1
                        for (W, nm, tm) in [(WK, 'WK', ktm[s]), (WV, 'WV', vtm[s])] + ([(WQ, 'WQ', qtm)] if n == 4 else []):
                            key = tm.name if hasattr(tm, 'name') else nm
                            for hf in range(2):
                                _, pp, ppk = psum()
                                for dc in range(8):
                                    T.op('pe', lambda e: e.matmul(pp[:n, :], lhsT=xT[:, dc, lc:lc + n], rhs=W[:, dc, hf * 512:(hf + 1) * 512],
                                                                  start=(dc == 0), stop=(dc == 7)),
                                         reads=['%s%d' % (nm, hf), 'xT'], writes=[ppk])
                                T.op('act', lambda e: e.copy(out=tm[:n, hf * 512:(hf + 1) * 512], in_=pp[:n, :]), reads=[ppk],
                                     writes=['tm_%s_%d' % (nm, s)])
                        T.dma('sp', o_k[col:col + n, :], ktm[s][:n, :], reads=['tm_WK_%d' % s], semkey='ktm%d_st' % s)
                        T.dma('sp', o_v[col:col + n, :], vtm[s][:n, :], reads=['tm_WV_%d' % s], semkey='vtm%d_st' % s)
                        if n == 128:
                            T.op('dve', lambda e: e.tensor_copy(out=vau[s][:, :, 0:64], in_=vtm[s][:, :].rearrange("p (h d) -> p h d", h=16)),
                                 reads=['tm_WV_%d' % s], writes=['vau%d' % s])
                            for i8 in range(8):
                                T.dma('sp', vloc_c[i8].rearrange("a (b c) -> (a b) c", c=65).rearrange("(h t) c -> t h c", h=2)[col:col + 128, :, :],
                                      vau[s][:, 2 * i8:2 * i8 + 2, :], reads=['vau%d' % s], writes=['d:vloc'], semkey='vau%d_st' % s)
                        else:
                            T.dma('sp', srow_d[:, 0:D], qtm[:n, :], reads=['tm_WQ_%d' % s], writes=['d:srow'], semkey='qtm_st')
                            T.dma('sp', srow_d[:, D:2 * D], ktm[s][:n, :], reads=['tm_WK_%d' % s], writes=['d:srow'], semkey='ktm%d_st2' % s)
                            T.dma('sp', srow_d[:, 2 * D:3 * D], vtm[s][:n, :], reads=['tm_WV_%d' % s], writes=['d:srow'], semkey='vtm%d_st2' % s)
                        _, pf, pfk = psum()
                        for dc in range(8):
                            T.op('pe', lambda e: e.matmul(pf[:n, :16], lhsT=xT[:, dc, lc:lc + n], rhs=WF[:, dc, :], start=(dc == 0), stop=(dc == 7)),
                                 reads=['WF', 'xT'], writes=[pfk])
                        lf = lft[s]
                        T.op('dve', lambda e: e.tensor_tensor(out=lf[:n, :], in0=pf[:n, :16], in1=bfB[:n, :], op=ALU.add),
                             reads=[pfk, 'bfB'], writes=['lft%d' % s])
                        T.op('act', lambda e: e.activation(out=lf[:n, :], in_=lf[:n, :], func=AF.Exp, scale=-1.0), reads=['lft%d' % s], writes=['lft%d' % s])
                        T.op('act', lambda e: e.activation(out=lf[:n, :], in_=lf[:n, :], func=AF.Ln, bias=1.0, scale=1.0), reads=['lft%d' % s], writes=['lft%d' % s])
                        T.op('dve', lambda e: e.tensor_scalar(out=lf[:n, :], in0=lf[:n, :], scalar1=-1.0, scalar2=None, op0=ALU.mult),
                             reads=['lft%d' % s], writes=['lft%d' % s])
                        T.dma('sp', o_lf[col:col + n, :], lf[:n, :], reads=['lft%d' % s], semkey='lft%d_st' % s)
                        if n == 4:
                            T.dma('sp', srow_d[:, 3 * D:3 * D + H], lf[:n, :], reads=['lft%d' % s], writes=['d:srow'], semkey='lft%d_st2' % s)
                T.barrier()


        GROUPS4 = [[0, 1, 2, 3], [4, 5, 6, 7]]

        def allgather(src, dst, rk, wk, name):
            T.dma('pool', None, None, reads=[rk], writes=[wk], semkey=name, inc=1,
                  fn=lambda e: e.collective_compute("AllGather", ALU.bypass, replica_groups=GROUPS4, ins=[src], outs=[dst]))

        def fox_attn(cT, OT):
            with contextlib.ExitStack() as ls:
                lsb = lambda name, shape, dtype=F32: ls.enter_context(nc.sbuf_tensor(un(name), list(shape), dtype))
                pre = contextlib.ExitStack()
                psb = lambda name, shape, dtype=F32: pre.enter_context(nc.sbuf_tensor(un(name), list(shape), dtype))
                wk_ = psb("wk_", [16, NP_]); r1 = psb("r1", [16, NP_])
                hi = psb("hi", [16, NP_], BF); mid = psb("mid", [16, NP_], BF); lo = psb("lo", [16, NP_], BF)
                ones3 = psb("ones3", [16, 3, NP_], BF)
                TG = psb("TG", [16, 4]); Dp = psb("Dp", [16, 4]); tmp4 = psb("tmp4", [16, 4])
                T.op('dve', lambda e: e.memset(ones3[:], 1.0), writes=['ones3'])
                fox_sample_setup(psb)
                PSN[0] = 4

                def split3(srckey):
                    T.op('dve', lambda e: e.tensor_copy(out=hi[:], in_=wk_[:]), reads=[srckey], writes=['hi'])
                    T.op('dve', lambda e: e.tensor_tensor(out=r1[:], in0=wk_[:], in1=hi[:], op=ALU.subtract), reads=[srckey, 'hi'], writes=['r1'])
                    T.op('dve', lambda e: e.tensor_copy(out=mid[:], in_=r1[:]), reads=['r1'], writes=['mid'])
                    T.op('dve', lambda e: e.tensor_tensor(out=r1[:], in0=r1[:], in1=mid[:], op=ALU.subtract), reads=['r1', 'mid'], writes=['r1'])
                    T.op('dve', lambda e: e.tensor_copy(out=lo[:], in_=r1[:]), reads=['r1'], writes=['lo'])

                T.op('dve', lambda e: e.tensor_scalar(out=wk_[:], in0=cT[:], scalar1=-1.0, scalar2=None, op0=ALU.mult), reads=['cT'], writes=['wk_'])
                split3('wk_')
                for i8 in range(8):
                    ktv = ktloc_c[i8].rearrange("(h r) t -> h r t", r=70)
                    hs = slice(2 * i8, 2 * i8 + 2)
                    T.dma('sp', ktv[:, 64:67, :], ones3[hs], reads=['ones3'], writes=['d:kt_a'], semkey='ones3_st')
                    T.dma('sp', ktv[:, 67, :], hi[hs], reads=['hi'], writes=['d:kt_b'], semkey='hi_st')
                    T.dma('sp', ktv[:, 68, :], mid[hs], reads=['mid'], writes=['d:kt_c'], semkey='mid_st')
                    T.dma('sp', ktv[:, 69, :], lo[hs], reads=['lo'], writes=['d:kt_d'], semkey='lo_st')
                with nc.allow_non_contiguous_dma(reason="tiny"):
                    T.dma('sp', totloc.rearrange("o h -> h o"), cT[:, NP_ - 1:NP_], reads=['cT'], writes=['d:tot'], semkey='tot_st')
                T.barrier()
                chk(2)
                for i8 in range(8):
                    allgather(ktloc_c[i8], ktg_c[i8], 'd:kt_a', 'd:ktg', 'cc_kt%d' % i8)
                    allgather(vloc_c[i8], vg_c[i8], 'd:kt_a', 'd:vg', 'cc_v%d' % i8)
                allgather(totloc_f, totg_f, 'd:tot', 'd:totg', 'cc_tot')
                with nc.allow_non_contiguous_dma(reason="tiny"):
                    T.dma('sp', TG[:], totg.rearrange("r h -> h r"), reads=['d:totg'], writes=['TG'], semkey='TG')
                for rp in range(4):
                    T.op('dve', lambda e: e.tensor_tensor(out=tmp4[:], in0=TG[:], in1=crank[:16, rp * 4:rp * 4 + 4], op=ALU.mult),
                         reads=['TG', 'crank'], writes=['tmp4'])
                    T.op('dve', lambda e: e.tensor_reduce(out=Dp[:, rp:rp + 1], in_=tmp4[:], axis=AX.X, op=ALU.add), reads=['tmp4'], writes=['Dp'])
                T.op('dve', lambda e: e.tensor_tensor(out=Dp[:], in0=Dp[:], in1=crank[:16, 16:20], op=ALU.add), reads=['Dp', 'crank'], writes=['Dp'])
                for v in range(5):
                    if v == 0:
                        T.op('dve', lambda e: e.tensor_copy(out=wk_[:], in_=cT[:]), reads=['cT'], writes=['wk_'])
                    else:
                        T.op('dve', lambda e: e.tensor_scalar(out=wk_[:], in0=cT[:], scalar1=Dp[:, v - 1:v], scalar2=None, op0=ALU.add),
                             reads=['cT', 'Dp'], writes=['wk_'])
                    split3('wk_')
                    T.dma('sp', qaug_d[v, :, 0, :], hi[:], reads=['hi'], writes=['d:qa%d' % v], semkey='hi_st')
                    T.dma('sp', qaug_d[v, :, 1, :], mid[:], reads=['mid'], writes=['d:qb%d' % v], semkey='mid_st')
                    T.dma('sp', qaug_d[v, :, 2, :], lo[:], reads=['lo'], writes=['d:qc%d' % v], semkey='lo_st')
                    T.dma('sp', qaug_d[v, :, 3:6, :], ones3[:], reads=['ones3'], writes=['d:qd%d' % v], semkey='ones3_st')
                T.barrier()
                chk(3)
                pre.close()
                QA = lsb("QA", [70, 5, NP_], BF); KL = lsb("KL", [70, NP_], BF); KG = lsb("KG", [70, 4, NP_], BF)
                VL = lsb("VL", [128, 16, 65], BF); VG = lsb("VG", [128, 4, 16, 65], BF)
                PT = [lsb("PT%d" % i, [128, 512], BF) for i in range(3)]
                rdt = lsb("rdt", [65, 512]); bcs = lsb("bcs", [64, 512])
                sgen = fox_sample_gen(OT, lsb)
                next(sgen, None)
                next(sgen, None)
                ktgv = [ktg_c[i8].rearrange("(g h r) t -> h r g t", g=4, h=2) for i8 in range(8)]
                vlv = [vloc_c[i8].rearrange("a (b c) -> (a b) c", c=65).rearrange("(h kb p) c -> h p kb c", h=2, p=128) for i8 in range(8)]
                vgv = [vg_c[i8].rearrange("a (b c) -> (a b) c", c=65).rearrange("(g h p j) c -> h p g j c", g=4, h=2, p=128) for i8 in range(8)]
                KGv = KG[:].rearrange("r g (p j) -> r g j p", j=16)
                pk_ = 0
                for h in range(16):
                    T.dma('sp', QA[0:64, :, :], bass.AP(qT_d.tensor, h * 64 * NP_, [[NP_, 64], [0, 5], [1, NP_]]), writes=['QAq'], semkey='QAq')
                    T.dma('sp', QA[64:70, :, :], qaug_d[:, h, :, :].rearrange("v r t -> r v t"), writes=['QAa'], semkey='QAa')
                    T.dma('sp', KL[:], ktloc_c[h // 2][(h % 2) * 70:(h % 2 + 1) * 70, :], writes=['KL'], semkey='KL')
                    T.dma('sp', KG[:], ktgv[h // 2][h % 2], writes=['KG'], semkey='KG')
                    T.dma('sp', VL[:], vlv[h // 2][h % 2], writes=['VL'], semkey='VL')
                    T.dma('sp', VG[:], vgv[h // 2][h % 2], writes=['VG'], semkey='VG')
                    items = []
                    for qt in range(4):
                        blocks = [('L', kb, 0) for kb in range(4 * qt + 4)] + [('G', j, g) for g in range(4) for j in range(16)]
                        for bi, (kind, kb, g) in enumerate(blocks):
                            items.append((qt, kind, kb, g, bi == 0, bi == len(blocks) - 1))
                    st = {}

                    def stage_a(i):
                        (qt, kind, kb, g, first, last) = items[i]
                        qc = qt * 512
                        _, S, Sk = psum()
                        p = i % 3
                        if kind == 'L':
                            off = max(0, kb * 128 - qc)
                            n = 512 - off
                            dg = kb >= 4 * qt
                            T.op('pe', lambda e: e.matmul(S[:, :n], lhsT=KL[:, kb * 128:(kb + 1) * 128], rhs=QA[:, 0, qc + off:qc + 512],
                                                          start=True, stop=not dg), reads=['KL', 'QAq', 'QAa'], writes=[Sk])
                            if dg:
                                T.op('pe', lambda e: e.matmul(S[:, 0:128], lhsT=identb[:, :], rhs=mbb[:, :], start=False, stop=True),
                                     reads=['identb', 'mbb'], writes=[Sk])
                        else:
                            off = 0
                            n = 512
                            T.op('pe', lambda e: e.matmul(S[:, :n], lhsT=KGv[:, g, kb, :], rhs=QA[:, 1 + g, qc:qc + 512],
                                                          start=True, stop=True), reads=['KG', 'QAq', 'QAa'], writes=[Sk])
                        T.op('act', lambda e: e.activation(out=PT[p][:, :n], in_=S[:, :n], func=AF.Exp), reads=[Sk], writes=['PT%d' % p])
                        st[i] = (p, off, n)

                    def stage_b(i):
                        (qt, kind, kb, g, first, last) = items[i]
                        qc = qt * 512
                        (p, off, n) = st.pop(i)
                        _, O, Ok = psum_r(6 + (qt % 2))
                        lhsV = VL[:, kb, :] if kind == 'L' else VG[:, g, kb, :]
                        vkey = 'VL' if kind == 'L' else 'VG'
                        T.op('pe', lambda e: e.matmul(O[0:65, off:512], lhsT=lhsV, rhs=PT[p][:, :n], start=first, stop=last),
                             reads=[vkey, 'PT%d' % p], writes=[Ok])
                        if last:
                            T.op('dve', lambda e: e.reciprocal(out=rdt[64:65, :], in_=O[64:65, :]), reads=[Ok], writes=['rdt'])
                            _, bc, bck = psum()
                            T.op('pe', lambda e: e.matmul(bc[0:64, :], lhsT=onesf[64:65, 0:64], rhs=rdt[64:65, :], start=True, stop=True),
                                 reads=['rdt', 'onesf'], writes=[bck])
                            T.op('act', lambda e: e.copy(out=bcs[:], in_=bc[0:64, :]), reads=[bck], writes=['bcs'])
                            T.op('dve', lambda e: e.tensor_tensor(out=OT[:, h, qc:qc + 512], in0=O[0:64, :], in1=bcs[:], op=ALU.mult),
                                 reads=[Ok, 'bcs'], writes=['OT'])

                    LAH = 2
                    for i in range(len(items) + LAH):
                        if i < len(items):
                            stage_a(i)
                        if i >= LAH:
                            stage_b(i - LAH)
                        if i % 36 == 35:
                            next(sgen, None)
                for _ in sgen:
                    pass
                T.barrier()
                PSN[0] = 6

        def fox_sample_setup(psb):
            q4 = psb("q4", [4, D]); k4 = psb("k4", [4, D]); p4 = psb("p4", [4, D]); sn = psb("sn", [4, H]); pn = psb("pn", [4, H])
            T.dma('sp', q4[:], srow_d[:, 0:D], reads=['d:srow'], writes=['q4'], semkey='q4')
            T.dma('sp', k4[:], srow_d[:, D:2 * D], reads=['d:srow'], writes=['k4'], semkey='k4')
            T.op('dve', lambda e: e.tensor_tensor(out=p4[:], in0=q4[:], in1=k4[:], op=ALU.mult), reads=['q4', 'k4'], writes=['p4'])
            T.op('dve', lambda e: e.tensor_reduce(out=sn[:], in_=p4[:].rearrange("p (h d) -> p h d", h=16), axis=AX.X, op=ALU.add),
                 reads=['p4'], writes=['sn'])
            T.op('act', lambda e: e.activation(out=pn[:], in_=sn[:], func=AF.Exp, scale=0.125), reads=['sn'], writes=['pn'])
            T.dma('sp', srow_d[:, 3 * D + H:3 * D + 2 * H], pn[:], reads=['pn'], writes=['d:srow2'], semkey='pn_st')

        def fox_sample_gen(OT, lsb):
            RL = 3 * D + 2 * H
            NVB = 8
            rows = lsb("rows", [1, NS, 2 * H]); rowsb = lsb("rowsb", [1, NS, D], BF); pnb = lsb("pnb", [1, NS, H], BF)
            idx = lsb("idx", [128, 1], I32); idxf = lsb("idxf", [128, 1]); idxall = lsb("idxall", [128, 32], I32); idx2 = lsb("idx2", [128, 1], I32)
            LF = lsb("LF", [128, 64, H]); Bi = lsb("Bi", [128, 64, H]); Sc = lsb("Sc", [128, 64, H]); Pm = lsb("Pm", [128, 64, H], BF)
            KB = [lsb("KB%d" % i, [128, 2, D], BF) for i in range(2)]; VB = [lsb("VB%d" % i, [128, 2, D], BF) for i in range(NVB)]
            prod = lsb("prod", [128, 2, D], BF); qB = lsb("qB", [128, 2, D], BF)
            Tt = lsb("Tt", [128, H]); At = lsb("At", [128, H]); dpart = lsb("dpart", [128, H]); rden = lsb("rden", [16, 1])
            On = lsb("On", [16, D], BF)
            T.dma('sp', rows[:], bass.AP(srow_d.tensor, 3 * D, [[0, 1], [RL, NS], [1, 2 * H]]), reads=['d:srow2', 'd:srow'], writes=['rows'], semkey='rows')
            T.dma('pool', rowsb[:], bass.AP(srow_d.tensor, 2 * D, [[0, 1], [RL, NS], [1, D]]), reads=['d:srow'], writes=['rowsb'], semkey='rowsb')
            T.op('dve', lambda e: e.tensor_copy(out=pnb[:], in_=rows[:, :, H:2 * H]), reads=['rows'], writes=['pnb'])

            def vgather(j):
                b = j % NVB
                T.dma('pool', None, None, reads=['idxall'], writes=['VB%d' % b], semkey='VB%d' % b,
                      fn=lambda e: e.indirect_dma_start(out=VB[b][:].rearrange("p t d -> p (t d)"), out_offset=None, in_=cv,
                                                        in_offset=bass.IndirectOffsetOnAxis(ap=idxall[:, j:j + 1], axis=0)))

            for s_ in range(NS):
                with nc.allow_non_contiguous_dma(reason="tiny"):
                    T.dma('sp', idx[:], bass.AP(ptab.tensor, s_ * 64, [[1, 64], [0, 2], [1, 1]]), writes=['idx'], semkey='idx')
                T.op('dve', lambda e: e.tensor_scalar(out=idx2[:], in0=idx[:], scalar1=2.0, scalar2=crank[:, 40:41], op0=ALU.mult, op1=ALU.add),
                     reads=['idx', 'crank'], writes=['idx2'])
                T.op('dve', lambda e: e.tensor_scalar(out=idxf[:], in0=idx2[:], scalar1=32.0, scalar2=None, op0=ALU.mult), reads=['idx2'], writes=['idxf'])
                T.op('dve', lambda e: e.tensor_scalar(out=idxall[:], in0=crank[:, 48:80], scalar1=idxf[:, 0:1], scalar2=None, op0=ALU.add),
                     reads=['idxf', 'crank'], writes=['idxall'])
                T.dma('pool', None, None, reads=['idx2'], writes=['LF'], semkey='LF',
                      fn=lambda e: e.indirect_dma_start(out=LF[:].rearrange("p t h -> p (t h)"), out_offset=None, in_=clf,
                                                        in_offset=bass.IndirectOffsetOnAxis(ap=idx2[:, 0:1], axis=0)))
                for t in range(2):
                    T.dma('pool', qB[:, t, :], bass.AP(srow_d.tensor, s_ * RL, [[0, 128], [1, D]]), reads=['d:srow'], writes=['qB%d' % t], semkey='qB%d' % t)
                for j in range(32):
                    b = j % 2
                    T.dma('pool', None, None, reads=['idxall'], writes=['KB%d' % b], semkey='KB%d' % b,
                          fn=lambda e: e.indirect_dma_start(out=KB[b][:].rearrange("p t d -> p (t d)"), out_offset=None, in_=ck,
                                                            in_offset=bass.IndirectOffsetOnAxis(ap=idxall[:, j:j + 1], axis=0)))
                    T.op('dve', lambda e: e.tensor_tensor(out=prod[:], in0=KB[b][:], in1=qB[:], op=ALU.mult),
                         reads=['KB%d' % b, 'qB0', 'qB1'], writes=['prod'])
                    T.op('dve', lambda e: e.tensor_reduce(out=Sc[:, 2 * j:2 * j + 2, :].rearrange("p t h -> p (t h)"),
                                                          in_=prod[:].rearrange("p t (h d) -> p (t h) d", h=16), axis=AX.X, op=ALU.add),
                         reads=['prod'], writes=['Sc'])
                    if j % 2 == 1:
                        yield
                T.op('dve', lambda e: e.tensor_reduce(out=Tt[:], in_=LF[:].rearrange("p t h -> p h t"), axis=AX.X, op=ALU.add), reads=['LF'], writes=['Tt'])
                yield
                for j in range(NVB):
                    vgather(j)
                yield
                yield
                _, lp, lpk = psum()
                T.op('pe', lambda e: e.matmul(lp[:, 0:H], lhsT=slowf[:], rhs=Tt[:], start=True, stop=False), reads=['slowf', 'Tt'], writes=[lpk])
                T.op('pe', lambda e: e.matmul(lp[:, 0:H], lhsT=onesf[0:1, 0:128], rhs=rows[0:1, s_, 0:H], start=False, stop=True),
                     reads=['onesf', 'rows'], writes=[lpk])
                T.op('dve', lambda e: e.tensor_tensor(out=At[:], in0=lp[:, 0:H], in1=Tt[:], op=ALU.add), reads=[lpk, 'Tt'], writes=['At'])
                for h in range(16):
                    T.op('dve', lambda e: e.tensor_tensor_scan(out=Bi[:, :, h], data0=onesf[:, 0:64], data1=LF[:, :, h], initial=At[:, h:h + 1],
                                                               op0=ALU.mult, op1=ALU.subtract), reads=['LF', 'At', 'onesf'], writes=['Bi'])
                T.op('dve', lambda e: e.scalar_tensor_tensor(out=Sc[:].rearrange("p t h -> p (t h)"), in0=Sc[:].rearrange("p t h -> p (t h)"),
                                                             scalar=0.125, in1=Bi[:].rearrange("p t h -> p (t h)"), op0=ALU.mult, op1=ALU.add),
                     reads=['Sc', 'Bi'], writes=['Sc'])
                T.op('act', lambda e: e.activation(out=Pm[:].rearrange("p t h -> p (t h)"), in_=Sc[:].rearrange("p t h -> p (t h)"), func=AF.Exp),
                     reads=['Sc'], writes=['Pm'])
                T.op('dve', lambda e: e.tensor_reduce(out=dpart[:], in_=Pm[:].rearrange("p t h -> p h t"), axis=AX.X, op=ALU.add), reads=['Pm'], writes=['dpart'])
                yield
                yield
                _, dn, dnk = psum()
                T.op('pe', lambda e: e.matmul(dn[0:16, 0:1], lhsT=dpart[:], rhs=onesf[:, 0:1], start=True, stop=False), reads=['dpart', 'onesf'], writes=[dnk])
                T.op('pe', lambda e: e.matmul(dn[0:16, 0:1], lhsT=rows[0:1, s_, H:2 * H], rhs=onesf[0:1, 0:1], start=False, stop=True),
                     reads=['rows', 'onesf'], writes=[dnk])
                T.op('dve', lambda e: e.reciprocal(out=rden[:], in_=dn[0:16, 0:1]), reads=[dnk], writes=['rden'])
                _, O0, O0k = psum_r(4)
                _, O1, O1k = psum_r(5)
                for q in range(4):
                    for j in range(q * NVB, (q + 1) * NVB):
                        b = j % NVB
                        for t in range(2):
                            pos = 2 * j + t
                            T.op('pe', lambda e: e.matmul(O0[0:16, :], lhsT=Pm[:, pos, :], rhs=VB[b][:, t, 0:512], start=(pos == 0), stop=False),
                                 reads=['Pm', 'VB%d' % b], writes=[O0k])
                            T.op('pe', lambda e: e.matmul(O1[0:16, :], lhsT=Pm[:, pos, :], rhs=VB[b][:, t, 512:1024], start=(pos == 0), stop=False),
                                 reads=['Pm', 'VB%d' % b], writes=[O1k])
                    if q < 3:
                        for j in range((q + 1) * NVB, (q + 2) * NVB):
                            vgather(j)
                        yield
                        yield
                T.op('pe', lambda e: e.matmul(O0[0:16, :], lhsT=pnb[0:1, s_, :], rhs=rowsb[0:1, s_, 0:512], start=False, stop=True),
                     reads=['pnb', 'rowsb'], writes=[O0k])
                T.op('pe', lambda e: e.matmul(O1[0:16, :], lhsT=pnb[0:1, s_, :], rhs=rowsb[0:1, s_, 512:1024], start=False, stop=True),
                     reads=['pnb', 'rowsb'], writes=[O1k])
                T.op('dve', lambda e: e.tensor_scalar(out=On[:, 0:512], in0=O0[0:16, :], scalar1=rden[:, 0:1], scalar2=None, op0=ALU.mult),
                     reads=[O0k, 'rden'], writes=['On'])
                T.op('dve', lambda e: e.tensor_scalar(out=On[:, 512:1024], in0=O1[0:16, :], scalar1=rden[:, 0:1], scalar2=None, op0=ALU.mult),
                     reads=[O1k, 'rden'], writes=['On'])
                _, Z, Zk = psum()
                for h in range(16):
                    T.op('pe', lambda e: e.matmul(Z[0:64, h:h + 1], lhsT=On[:, h * 64:(h + 1) * 64], rhs=identb[0:16, h:h + 1], start=True, stop=True),
                         reads=['On', 'identb'], writes=[Zk])
                T.op('dve', lambda e: e.tensor_copy(out=OT[:, :, NP_ + s_], in_=Z[0:64, 0:16]), reads=[Zk], writes=['OTs'])
                yield

        gqT = dint("gqT", [GH, 128, NT], BF)
        gkT = dint("gkT", [GH, 128, NT], BF)
        glaT = dint("glaT", [GH, 128, NT])
        grT = dint("grT", [8, 128, NT], BF)
        gv = dint("gv", [NP_, D], BF)
        gsv = dint("gsv", [NS, D + 512])

        def gla_proj():
            xT, ld_f, ld_b, Gt, Bt, zt, xo, junk, st_ = WKH.xT, WKH.ld_f, WKH.ld_b, WKH.Gt, WKH.Bt, WKH.zt, WKH.xo, WKH.junk, WKH.st_
            with contextlib.ExitStack() as ls:
                lsb = lambda name, shape, dtype=F32: ls.enter_context(nc.sbuf_tensor(un(name), list(shape), dtype))
                WGQ = lsb("WGQ", [128, 8, 512], BF); WGK = lsb("WGK", [128, 8, 512], BF)
                WGV = lsb("WGV", [128, 8, D], BF); WGR = lsb("WGR", [128, 8, D], BF); WGA = lsb("WGA", [128, 8, 16], BF)
                WA2 = lsb("WA2", [16, 512], BF); nba = lsb("nba", [128, 4]); bac = lsb("bac", [128, 4])
                SQ = lsb("SQ", [128, 4, 512], BF); SK = lsb("SK", [128, 4, 512], BF); SL = lsb("SL", [128, 4, 512]); SR = lsb("SR", [128, 8, 512], BF)
                alr = lsb("alr", [16, 512], BF)
                vt = [lsb("vt%d" % i, [128, D], BF) for i in range(2)]
                vs_ = lsb("vs_", [4, D + 512])
                wv = gla_w_in.rearrange("(c p) f -> p c f", p=128)
                T.dma('pool', WGQ[:], wv[:, :, 0:512], writes=['WGQ'], semkey='WGQ')
                T.dma('pool', WGK[:], wv[:, :, 512:1024], writes=['WGK'], semkey='WGK')
                for hf in range(2):
                    T.dma('pool', WGV[:, :, hf * 512:(hf + 1) * 512], wv[:, :, 1024 + hf * 512:1024 + (hf + 1) * 512], writes=['WGV%d' % hf], semkey='WGV%d' % hf)
                    T.dma('pool', WGR[:, :, hf * 512:(hf + 1) * 512], wv[:, :, 2048 + hf * 512:2048 + (hf + 1) * 512], writes=['WGR%d' % hf], semkey='WGR%d' % hf)
                T.dma('pool', WGA[:], wv[:, :, 3072:3088], writes=['WGA'], semkey='WGA')
                T.dma('pool', WA2[:], gla_w_a2, writes=['WA2'], semkey='WA2')
                with nc.allow_non_contiguous_dma(reason="tiny"):
                    T.dma('sp', bac[:], gla_b_a.rearrange("o (h p) -> p (o h)", p=128), writes=['bac'], semkey='bac')
                T.op('dve', lambda e: e.tensor_scalar(out=nba[:], in0=bac[:], scalar1=-1.0, scalar2=None, op0=ALU.mult), reads=['bac'], writes=['nba'])
                k = 0
                for pas in PASSES:
                    c0 = pas['c0']
                    load_xT(pas, xres)
                    for (gc, gn) in pas['groups']:
                        lc = gc - c0
                        for (W, nm, ST, scale) in [(WGQ, 'WGQ', SQ, 128.0 ** -0.5), (WGK, 'WGK', SK, 1.0)]:
                            for h in range(4):
                                _, pp, ppk = psum()
                                for dc in range(8):
                                    T.op('pe', lambda e: e.matmul(pp[:, :gn], lhsT=W[:, dc, h * 128:(h + 1) * 128], rhs=xT[:, dc, lc:lc + gn],
                                                                  start=(dc == 0), stop=(dc == 7)), reads=[nm, 'xT'], writes=[ppk])
                                T.op('act', lambda e: e.activation(out=ST[:, h, :gn], in_=pp[:, :gn], func=AF.Copy, scale=scale), reads=[ppk], writes=[nm + 'S'])
                        T.dma('sp', gqT.rearrange("h p t -> p h t")[:, :, gc:gc + gn], SQ[:, :, :gn], reads=['WGQS'], writes=['d:gq'], semkey='SQ_st')
                        T.dma('sp', gkT.rearrange("h p t -> p h t")[:, :, gc:gc + gn], SK[:, :, :gn], reads=['WGKS'], writes=['d:gk'], semkey='SK_st')
                        _, pa, pak = psum()
                        for dc in range(8):
                            T.op('pe', lambda e: e.matmul(pa[:16, :gn], lhsT=WGA[:, dc, :], rhs=xT[:, dc, lc:lc + gn], start=(dc == 0), stop=(dc == 7)),
                                 reads=['WGA', 'xT'], writes=[pak])
                        T.op('act', lambda e: e.copy(out=alr[:, :gn], in_=pa[:16, :gn]), reads=[pak], writes=['alr'])
                        for h in range(4):
                            _, pp, ppk = psum()
                            T.op('pe', lambda e: e.matmul(pp[:, :gn], lhsT=WA2[:, h * 128:(h + 1) * 128], rhs=alr[:, :gn], start=True, stop=True),
                                 reads=['WA2', 'alr'], writes=[ppk])
                            T.op('act', lambda e: e.activation(out=SL[:, h, :gn], in_=pp[:, :gn], func=AF.Exp, bias=nba[:, h:h + 1], scale=-1.0),
                                 reads=[ppk, 'nba'], writes=['SL'])
                        T.op('act', lambda e: e.activation(out=SL[:, :, :gn], in_=SL[:, :, :gn], func=AF.Ln, bias=1.0, scale=1.0), reads=['SL'], writes=['SL'])
                        T.op('dve', lambda e: e.tensor_scalar(out=SL[:, :, :gn], in0=SL[:, :, :gn], scalar1=-1.0 / 16.0, scalar2=None, op0=ALU.mult),
                             reads=['SL'], writes=['SL'])
                        T.dma('sp', glaT.rearrange("h p t -> p h t")[:, :, gc:gc + gn], SL[:, :, :gn], reads=['SL'], writes=['d:gla'], semkey='SL_st')
                        for c in range(8):
                            _, pp, ppk = psum()
                            for dc in range(8):
                                T.op('pe', lambda e: e.matmul(pp[:, :gn], lhsT=WGR[:, dc, c * 128:(c + 1) * 128], rhs=xT[:, dc, lc:lc + gn],
                                                              start=(dc == 0), stop=(dc == 7)), reads=['WGR%d' % (c // 4), 'xT'], writes=[ppk])
                            T.op('act', lambda e: e.activation(out=SR[:, c, :gn], in_=pp[:, :gn], func=AF.Silu), reads=[ppk], writes=['SR'])
                        T.dma('sp', grT.rearrange("c p t -> p c t")[:, :, gc:gc + gn], SR[:, :, :gn], reads=['SR'], writes=['d:gr'], semkey='SR_st')
                    for (col, n) in pas['subs']:
                        lc = col - c0
                        s_ = k % 2
                        k += 1
                        dstt = vt[s_] if n == 128 else vs_
                        dk_ = ('vt%d' % s_) if n == 128 else 'vs_'
                        for hf in range(2):
                            _, pp, ppk = psum()
                            for dc in range(8):
                                T.op('pe', lambda e: e.matmul(pp[:n, :], lhsT=xT[:, dc, lc:lc + n], rhs=WGV[:, dc, hf * 512:(hf + 1) * 512],
                                                              start=(dc == 0), stop=(dc == 7)), reads=['WGV%d' % hf, 'xT'], writes=[ppk])
                            T.op('act', lambda e: e.copy(out=dstt[:n, hf * 512:(hf + 1) * 512], in_=pp[:n, :]), reads=[ppk], writes=[dk_])
                        if n == 128:
                            T.dma('sp', gv[col:col + n, :], vt[s_][:n, :], reads=[dk_], writes=['d:gv'], semkey='vt%d_st' % s_)
                        else:
                            _, pp, ppk = psum()
                            for dc in range(8):
                                T.op('pe', lambda e: e.matmul(pp[:n, :], lhsT=xT[:, dc, lc:lc + n], rhs=WGK[:, dc, :], start=(dc == 0), stop=(dc == 7)),
                                     reads=['WGK', 'xT'], writes=[ppk])
                            T.op('act', lambda e: e.copy(out=vs_[:n, D:D + 512], in_=pp[:n, :]), reads=[ppk], writes=['vs_'])
                            T.dma('sp', gsv[:, :], vs_[:, :], reads=['vs_'], writes=['d:gsv'], semkey='vs_st')
                T.barrier()

        def gla_rec(OTg):
            with contextlib.ExitStack() as ls:
                lsb = lambda name, shape, dtype=F32: ls.enter_context(nc.sbuf_tensor(un(name), list(shape), dtype))
                QG = lsb("QG", [128, 4, 512], BF); KGt = lsb("KGt", [128, 4, 512], BF); LA = lsb("LA", [128, 4, 512])
                VT = lsb("VT", [128, 4, D], BF); RG = lsb("RG", [128, 8, 512], BF); OF = lsb("OF", [128, 8, 512])
                Sf = lsb("Sf", [128, 4, 256]); Sb = lsb("Sb", [128, 4, 256], BF); Gs = lsb("Gs", [128, 4])
                Lt = lsb("Lt", [128, 16]); Et = lsb("Et", [128, 16]); tm4 = lsb("tm4", [128, 4])
                BC = [lsb("BC%d" % i, [128, 128]) for i in range(8)]; E1 = [lsb("E1%d" % i, [128, 128]) for i in range(8)]
                E2 = [lsb("E2%d" % i, [128, 128]) for i in range(8)]; eb = [lsb("eb%d" % i, [128, 1]) for i in range(8)]
                qd = [lsb("qd%d" % i, [128, 128], BF) for i in range(8)]; ki = [lsb("ki%d" % i, [128, 128], BF) for i in range(8)]
                ke = [lsb("ke%d" % i, [128, 128], BF) for i in range(8)]; KE = [lsb("KE%d" % i, [128, 128], BF) for i in range(8)]
                at = [lsb("at%d" % i, [128, 128], BF) for i in range(8)]
                sq2 = lsb("sq2", [128, 512]); Mt = lsb("Mt", [128, 512]); Vt = lsb("Vt", [128, 512]); tt = lsb("tt", [128, 512])
                gng = lsb("gng", [128, 8])
                with nc.allow_non_contiguous_dma(reason="tiny"):
                    T.dma('sp', gng[:], gla_norm_g.rearrange("o (c p) -> p (o c)", p=128), writes=['gng'], semkey='gng')

                def ln_gate(n, cols_out, rg_ap, rgkey):
                    for h in range(4):
                        _, ps1, ps1k = psum()
                        _, ps2, ps2k = psum()
                        for c2 in range(2):
                            c = h * 2 + c2
                            T.op('act', lambda e: e.activation(out=sq2[:, :n], in_=OF[:, c, :n], func=AF.Square), reads=['OF'], writes=['sq2'])
                            T.op('pe', lambda e: e.matmul(ps1[:, :n], lhsT=onesf[:, 0:128], rhs=OF[:, c, :n], start=(c2 == 0), stop=(c2 == 1)),
                                 reads=['onesf', 'OF'], writes=[ps1k])
                            T.op('pe', lambda e: e.matmul(ps2[:, :n], lhsT=onesf[:, 0:128], rhs=sq2[:, :n], start=(c2 == 0), stop=(c2 == 1)),
                                 reads=['onesf', 'sq2'], writes=[ps2k])
                        T.op('act', lambda e: e.activation(out=Mt[:, :n], in_=ps1[:, :n], func=AF.Copy, scale=1.0 / GDV), reads=[ps1k], writes=['Mt'])
                        T.op('dve', lambda e: e.tensor_tensor(out=Vt[:, :n], in0=Mt[:, :n], in1=Mt[:, :n], op=ALU.mult), reads=['Mt'], writes=['Vt'])
                        T.op('dve', lambda e: e.scalar_tensor_tensor(out=Vt[:, :n], in0=ps2[:, :n], scalar=1.0 / GDV, in1=Vt[:, :n], op0=ALU.mult, op1=ALU.subtract),
                             reads=[ps2k, 'Vt'], writes=['Vt'])
                        T.op('dve', lambda e: e.tensor_scalar(out=Vt[:, :n], in0=Vt[:, :n], scalar1=EPS, scalar2=None, op0=ALU.add), reads=['Vt'], writes=['Vt'])
                        T.op('act', lambda e: e.activation(out=Vt[:, :n], in_=Vt[:, :n], func=AF.Sqrt), reads=['Vt'], writes=['Vt'])
                        T.op('dve', lambda e: e.reciprocal(out=Vt[:, :n], in_=Vt[:, :n]), reads=['Vt'], writes=['Vt'])
                        for c2 in range(2):
                            c = h * 2 + c2
                            T.op('dve', lambda e: e.tensor_tensor(out=tt[:, :n], in0=OF[:, c, :n], in1=Mt[:, :n], op=ALU.subtract), reads=['OF', 'Mt'], writes=['tt'])
                            T.op('dve', lambda e: e.tensor_tensor(out=tt[:, :n], in0=tt[:, :n], in1=Vt[:, :n], op=ALU.mult), reads=['tt', 'Vt'], writes=['tt'])
                            T.op('dve', lambda e: e.scalar_tensor_tensor(out=OTg[:, c, cols_out:cols_out + n], in0=tt[:, :n], scalar=gng[:, c:c + 1],
                                                                         in1=rg_ap[:, c, :n], op0=ALU.mult, op1=ALU.mult),
                                 reads=['tt', 'gng', rgkey], writes=['OTg'])

                kk = [0]

                def run(with_out):
                    for g4 in range(4):
                        gc = g4 * 512
                        T.dma('sp', KGt[:], gkT.rearrange("h p t -> p h t")[:, :, gc:gc + 512], reads=['d:gk'], writes=['KGt'], semkey='KGt')
                        T.dma('sp', LA[:], glaT.rearrange("h p t -> p h t")[:, :, gc:gc + 512], reads=['d:gla'], writes=['LA'], semkey='LA')
                        T.dma('sp', VT[:], gv[gc:gc + 512, :].rearrange("(c p) d -> p c d", p=128), reads=['d:gv'], writes=['VT'], semkey='VT')
                        if with_out:
                            T.dma('sp', QG[:], gqT.rearrange("h p t -> p h t")[:, :, gc:gc + 512], reads=['d:gq'], writes=['QG'], semkey='QG')
                            T.dma('sp', RG[:], grT.rearrange("c p t -> p c t")[:, :, gc:gc + 512], reads=['d:gr'], writes=['RG'], semkey='RG')
                        def front(ch):
                            cs = slice(ch * 128, ch * 128 + 128)
                            HB = [(h, (ch % 2) * 4 + h) for h in range(4)]
                            for (h, b) in HB:
                                T.op('dve', lambda e: e.tensor_tensor_scan(out=BC[b][:], data0=onesf[:, 0:128], data1=LA[:, h, cs], initial=0.0,
                                                                           op0=ALU.mult, op1=ALU.add), reads=['LA', 'onesf'], writes=['BC%d' % b])
                            for (h, b) in HB:
                                T.op('act', lambda e: e.activation(out=eb[b][:], in_=BC[b][:, 127:128], func=AF.Exp), reads=['BC%d' % b], writes=['eb%d' % b])
                                T.op('act', lambda e: e.activation(out=E2[b][:], in_=BC[b][:], func=AF.Exp, scale=-1.0), reads=['BC%d' % b], writes=['E2%d' % b])
                                if with_out:
                                    T.op('act', lambda e: e.activation(out=E1[b][:], in_=BC[b][:], func=AF.Exp), reads=['BC%d' % b], writes=['E1%d' % b])
                            for (h, b) in HB:
                                T.op('dve', lambda e: e.scalar_tensor_tensor(out=ke[b][:], in0=KGt[:, h, cs], scalar=eb[b][:, 0:1], in1=E2[b][:],
                                                                             op0=ALU.mult, op1=ALU.mult), reads=['KGt', 'eb%d' % b, 'E2%d' % b], writes=['ke%d' % b])
                                if with_out:
                                    T.op('dve', lambda e: e.tensor_tensor(out=qd[b][:], in0=QG[:, h, cs], in1=E1[b][:], op=ALU.mult), reads=['QG', 'E1%d' % b], writes=['qd%d' % b])
                                    T.op('dve', lambda e: e.tensor_tensor(out=ki[b][:], in0=KGt[:, h, cs], in1=E2[b][:], op=ALU.mult), reads=['KGt', 'E2%d' % b], writes=['ki%d' % b])
                            for (h, b) in HB:
                                _, pt_, ptk = psum()
                                ptb = pt_[:].bitcast(BF)
                                T.op('pe', lambda e: e.transpose(out=ptb[0:128, 0:128], in_=ke[b][:, :], identity=identb[:, :]), reads=['ke%d' % b, 'identb'], writes=[ptk])
                                T.op('act', lambda e: e.copy(out=KE[b][:], in_=ptb[0:128, 0:128]), reads=[ptk], writes=['KE%d' % b])
                            if with_out:
                                for (h, b) in HB:
                                    _, pa, pak = psum()
                                    T.op('pe', lambda e: e.matmul(pa[0:128, 0:128], lhsT=ki[b][:, :], rhs=qd[b][:, :], start=True, stop=True),
                                         reads=['ki%d' % b, 'qd%d' % b], writes=[pak])
                                    T.op('dve', lambda e: e.tensor_tensor(out=at[b][:], in0=pa[0:128, 0:128], in1=trif[0:128, 0:128], op=ALU.mult),
                                         reads=[pak, 'trif'], writes=['at%d' % b])
                            if not with_out:
                                for (h, b) in HB:
                                    T.op('dve', lambda e: e.tensor_tensor(out=Gs[:, h:h + 1], in0=Gs[:, h:h + 1], in1=BC[b][:, 127:128], op=ALU.add),
                                         reads=['Gs%d' % h, 'BC%d' % b], writes=['Gs%d' % h])

                        def back(ch):
                            cs = slice(ch * 128, ch * 128 + 128)
                            HB = [(h, (ch % 2) * 4 + h) for h in range(4)]
                            if with_out:
                                for (h, b) in HB:
                                    for c2 in range(2):
                                        _, po, pok = psum()
                                        T.op('pe', lambda e: e.matmul(po[:, 0:128], lhsT=VT[:, ch, h * 256 + c2 * 128:h * 256 + (c2 + 1) * 128], rhs=at[b][:, :],
                                                                      start=True, stop=False), reads=['VT', 'at%d' % b], writes=[pok])
                                        T.op('pe', lambda e: e.matmul(po[:, 0:128], lhsT=Sb[:, h, c2 * 128:(c2 + 1) * 128], rhs=qd[b][:, :],
                                                                      start=False, stop=True), reads=['Sb%d' % h, 'qd%d' % b], writes=[pok])
                                        T.op('act', lambda e: e.copy(out=OF[:, h * 2 + c2, cs], in_=po[:, 0:128]), reads=[pok], writes=['OF'])
                            for (h, b) in HB:
                                _, pst, pstk = psum()
                                T.op('pe', lambda e: e.matmul(pst[:, 0:256], lhsT=KE[b][:, :], rhs=VT[:, ch, h * 256:(h + 1) * 256], start=True, stop=True),
                                     reads=['KE%d' % b, 'VT'], writes=[pstk])
                                T.op('dve', lambda e: e.scalar_tensor_tensor(out=Sf[:, h, :], in0=Sf[:, h, :], scalar=eb[b][:, 0:1], in1=pst[:, 0:256],
                                                                             op0=ALU.mult, op1=ALU.add), reads=['Sf%d' % h, 'eb%d' % b, pstk], writes=['Sf%d' % h])
                                T.op('act', lambda e: e.copy(out=Sb[:, h, :], in_=Sf[:, h, :]), reads=['Sf%d' % h], writes=['Sb%d' % h])

                        front(0)
                        for ch in range(4):
                            if ch + 1 < 4:
                                front(ch + 1)
                            back(ch)
                        if with_out:
                            ln_gate(512, gc, RG, 'RG')

                sgs = contextlib.ExitStack()
                SG = sgs.enter_context(nc.sbuf_tensor(un("SG"), [128, 4, 4, GDV + 64], F32))
                T.op('dve', lambda e: e.memset(Sf[:], 0.0), writes=['Sf0', 'Sf1', 'Sf2', 'Sf3'])
                T.op('dve', lambda e: e.memset(Sb[:], 0.0), writes=['Sb0', 'Sb1', 'Sb2', 'Sb3'])
                T.op('dve', lambda e: e.memset(Gs[:], 0.0), writes=['Gs0', 'Gs1', 'Gs2', 'Gs3'])
                run(False)
                T.dma('sp', sloc.rearrange("(h p) c -> p h c", p=128)[:, :, 0:GDV], Sf[:], reads=['Sf0', 'Sf1', 'Sf2', 'Sf3'], writes=['d:sloc'], semkey='Sf_st')
                with nc.allow_non_contiguous_dma(reason="tiny"):
                    T.dma('sp', sloc.rearrange("(h p) c -> p h c", p=128)[:, :, GDV:GDV + 1], Gs[:].rearrange("p (h o) -> p h o", o=1), reads=['Gs0', 'Gs1', 'Gs2', 'Gs3'], writes=['d:sloc2'], semkey='Gs_st')
                T.barrier()
                allgather(sloc, sg_, 'd:sloc', 'd:sg', 'cc_s')
                T.dma('sp', SG[:], sg_.rearrange("(g h p) c -> p g h c", g=4, h=4), reads=['d:sg'], writes=['SG'], semkey='SG')
                for h in range(4):
                    for rp in range(4):
                        T.op('dve', lambda e: e.tensor_tensor(out=tm4[:], in0=SG[:, :, h, GDV], in1=crank[:, 20 + rp * 4:24 + rp * 4], op=ALU.mult),
                             reads=['SG', 'crank'], writes=['tm4'])
                        T.op('dve', lambda e: e.tensor_reduce(out=Lt[:, h * 4 + rp:h * 4 + rp + 1], in_=tm4[:], axis=AX.X, op=ALU.add), reads=['tm4'], writes=['Lt'])
                T.op('act', lambda e: e.activation(out=Et[:], in_=Lt[:], func=AF.Exp), reads=['Lt'], writes=['Et'])
                for h in range(4):
                    T.op('dve', lambda e: e.tensor_tensor(out=Et[:, h * 4:h * 4 + 4], in0=Et[:, h * 4:h * 4 + 4], in1=crank[:, 36:40], op=ALU.mult),
                         reads=['Et', 'crank'], writes=['Et'])
                for h in range(4):
                    T.op('dve', lambda e: e.tensor_scalar(out=Sf[:, h, :], in0=SG[:, 0, h, 0:GDV], scalar1=Et[:, h * 4:h * 4 + 1], scalar2=None, op0=ALU.mult),
                         reads=['SG', 'Et'], writes=['Sf%d' % h])
                    for rp in range(1, 4):
                        T.op('dve', lambda e: e.scalar_tensor_tensor(out=Sf[:, h, :], in0=SG[:, rp, h, 0:GDV], scalar=Et[:, h * 4 + rp:h * 4 + rp + 1],
                                                                     in1=Sf[:, h, :], op0=ALU.mult, op1=ALU.add), reads=['SG', 'Et', 'Sf%d' % h], writes=['Sf%d' % h])
                    T.op('act', lambda e: e.copy(out=Sb[:, h, :], in_=Sf[:, h, :]), reads=['Sf%d' % h], writes=['Sb%d' % h])
                T.barrier()
                sgs.close()
                run(True)
                T.dma('sp', o_sp.rearrange("(h p) c -> p h c", p=128), Sf[:], reads=['Sf0', 'Sf1', 'Sf2', 'Sf3'], semkey='Sf_st')
                T.barrier()
                QSs = lsb("QSs", [128, 4, NS], BF); LAs = lsb("LAs", [128, 4, NS]); ELs = lsb("ELs", [128, 4, NS]); RGs = lsb("RGs", [128, 8, NS], BF)
                KVb = lsb("KVb", [1, NS, D + 512], BF)
                S0 = [lsb("S0%d" % i, [128, 256]) for i in range(2)]; Snb = [lsb("Snb%d" % i, [128, 256], BF) for i in range(2)]
                with nc.allow_non_contiguous_dma(reason="tiny"):
                    T.dma('sp', QSs[:], gqT.rearrange("h p t -> p h t")[:, :, NP_:NT], reads=['d:gq'], writes=['QSs'], semkey='QSs')
                    T.dma('sp', LAs[:], glaT.rearrange("h p t -> p h t")[:, :, NP_:NT], reads=['d:gla'], writes=['LAs'], semkey='LAs')
                    T.dma('sp', RGs[:], grT.rearrange("c p t -> p c t")[:, :, NP_:NT], reads=['d:gr'], writes=['RGs'], semkey='RGs')
                T.dma('pool', KVb[:], bass.AP(gsv.tensor, 0, [[0, 1], [D + 512, NS], [1, D + 512]]), reads=['d:gsv'], writes=['KVb'], semkey='KVb')
                T.op('act', lambda e: e.activation(out=ELs[:], in_=LAs[:], func=AF.Exp), reads=['LAs'], writes=['ELs'])
                i_ = 0
                for s_ in range(NS):
                    for h in range(4):
                        b = i_ % 2
                        i_ += 1
                        T.dma('sp', S0[b][:], sgla[s_, h], writes=['S0%d' % b], semkey='S0%d' % b)
                        _, pkv, pkvk = psum()
                        T.op('pe', lambda e: e.matmul(pkv[:, 0:256], lhsT=KVb[0:1, s_, D + h * 128:D + (h + 1) * 128], rhs=KVb[0:1, s_, h * 256:(h + 1) * 256],
                                                      start=True, stop=True), reads=['KVb'], writes=[pkvk])
                        T.op('dve', lambda e: e.scalar_tensor_tensor(out=S0[b][:], in0=S0[b][:], scalar=ELs[:, h, s_:s_ + 1], in1=pkv[:, 0:256],
                                                                     op0=ALU.mult, op1=ALU.add), reads=['S0%d' % b, 'ELs', pkvk], writes=['S0%d' % b])
                        T.dma('sp', o_ss[(s_ * 4 + h) * 128:(s_ * 4 + h + 1) * 128, :], S0[b][:], reads=['S0%d' % b], semkey='S0%d_st' % b)
                        T.op('act', lambda e: e.copy(out=Snb[b][:], in_=S0[b][:]), reads=['S0%d' % b], writes=['Snb%d' % b])
                        for c2 in range(2):
                            _, po, pok = psum()
                            T.op('pe', lambda e: e.matmul(po[:, 0:1], lhsT=Snb[b][:, c2 * 128:(c2 + 1) * 128], rhs=QSs[:, h, s_:s_ + 1], start=True, stop=True),
                                 reads=['Snb%d' % b, 'QSs'], writes=[pok])
                            T.op('act', lambda e: e.copy(out=OF[:, h * 2 + c2, s_:s_ + 1], in_=po[:, 0:1]), reads=[pok], writes=['OF'])
                ln_gate(NS, NP_, RGs, 'RGs')
                T.barrier()

        with contextlib.ExitStack() as s1:
            cT = s1.enter_context(nc.sbuf_tensor("cT", [16, NP_], F32))
            with contextlib.ExitStack() as w1:
                alloc_work(w1)
                ffn_stage(0, 0, 0, x_in, xres)
                fox_proj(cT)
            chk(1)
            with contextlib.ExitStack() as s_ot:
                OT = s_ot.enter_context(nc.sbuf_tensor("OT", [64, 16, NT], BF))
                fox_attn(cT, OT)
                chk(5)
                with contextlib.ExitStack() as w2:
                    alloc_work(w2)
                    wo_stage(OT, fox_w_o, 0, 1)
                chk(6)
        with contextlib.ExitStack() as w3:
            alloc_work(w3)
            ffn_stage(0, 1, 2, xres, xres)
            ffn_stage(1, 0, 0, xres, xres)
            chk(7)
            gla_proj()
            chk(8)
            with contextlib.ExitStack() as s2:
                OTg = s2.enter_context(nc.sbuf_tensor("OTg", [128, 8, NT], BF))
                gla_rec(OTg)
                chk(9)
                wo_stage(OTg, gla_w_o, 1, 1)
            ffn_stage(1, 1, 2, xres, o_y)
        T.finish()
    return nc


_NC = None


def kernel(x_prompt, x_sample, cache_fox_k, cache_fox_v, cache_fox_logf, state_gla, page_table,
           ln_g, ln_b, ffn_w_in, ffn_w_out, fox_w_in, fox_b_f, fox_w_o,
           gla_w_in, gla_w_a2, gla_b_a, gla_norm_g, gla_w_o):
    global _NC
    f = lambda a: np.ascontiguousarray(np.asarray(a))
    NPOOL = int(np.asarray(cache_fox_k).shape[1])
    nc = build(8, NPOOL)
    ident = np.eye(128, dtype=np.float32)
    r_ = np.arange(128)
    tri = (r_[:, None] <= r_[None, :]).astype(np.float32)
    slow = (r_[:, None] > r_[None, :]).astype(np.float32)
    ck = f(cache_fox_k).reshape(NPOOL * 64, 2 * D)
    cv = f(cache_fox_v).reshape(NPOOL * 64, 2 * D)
    clf = f(cache_fox_logf).reshape(NPOOL * 2, 64 * H)
    shared = dict(cache_k=ck, cache_v=cv, cache_lf=clf, ln_g=f(ln_g), ln_b=f(ln_b), ffn_w_in=f(ffn_w_in),
                  ffn_w_out=f(ffn_w_out), fox_w_in=f(fox_w_in)[0], fox_b_f=f(fox_b_f), fox_w_o=f(fox_w_o)[0],
                  gla_w_in=f(gla_w_in)[0], gla_w_a2=f(gla_w_a2)[0], gla_b_a=f(gla_b_a), gla_norm_g=f(gla_norm_g),
                  gla_w_o=f(gla_w_o)[0], c_ident=ident, c_tri=tri, c_slow=slow)
    xp = f(x_prompt); xs = f(x_sample); sg = f(state_gla); pt = f(page_table)
    in_maps = []
    for c in range(8):
        b, r = c // 4, c % 4
        x_in = np.concatenate([xp[b, r * NP_:(r + 1) * NP_, :], xs[4 * c:4 * c + 4, 0, :]], axis=0)
        cr = np.zeros((128, 80), np.float32)
        cr[:, 40] = np.arange(128) % 2
        cr[:, 48:80] = np.arange(32)[None, :]
        for a in range(4):
            for bb in range(4):
                cr[:, a * 4 + bb] = 1.0 if (a <= bb < r) else 0.0
                cr[:, 20 + a * 4 + bb] = 1.0 if (a < bb < r) else 0.0
            cr[:, 16 + a] = 0.0 if a < r else PEN
            cr[:, 36 + a] = 1.0 if a < r else 0.0
        m = dict(shared)
        m.update(x_in=np.ascontiguousarray(x_in), state_gla=np.ascontiguousarray(sg[0, 4 * c:4 * c + 4]),
                 page_table=np.ascontiguousarray(pt[4 * c:4 * c + 4]), c_rank=cr)
        in_maps.append(m)
    res = run_bass_kernel_spmd(nc, in_maps, core_ids=list(range(8)))
    R = res.results
    y_p = np.zeros((2, 8192, D), np.float32); y_s = np.zeros((32, 1, D), np.float32)
    k_p = np.zeros((1, 2, 8192, H, HD), np.float32); v_p = np.zeros_like(k_p); lf_p = np.zeros((1, 2, 8192, H), np.float32)
    k_s = np.zeros((1, 32, 1, H, HD), np.float32); v_s = np.zeros_like(k_s); lf_s = np.zeros((1, 32, 1, H), np.float32)
    sg_p = np.zeros((1, 2, GH, GDK, GDV), np.float32); sg_s = np.zeros((1, 32, GH, GDK, GDV), np.float32)
    for c in range(8):
        b, r = c // 4, c % 4
        sl = slice(r * NP_, (r + 1) * NP_)
        y_p[b, sl] = R[c]["o_y"][:NP_]; y_s[4 * c:4 * c + 4, 0] = R[c]["o_y"][NP_:]
        k_p[0, b, sl] = R[c]["o_k"][:NP_].reshape(NP_, H, HD); k_s[0, 4 * c:4 * c + 4, 0] = R[c]["o_k"][NP_:].reshape(4, H, HD)
        v_p[0, b, sl] = R[c]["o_v"][:NP_].reshape(NP_, H, HD); v_s[0, 4 * c:4 * c + 4, 0] = R[c]["o_v"][NP_:].reshape(4, H, HD)
        lf_p[0, b, sl] = R[c]["o_lf"][:NP_]; lf_s[0, 4 * c:4 * c + 4, 0] = R[c]["o_lf"][NP_:]
        if r == 3:
            sg_p[0, b] = R[c]["o_sp"].reshape(GH, GDK, GDV)
        sg_s[0, 4 * c:4 * c + 4] = R[c]["o_ss"].reshape(4, GH, GDK, GDV)
    return (y_p, y_s, k_p, v_p, lf_p, k_s, v_s, lf_s, sg_p, sg_s)
```

```python
import contextlib
import os
import numpy as np
import concourse.bass as bass
import concourse.mybir as mybir
from concourse.bass_utils import run_bass_kernel_spmd

F32 = mybir.dt.float32
BF = mybir.dt.bfloat16
I32 = mybir.dt.int32
AF = mybir.ActivationFunctionType
ALU = mybir.AluOpType
AX = mybir.AxisListType

D = 1024
NP_ = 2048
NS = 4
NT = NP_ + NS
DFF = 2816
H = 16
HD = 64
ALPHA = 4.0 ** 0.25
EPS = 1e-5
GH = 4
GDK = 128
GDV = 256
GC = 64
NPOOL = 2560
PEN = -30000.0

PASSES = [
    dict(c0=0, groups=[(0, 512), (512, 512)], subs=[(i * 128, 128) for i in range(8)]),
    dict(c0=1024, groups=[(1024, 512), (1536, 512), (2048, 4)],
         subs=[(1024 + i * 128, 128) for i in range(8)] + [(2048, 4)]),
]


class _Stop(Exception):
    pass


_TRS = []


def chk(level):
    if int(os.environ.get('KSTOP', '0')) == level:
        _TRS[-1].finish()
        _TRS[-1].stopped = True


class TR:
    def __init__(self, nc, es):
        self.nc = nc
        self.es = es
        self.E = {'pe': nc.tensor, 'act': nc.scalar, 'dve': nc.vector, 'pool': nc.gpsimd, 'sp': nc.sync}
        self.esem = {}
        self.ecnt = {}
        self.nsem = 0
        for e in self.E:
            self._newsem(e)
        self.waited = {e: {} for e in self.E}
        self.lastw = {}
        self.readers = {}
        self.dsem = {}
        self.dcnt = {}
        self.allsems = []
        self.freed = []
        self.psi = 0
        self.stopped = False
        _TRS.append(self)

    def _newsem(self, e):
        self.nsem += 1
        s = self.es.enter_context(self.nc.semaphore("se_%s_%d" % (e, self.nsem)))
        self.esem[e] = s
        self.ecnt[e] = 0

    def _need(self, eng, toks):
        for (sem, val, owner) in toks:
            if eng == 'pe' and owner == 'pe':
                continue
            w = self.waited[eng]
            k = id(sem)
            if w.get(k, 0) >= val:
                continue
            self.E[eng].wait_ge(sem, val)
            w[k] = val

    def deps(self, eng, reads, writes):
        toks = []
        for k in reads:
            if k in self.lastw:
                toks.append(self.lastw[k])
        for k in writes:
            if k in self.lastw:
                toks.append(self.lastw[k])
            toks += list(self.readers.get(k, {}).values())
        self._need(eng, toks)

    def commit(self, tok, reads, writes):
        for k in reads:
            self.readers.setdefault(k, {})[id(tok[0])] = tok
        for k in writes:
            self.lastw[k] = tok
            self.readers[k] = {}

    def op(self, eng, fn, reads=(), writes=()):
        if self.stopped:
            return None
        self.deps(eng, reads, writes)
        ins = fn(self.E[eng])
        if self.ecnt[eng] >= 30000:
            self._newsem(eng)
        self.ecnt[eng] += 1
        ins.then_inc(self.esem[eng], 1)
        tok = (self.esem[eng], self.ecnt[eng], eng)
        self.commit(tok, reads, writes)
        return tok

    def _dsem(self, key):
        if key not in self.dsem:
            if self.freed:
                sem, cnt = self.freed.pop()
                self.dsem[key] = sem
                self.dcnt[key] = cnt
            else:
                self.nsem += 1
                self.dsem[key] = self.es.enter_context(self.nc.semaphore("sd_%d" % self.nsem))
                self.dcnt[key] = 0
        return self.dsem[key]

    def dma(self, q, out, in_, reads=(), writes=(), semkey=None, inc=16, fn=None, **kw):
        if self.stopped:
            return None
        self.deps(q, reads, writes)
        sem = self._dsem(semkey)
        if fn is None:
            ins = self.E[q].dma_start(out=out, in_=in_, **kw)
        else:
            ins = fn(self.E[q])
        self.dcnt[semkey] += inc
        ins.then_inc(sem, inc)
        tok = (sem, self.dcnt[semkey], 'dma')
        self.commit(tok, reads, writes)
        return tok

    def barrier(self, keep=None):
        if self.stopped:
            return
        kept = lambda k: keep is not None and k.startswith(keep)
        toks = [(self.esem[e], self.ecnt[e], e) for e in self.E if self.ecnt[e] > 0]
        toks += [(self.dsem[k], self.dcnt[k], 'dma') for k in self.dsem if self.dcnt[k] > 0 and not kept(k)]
        for e in self.E:
            self._need(e, [t for t in toks if not (t[2] == e and e != 'pe')] if e != 'pe' else [t for t in toks if t[2] != 'pe'])
        for k in list(self.dsem):
            if kept(k):
                continue
            self.freed.append((self.dsem[k], self.dcnt[k]))
            del self.dsem[k]
            del self.dcnt[k]

    def finish(self):
        if self.stopped:
            return
        toks = [(self.esem[e], self.ecnt[e], e) for e in self.E if self.ecnt[e] > 0 and e != 'sp']
        toks += [(self.dsem[k], self.dcnt[k], 'dma') for k in self.dsem if self.dcnt[k] > 0]
        toks += [(sm, c, 'dma') for (sm, c) in self.freed if c > 0]
        self._need('sp', toks)


def build(ncores=8, NPOOL=NPOOL):
    nc = bass.Bass("TRN2", target_bir_lowering=False, num_devices=ncores)
    dt = nc.dram_tensor

    def din(name, shape, dtype=F32):
        return dt(name, list(shape), dtype, kind="ExternalInput").ap()

    def dout(name, shape, dtype=F32):
        return dt(name, list(shape), dtype, kind="ExternalOutput").ap()

    def dint(name, shape, dtype=F32):
        return dt(name, list(shape), dtype, kind="Internal").ap()

    x_in = din("x_in", [NT, D])
    ck = din("cache_k", [NPOOL * 64, 2 * D])
    cv = din("cache_v", [NPOOL * 64, 2 * D])
    clf = din("cache_lf", [NPOOL * 2, 64 * H])
    sgla = din("state_gla", [NS, GH, GDK, GDV])
    ptab = din("page_table", [NS, 64], I32)
    ln_g = din("ln_g", [2, 3, D])
    ln_b = din("ln_b", [2, 3, D])
    ffn_w_in = din("ffn_w_in", [2, 2, D, 2 * DFF])
    ffn_w_out = din("ffn_w_out", [2, 2, DFF, D])
    fox_w_in = din("fox_w_in", [D, 3 * D + H])
    fox_b_f = din("fox_b_f", [1, H])
    fox_w_o = din("fox_w_o", [D, D])
    gla_w_in = din("gla_w_in", [D, 3088])
    gla_w_a2 = din("gla_w_a2", [16, 512])
    gla_b_a = din("gla_b_a", [1, 512])
    gla_norm_g = din("gla_norm_g", [1, D])
    gla_w_o = din("gla_w_o", [D, D])
    c_ident = din("c_ident", [128, 128])
    c_tri = din("c_tri", [128, 128])
    c_slow = din("c_slow", [128, 128])
    c_rank = din("c_rank", [128, 80])
    o_y = dout("o_y", [NT, D])
    o_k = dout("o_k", [NT, D])
    o_v = dout("o_v", [NT, D])
    o_lf = dout("o_lf", [NT, H])
    o_sp = dout("o_sp", [GH * GDK, GDV])
    o_ss = dout("o_ss", [NS * GH * GDK, GDV])
    xres = dint("xres", [NT, D])
    qT_d = dint("qT_d", [H, HD, NP_], BF)
    ktloc_c = [dint("ktloc%d" % i, [140, NP_], BF) for i in range(8)]
    ktg_c = [dint("ktg%d" % i, [560, NP_], BF) for i in range(8)]
    vloc_c = [dint("vloc%d" % i, [256, 1040], BF) for i in range(8)]
    vg_c = [dint("vg%d" % i, [1024, 1040], BF) for i in range(8)]
    totloc_f = dint("totloc", [1, 128])
    totloc = totloc_f[:, 0:H]
    totg_f = dint("totg", [4, 128])
    totg = totg_f[:, 0:H]
    qaug_d = dint("qaug_d", [5, H, 6, NP_], BF)
    srow_d = dint("srow_d", [NS, 3 * D + 2 * H])
    gq_d = dint("gq_d", [NT, 512])
    sloc = dint("sloc", [GH * GDK, GDV + 64])
    sg_ = dint("sgath", [4 * GH * GDK, GDV + 64])

    _uc = [0]

    def un(name):
        _uc[0] += 1
        return "%s_%d" % (name, _uc[0])

    es = contextlib.ExitStack()
    with es:
        T = TR(nc, es)
        sb = lambda name, shape, dtype=F32: es.enter_context(nc.sbuf_tensor(un(name), list(shape), dtype))
        PS = [es.enter_context(nc.psum_tensor("psb%d" % i, [128, 512], F32)) for i in range(8)]

        PSN = [6]

        def psum():
            T.psi = (T.psi + 1) % PSN[0]
            i = T.psi
            return i, PS[i], 'ps%d' % i

        def psum_r(i):
            return i, PS[i], 'ps%d' % i

        identf = sb("identf", [128, 128]); identb = sb("identb", [128, 128], BF)
        trib = sb("trib", [128, 128], BF); slowf = sb("slowf", [128, 128])
        onesf = sb("onesf", [128, 512]); onesb = sb("onesb", [128, 128], BF)
        crank = sb("crank", [128, 80])
        trif = sb("trif", [128, 128])
        T.dma('sp', identf[:], c_ident, writes=['identf'], semkey='c0')
        T.dma('sp', trif[:], c_tri, writes=['trif'], semkey='c1')
        T.dma('sp', slowf[:], c_slow, writes=['slowf'], semkey='c2')
        T.dma('sp', crank[:], c_rank, writes=['crank'], semkey='c3')
        T.op('dve', lambda e: e.tensor_copy(out=identb[:], in_=identf[:]), reads=['identf'], writes=['identb'])
        T.op('dve', lambda e: e.tensor_copy(out=trib[:], in_=trif[:]), reads=['trif'], writes=['trib'])
        T.op('dve', lambda e: e.memset(onesf[:], 1.0), writes=['onesf'])
        mbb = sb("mbb", [128, 128], BF)
        T.op('dve', lambda e: e.tensor_scalar(out=mbb[:], in0=slowf[:], scalar1=PEN, scalar2=None, op0=ALU.mult), reads=['slowf'], writes=['mbb'])
        T.op('dve', lambda e: e.memset(onesb[:], 1.0), writes=['onesb'])

        class _W:
            pass
        WKH = _W()

        def alloc_work(stack):
            a = lambda name, shape, dtype=F32: stack.enter_context(nc.sbuf_tensor(un(name), list(shape), dtype))
            WKH.xT = a("xT", [128, 8, 1028], BF)
            WKH.ld_f = [a("ld_f%d" % i, [128, D]) for i in range(2)]
            WKH.ld_b = [a("ld_b%d" % i, [128, D], BF) for i in range(2)]
            WKH.Gt = a("Gt", [128, D]); WKH.Bt = a("Bt", [128, D])
            WKH.zt = a("zt", [128, D]); WKH.xo = [a("xo%d" % i, [128, D]) for i in range(2)]
            WKH.junk = a("junk", [128, D])
            WKH.st_ = a("stats", [128, 16])
            T.barrier()
        cnt = dict(ld=0, xo=0, misc=0)

        def load_xT(pas, src):
            xT, ld_f, ld_b, Gt, Bt, zt, xo, junk, st_ = WKH.xT, WKH.ld_f, WKH.ld_b, WKH.Gt, WKH.Bt, WKH.zt, WKH.xo, WKH.junk, WKH.st_
            c0 = pas['c0']
            for (col, n) in pas['subs']:
                s = cnt['ld'] % 2
                cnt['ld'] += 1
                T.dma('sp', ld_f[s][:n, :], src[col:col + n, :], reads=['d:xres:%d' % col], writes=['ld_f%d' % s],
                      semkey='ld_f%d' % s)
                T.op('act', lambda e: e.copy(out=ld_b[s][:n, :], in_=ld_f[s][:n, :]), reads=['ld_f%d' % s],
                     writes=['ld_b%d' % s])
                pi, pt, pk = psum()
                ptb = pt[:].bitcast(BF)
                for dc in range(8):
                    T.op('pe', lambda e: e.transpose(out=ptb[:, dc * 128:dc * 128 + n], in_=ld_b[s][:n, dc * 128:(dc + 1) * 128],
                                                     identity=identb[:n, :n]),
                         reads=['ld_b%d' % s, 'identb'], writes=[pk])
                src_v = ptb[:, 0:1024].rearrange("p (c t) -> p c t", c=8)[:, :, 0:n]
                T.op('dve', lambda e: e.tensor_copy(out=xT[:, :, col - c0:col - c0 + n], in_=src_v), reads=[pk],
                     writes=['xT'])

        def ln_stage(col, n, ybanks, res_src, gi, gk, dst, gb_loaded):
            xT, ld_f, ld_b, Gt, Bt, zt, xo, junk, st_ = WKH.xT, WKH.ld_f, WKH.ld_b, WKH.Gt, WKH.Bt, WKH.zt, WKH.xo, WKH.junk, WKH.st_
            if not gb_loaded:
                T.dma('sp', Gt[:], ln_g[gi, gk:gk + 1, :].partition_broadcast(128), writes=['Gt'], semkey='Gt')
                T.dma('sp', Bt[:], ln_b[gi, gk:gk + 1, :].partition_broadcast(128), writes=['Bt'], semkey='Bt')
            s = cnt['ld'] % 2
            cnt['ld'] += 1
            xr = ld_f[s]
            T.dma('sp', xr[:n, :], res_src[col:col + n, :], reads=['d:xres:%d' % col], writes=['ld_f%d' % s],
                  semkey='ld_f%d' % s)
            for hf in range(2):
                (yp, yk) = ybanks[hf]
                T.op('dve', lambda e: e.scalar_tensor_tensor(out=zt[:n, hf * 512:(hf + 1) * 512], in0=xr[:n, hf * 512:(hf + 1) * 512],
                                                             scalar=ALPHA, in1=yp[:n, :], op0=ALU.mult, op1=ALU.add,
                                                             accum_out=st_[:n, hf:hf + 1]),
                     reads=['ld_f%d' % s, yk], writes=['zt', 'st'])
            T.op('act', lambda e: e.activation(out=junk[:n, :], in_=zt[:n, :], func=AF.Square, accum_out=st_[:n, 2:3]),
                 reads=['zt'], writes=['junk', 'st2'])
            T.op('dve', lambda e: e.tensor_tensor(out=st_[:n, 3:4], in0=st_[:n, 0:1], in1=st_[:n, 1:2], op=ALU.add),
                 reads=['st'], writes=['st'])
            T.op('dve', lambda e: e.tensor_scalar(out=st_[:n, 4:5], in0=st_[:n, 3:4], scalar1=1.0 / D, scalar2=None, op0=ALU.mult),
                 reads=['st'], writes=['st'])
            T.op('dve', lambda e: e.tensor_tensor(out=st_[:n, 5:6], in0=st_[:n, 4:5], in1=st_[:n, 4:5], op=ALU.mult),
                 reads=['st'], writes=['st'])
            T.op('dve', lambda e: e.scalar_tensor_tensor(out=st_[:n, 6:7], in0=st_[:n, 2:3], scalar=1.0 / D, in1=st_[:n, 5:6],
                                                         op0=ALU.mult, op1=ALU.subtract),
                 reads=['st', 'st2'], writes=['st'])
            T.op('dve', lambda e: e.tensor_scalar(out=st_[:n, 6:7], in0=st_[:n, 6:7], scalar1=EPS, scalar2=None, op0=ALU.add),
                 reads=['st'], writes=['st'])
            T.op('act', lambda e: e.activation(out=st_[:n, 7:8], in_=st_[:n, 6:7], func=AF.Sqrt), reads=['st'], writes=['st3'])
            T.op('dve', lambda e: e.reciprocal(out=st_[:n, 8:9], in_=st_[:n, 7:8]), reads=['st3'], writes=['st'])
            T.op('dve', lambda e: e.tensor_scalar(out=zt[:n, :], in0=zt[:n, :], scalar1=st_[:n, 4:5], scalar2=st_[:n, 8:9],
                                                  op0=ALU.subtract, op1=ALU.mult), reads=['zt', 'st'], writes=['zt'])
            T.op('dve', lambda e: e.tensor_tensor(out=zt[:n, :], in0=zt[:n, :], in1=Gt[:n, :], op=ALU.mult),
                 reads=['zt', 'Gt'], writes=['zt'])
            o = cnt['xo'] % 2
            cnt['xo'] += 1
            T.op('dve', lambda e: e.tensor_tensor(out=xo[o][:n, :], in0=zt[:n, :], in1=Bt[:n, :], op=ALU.add),
                 reads=['zt', 'Bt'], writes=['xo%d' % o])
            T.dma('sp', dst[col:col + n, :], xo[o][:n, :], reads=['xo%d' % o], writes=['d:xres:%d' % col], semkey='xo%d_st' % o)

        def ffn_stage(li, hi, gk, src, dst):
            xT, ld_f, ld_b, Gt, Bt, zt, xo, junk, st_ = WKH.xT, WKH.ld_f, WKH.ld_b, WKH.Gt, WKH.Bt, WKH.zt, WKH.xo, WKH.junk, WKH.st_
            with contextlib.ExitStack() as ls:
                lsb = lambda name, shape, dtype=F32: ls.enter_context(nc.sbuf_tensor(un(name), list(shape), dtype))
                aT = lsb("aT", [128, 22, 1028], BF)
                WOUT = lsb("WOUT", [128, 22, D], BF)
                WIN = [lsb("WIN%d" % i, [128, 8, 2, 256], BF) for i in range(2)]
                sg = [lsb("sg%d" % i, [128, 512]) for i in range(2)]
                w_in = ffn_w_in[li, hi].rearrange("(c p) f -> p c f", p=128)
                w_out = ffn_w_out[li, hi].rearrange("(c p) d -> p c d", p=128)
                first = True
                for pas in PASSES:
                    c0 = pas['c0']
                    load_xT(pas, src)

                    def load_w(jc):
                        s = jc % 2
                        for gu in range(2):
                            T.dma('pool', WIN[s][:, :, gu, :], w_in[:, :, gu * DFF + jc * 256: gu * DFF + (jc + 1) * 256],
                                  writes=['WIN%d_%d' % (s, gu)], semkey='WIN%d_%d' % (s, gu))
                        T.dma('pool', WOUT[:, 2 * jc:2 * jc + 2, :], w_out[:, 2 * jc:2 * jc + 2, :], writes=['WOUT%d' % jc],
                              semkey='WOUT%d' % jc)

                    load_w(0)
                    k = 0
                    for jc in range(11):
                        if jc + 1 < 11:
                            load_w(jc + 1)
                        s = jc % 2
                        for sub in range(2):
                            f = 2 * jc + sub
                            for (gc, gn) in pas['groups']:
                                lc = gc - c0
                                _, pg, pgk = psum()
                                _, pu, puk = psum()
                                for dc in range(8):
                                    T.op('pe', lambda e: e.matmul(pg[:, :gn], lhsT=WIN[s][:, dc, 0, sub * 128:(sub + 1) * 128],
                                                                  rhs=xT[:, dc, lc:lc + gn], start=(dc == 0), stop=(dc == 7)),
                                         reads=['WIN%d_0' % s, 'xT'], writes=[pgk])
                                for dc in range(8):
                                    T.op('pe', lambda e: e.matmul(pu[:, :gn], lhsT=WIN[s][:, dc, 1, sub * 128:(sub + 1) * 128],
                                                                  rhs=xT[:, dc, lc:lc + gn], start=(dc == 0), stop=(dc == 7)),
                                         reads=['WIN%d_1' % s, 'xT'], writes=[puk])
                                q = k % 2
                                k += 1
                                T.op('act', lambda e: e.activation(out=sg[q][:, :gn], in_=pg[:, :gn], func=AF.Silu),
                                     reads=[pgk], writes=['sg%d' % q])
                                T.op('dve', lambda e: e.scalar_tensor_tensor(out=aT[:, f, lc:lc + gn], in0=pu[:, :gn], scalar=0.5,
                                                                             in1=sg[q][:, :gn], op0=ALU.mult, op1=ALU.mult),
                                     reads=[puk, 'sg%d' % q], writes=['aT'])
                    for (col, n) in pas['subs']:
                        lc = col - c0
                        _, y0, y0k = psum()
                        _, y1, y1k = psum()
                        for f in range(22):
                            T.op('pe', lambda e: e.matmul(y0[:n, :], lhsT=aT[:, f, lc:lc + n], rhs=WOUT[:, f, 0:512],
                                                          start=(f == 0), stop=(f == 21)),
                                 reads=['aT', 'WOUT%d' % (f // 2)], writes=[y0k])
                            T.op('pe', lambda e: e.matmul(y1[:n, :], lhsT=aT[:, f, lc:lc + n], rhs=WOUT[:, f, 512:1024],
                                                          start=(f == 0), stop=(f == 21)),
                                 reads=['aT', 'WOUT%d' % (f // 2)], writes=[y1k])
                        ln_stage(col, n, [(y0, y0k), (y1, y1k)], src, li, gk, dst, not first)
                        first = False
                T.barrier()

        def wo_stage(OT, w_o_ap, li, gk):
            with contextlib.ExitStack() as ls:
                KP, NCH = OT.shape[0], OT.shape[1]
                WO = ls.enter_context(nc.sbuf_tensor(un("WO"), [KP, NCH, D], BF))
                hh = NCH // 2
                T.dma('pool', WO[:, 0:hh, :], w_o_ap.rearrange("(h p) d -> p h d", p=KP)[:, 0:hh, :], writes=['WO0'], semkey='WO0')
                T.dma('pool', WO[:, hh:NCH, :], w_o_ap.rearrange("(h p) d -> p h d", p=KP)[:, hh:NCH, :], writes=['WO1'], semkey='WO1')
                first = True
                for pas in PASSES:
                    for (col, n) in pas['subs']:
                        _, y0, y0k = psum()
                        _, y1, y1k = psum()
                        for h in range(NCH):
                            T.op('pe', lambda e: e.matmul(y0[:n, :], lhsT=OT[:, h, col:col + n], rhs=WO[:, h, 0:512],
                                                          start=(h == 0), stop=(h == NCH - 1)),
                                 reads=['OT', 'OTs', 'WO%d' % (h // hh)], writes=[y0k])
                            T.op('pe', lambda e: e.matmul(y1[:n, :], lhsT=OT[:, h, col:col + n], rhs=WO[:, h, 512:1024],
                                                          start=(h == 0), stop=(h == NCH - 1)),
                                 reads=['OT', 'OTs', 'WO%d' % (h // hh)], writes=[y1k])
                        ln_stage(col, n, [(y0, y0k), (y1, y1k)], xres, li, gk, xres, not first)
                        first = False
                T.barrier()

        def fox_proj(cT):
            xT, ld_f, ld_b, Gt, Bt, zt, xo, junk, st_ = WKH.xT, WKH.ld_f, WKH.ld_b, WKH.Gt, WKH.Bt, WKH.zt, WKH.xo, WKH.junk, WKH.st_
            with contextlib.ExitStack() as ls:
                lsb = lambda name, shape, dtype=F32: ls.enter_context(nc.sbuf_tensor(un(name), list(shape), dtype))
                WQ = lsb("WQ", [128, 8, D], BF); WK = lsb("WK", [128, 8, D], BF); WV = lsb("WV", [128, 8, D], BF)
                WF = lsb("WF", [128, 8, H], BF)
                QS = lsb("QS", [64, 16, 512], BF); KS = lsb("KS", [64, 16, 512], BF)
                ktm = [lsb("ktm%d" % i, [128, D]) for i in range(2)]
                vtm = [lsb("vtm%d" % i, [128, D]) for i in range(2)]
                qtm = lsb("qtm", [128, D])
                vau = [lsb("vau%d" % i, [128, 16, 65], BF) for i in range(2)]
                bfB = lsb("bfB", [128, H]); nbf = lsb("nbf", [16, 1]); bfc = lsb("bfc", [16, 1])
                lft = [lsb("lft%d" % i, [128, H]) for i in range(2)]
                lfT = lsb("lfT", [16, 512])
                wv = fox_w_in.rearrange("(c p) f -> p c f", p=128)
                for i, (W, nm) in enumerate([(WQ, 'WQ'), (WK, 'WK'), (WV, 'WV')]):
                    for hf in range(2):
                        T.dma('pool', W[:, :, hf * 512:(hf + 1) * 512], wv[:, :, i * D + hf * 512: i * D + (hf + 1) * 512],
                              writes=['%s%d' % (nm, hf)], semkey='%s%d' % (nm, hf))
                T.dma('pool', WF[:], wv[:, :, 3 * D:3 * D + H], writes=['WF'], semkey='WF')
                T.dma('sp', bfB[:], fox_b_f[0:1, :].partition_broadcast(128), writes=['bfB'], semkey='bfB')
                T.dma('sp', bfc[:], fox_b_f.rearrange("o h -> h o"), writes=['bfc'], semkey='bfc')
                T.op('dve', lambda e: e.tensor_scalar(out=nbf[:], in0=bfc[:], scalar1=-1.0, scalar2=None, op0=ALU.mult),
                     reads=['bfc'], writes=['nbf'])
                for i in range(2):
                    T.op('dve', lambda e: e.memset(vau[i][:], 1.0), writes=['vau%d' % i])
                k = 0
                for pas in PASSES:
                    c0 = pas['c0']
                    load_xT(pas, xres)
                    for (gc, gn) in pas['groups']:
                        if gn != 512:
                            continue
                        lc = gc - c0
                        for (W, nm, ST, scale) in [(WQ, 'WQ', QS, 0.125), (WK, 'WK', KS, 1.0)]:
                            for h in range(16):
                                _, pp, ppk = psum()
                                for dc in range(8):
                                    T.op('pe', lambda e: e.matmul(pp[:64, :], lhsT=W[:, dc, h * 64:(h + 1) * 64], rhs=xT[:, dc, lc:lc + 512],
                                                                  start=(dc == 0), stop=(dc == 7)),
                                         reads=['%s%d' % (nm, h // 8), 'xT'], writes=[ppk])
                                T.op('act', lambda e: e.activation(out=ST[:, h, :], in_=pp[:64, :], func=AF.Copy, scale=scale),
                                     reads=[ppk], writes=[nm + 'S'])
                        T.dma('sp', qT_d.rearrange("h d t -> d h t")[:, :, gc:gc + 512], QS[:], reads=['WQS'], writes=['d:qT'], semkey='QS_st')
                        for i8 in range(8):
                            T.dma('sp', ktloc_c[i8].rearrange("(h r) t -> r h t", r=70)[0:64, :, gc:gc + 512], KS[:, 2 * i8:2 * i8 + 2, :], reads=['WKS'],
                                  writes=['d:ktloc'], semkey='KS_st')
                        _, pf, pfk = psum()
                        for dc in range(8):
                            T.op('pe', lambda e: e.matmul(pf[:16, :], lhsT=WF[:, dc, :], rhs=xT[:, dc, lc:lc + 512], start=(dc == 0), stop=(dc == 7)),
                                 reads=['WF', 'xT'], writes=[pfk])
                        T.op('act', lambda e: e.activation(out=lfT[:], in_=pf[:16, :], func=AF.Exp, bias=nbf[:], scale=-1.0),
                             reads=[pfk, 'nbf'], writes=['lfT'])
                        T.op('act', lambda e: e.activation(out=lfT[:], in_=lfT[:], func=AF.Ln, bias=1.0, scale=1.0), reads=['lfT'], writes=['lfT'])
                        T.op('dve', lambda e: e.tensor_scalar(out=lfT[:], in0=lfT[:], scalar1=-1.0, scalar2=None, op0=ALU.mult),
                             reads=['lfT'], writes=['lfT'])
                        init = 0.0 if gc == 0 else cT[:, gc - 1:gc]
                        T.op('dve', lambda e: e.tensor_tensor_scan(out=cT[:, gc:gc + 512], data0=onesf[:16, :], data1=lfT[:], initial=init,
                                                                   op0=ALU.mult, op1=ALU.add), reads=['lfT', 'onesf', 'cT'], writes=['cT'])
                    for (col, n) in pas['subs']:
                        lc = col - c0
                        s = k % 2
                        k += 1
                        for (W, nm, tm) in [(WK, 'WK', ktm[s]), (WV, 'WV', vtm[s])] + ([(WQ, 'WQ', qtm)] if n == 4 else []):
                            key = tm.name if hasattr(tm, 'name') else nm
                            for hf in range(2):
                                _, pp, ppk = psum()
                                for dc in range(8):
                                    T.op('pe', lambda e: e.matmul(pp[:n, :], lhsT=xT[:, dc, lc:lc + n], rhs=W[:, dc, hf * 512:(hf + 1) * 512],
                                                                  start=(dc == 0), stop=(dc == 7)),
                                         reads=['%s%d' % (nm, hf), 'xT'], writes=[ppk])
                                T.op('act', lambda e: e.copy(out=tm[:n, hf * 512:(hf + 1) * 512], in_=pp[:n, :]), reads=[ppk],
                                     writes=['tm_%s_%d' % (nm, s)])
                        T.dma('sp', o_k[col:col + n, :], ktm[s][:n, :], reads=['tm_WK_%d' % s], semkey='ktm%d_st' % s)
                        T.dma('sp', o_v[col:col + n, :], vtm[s][:n, :], reads=['tm_WV_%d' % s], semkey='vtm%d_st' % s)
                        if n == 128:
                            T.op('dve', lambda e: e.tensor_copy(out=vau[s][:, :, 0:64], in_=vtm[s][:, :].rearrange("p (h d) -> p h d", h=16)),
                                 reads=['tm_WV_%d' % s], writes=['vau%d' % s])
                            for i8 in range(8):
                                T.dma('sp', vloc_c[i8].rearrange("a (b c) -> (a b) c", c=65).rearrange("(h t) c -> t h c", h=2)[col:col + 128, :, :],
                                      vau[s][:, 2 * i8:2 * i8 + 2, :], reads=['vau%d' % s], writes=['d:vloc'], semkey='vau%d_st' % s)
                        else:
                            T.dma('sp', srow_d[:, 0:D], qtm[:n, :], reads=['tm_WQ_%d' % s], writes=['d:srow'], semkey='qtm_st')
                            T.dma('sp', srow_d[:, D:2 * D], ktm[s][:n, :], reads=['tm_WK_%d' % s], writes=['d:srow'], semkey='ktm%d_st2' % s)
                            T.dma('sp', srow_d[:, 2 * D:3 * D], vtm[s][:n, :], reads=['tm_WV_%d' % s], writes=['d:srow'], semkey='vtm%d_st2' % s)
                        _, pf, pfk = psum()
                        for dc in range(8):
                            T.op('pe', lambda e: e.matmul(pf[:n, :16], lhsT=xT[:, dc, lc:lc + n], rhs=WF[:, dc, :], start=(dc == 0), stop=(dc == 7)),
                                 reads=['WF', 'xT'], writes=[pfk])
                        lf = lft[s]
                        T.op('dve', lambda e: e.tensor_tensor(out=lf[:n, :], in0=pf[:n, :16], in1=bfB[:n, :], op=ALU.add),
                             reads=[pfk, 'bfB'], writes=['lft%d' % s])
                        T.op('act', lambda e: e.activation(out=lf[:n, :], in_=lf[:n, :], func=AF.Exp, scale=-1.0), reads=['lft%d' % s], writes=['lft%d' % s])
                        T.op('act', lambda e: e.activation(out=lf[:n, :], in_=lf[:n, :], func=AF.Ln, bias=1.0, scale=1.0), reads=['lft%d' % s], writes=['lft%d' % s])
                        T.op('dve', lambda e: e.tensor_scalar(out=lf[:n, :], in0=lf[:n, :], scalar1=-1.0, scalar2=None, op0=ALU.mult),
                             reads=['lft%d' % s], writes=['lft%d' % s])
                        T.dma('sp', o_lf[col:col + n, :], lf[:n, :], reads=['lft%d' % s], semkey='lft%d_st' % s)
                        if n == 4:
                            T.dma('sp', srow_d[:, 3 * D:3 * D + H], lf[:n, :], reads=['lft%d' % s], writes=['d:srow'], semkey='lft%d_st2' % s)
                T.barrier()


        GROUPS4 = [[0, 1, 2, 3], [4, 5, 6, 7]]

        def allgather(src, dst, rk, wk, name):
            T.dma('pool', None, None, reads=[rk], writes=[wk], semkey=name, inc=1,
                  fn=lambda e: e.collective_compute("AllGather", ALU.bypass, replica_groups=GROUPS4, ins=[src], outs=[dst]))

        def fox_attn(cT, OT):
            with contextlib.ExitStack() as ls:
                lsb = lambda name, shape, dtype=F32: ls.enter_context(nc.sbuf_tensor(un(name), list(shape), dtype))
                pre = contextlib.ExitStack()
                psb = lambda name, shape, dtype=F32: pre.enter_context(nc.sbuf_tensor(un(name), list(shape), dtype))
                wk_ = psb("wk_", [16, NP_]); r1 = psb("r1", [16, NP_])
                hi = psb("hi", [16, NP_], BF); mid = psb("mid", [16, NP_], BF); lo = psb("lo", [16, NP_], BF)
                ones3 = psb("ones3", [16, 3, NP_], BF)
                TG = psb("TG", [16, 4]); Dp = psb("Dp", [16, 4]); tmp4 = psb("tmp4", [16, 4])
                T.op('dve', lambda e: e.memset(ones3[:], 1.0), writes=['ones3'])
                fox_sample_setup(psb)
                PSN[0] = 4

                def split3(srckey):
                    T.op('dve', lambda e: e.tensor_copy(out=hi[:], in_=wk_[:]), reads=[srckey], writes=['hi'])
                    T.op('dve', lambda e: e.tensor_tensor(out=r1[:], in0=wk_[:], in1=hi[:], op=ALU.subtract), reads=[srckey, 'hi'], writes=['r1'])
                    T.op('dve', lambda e: e.tensor_copy(out=mid[:], in_=r1[:]), reads=['r1'], writes=['mid'])
                    T.op('dve', lambda e: e.tensor_tensor(out=r1[:], in0=r1[:], in1=mid[:], op=ALU.subtract), reads=['r1', 'mid'], writes=['r1'])
                    T.op('dve', lambda e: e.tensor_copy(out=lo[:], in_=r1[:]), reads=['r1'], writes=['lo'])

                T.op('dve', lambda e: e.tensor_scalar(out=wk_[:], in0=cT[:], scalar1=-1.0, scalar2=None, op0=ALU.mult), reads=['cT'], writes=['wk_'])
                split3('wk_')
                for i8 in range(8):
                    ktv = ktloc_c[i8].rearrange("(h r) t -> h r t", r=70)
                    hs = slice(2 * i8, 2 * i8 + 2)
                    T.dma('sp', ktv[:, 64:67, :], ones3[hs], reads=['ones3'], writes=['d:kt_a'], semkey='ones3_st')
                    T.dma('sp', ktv[:, 67, :], hi[hs], reads=['hi'], writes=['d:kt_b'], semkey='hi_st')
                    T.dma('sp', ktv[:, 68, :], mid[hs], reads=['mid'], writes=['d:kt_c'], semkey='mid_st')
                    T.dma('sp', ktv[:, 69, :], lo[hs], reads=['lo'], writes=['d:kt_d'], semkey='lo_st')
                with nc.allow_non_contiguous_dma(reason="tiny"):
                    T.dma('sp', totloc.rearrange("o h -> h o"), cT[:, NP_ - 1:NP_], reads=['cT'], writes=['d:tot'], semkey='tot_st')
                T.barrier()
                chk(2)
                allgather(totloc_f, totg_f, 'd:tot', 'd:totg', 'cc_tot')
                for i8 in range(8):
                    allgather(ktloc_c[i8], ktg_c[i8], 'd:kt_a', 'd:ktg%d' % i8, 'cc_kt%d' % i8)
                    allgather(vloc_c[i8], vg_c[i8], 'd:kt_a', 'd:vg%d' % i8, 'cc_v%d' % i8)
                with nc.allow_non_contiguous_dma(reason="tiny"):
                    T.dma('sp', TG[:], totg.rearrange("r h -> h r"), reads=['d:totg'], writes=['TG'], semkey='TG')
                for rp in range(4):
                    T.op('dve', lambda e: e.tensor_tensor(out=tmp4[:], in0=TG[:], in1=crank[:16, rp * 4:rp * 4 + 4], op=ALU.mult),
                         reads=['TG', 'crank'], writes=['tmp4'])
                    T.op('dve', lambda e: e.tensor_reduce(out=Dp[:, rp:rp + 1], in_=tmp4[:], axis=AX.X, op=ALU.add), reads=['tmp4'], writes=['Dp'])
                T.op('dve', lambda e: e.tensor_tensor(out=Dp[:], in0=Dp[:], in1=crank[:16, 16:20], op=ALU.add), reads=['Dp', 'crank'], writes=['Dp'])
                for v in range(5):
                    if v == 0:
                        T.op('dve', lambda e: e.tensor_copy(out=wk_[:], in_=cT[:]), reads=['cT'], writes=['wk_'])
                    else:
                        T.op('dve', lambda e: e.tensor_scalar(out=wk_[:], in0=cT[:], scalar1=Dp[:, v - 1:v], scalar2=None, op0=ALU.add),
                             reads=['cT', 'Dp'], writes=['wk_'])
                    split3('wk_')
                    T.dma('sp', qaug_d[v, :, 0, :], hi[:], reads=['hi'], writes=['d:qa%d' % v], semkey='hi_st')
                    T.dma('sp', qaug_d[v, :, 1, :], mid[:], reads=['mid'], writes=['d:qb%d' % v], semkey='mid_st')
                    T.dma('sp', qaug_d[v, :, 2, :], lo[:], reads=['lo'], writes=['d:qc%d' % v], semkey='lo_st')
                    T.dma('sp', qaug_d[v, :, 3:6, :], ones3[:], reads=['ones3'], writes=['d:qd%d' % v], semkey='ones3_st')
                T.barrier(keep='cc_')
                chk(3)
                pre.close()
                QA = lsb("QA", [70, 5, NP_], BF); KL = lsb("KL", [70, NP_], BF); KG = lsb("KG", [70, 4, NP_], BF)
                VL = lsb("VL", [128, 16, 65], BF); VG = lsb("VG", [128, 4, 16, 65], BF)
                PT = [lsb("PT%d" % i, [128, 512], BF) for i in range(3)]
                rdt = lsb("rdt", [65, 512]); bcs = lsb("bcs", [64, 512])
                sgen = fox_sample_gen(OT, lsb)
                next(sgen, None)
                next(sgen, None)
                ktgv = [ktg_c[i8].rearrange("(g h r) t -> h r g t", g=4, h=2) for i8 in range(8)]
                vlv = [vloc_c[i8].rearrange("a (b c) -> (a b) c", c=65).rearrange("(h kb p) c -> h p kb c", h=2, p=128) for i8 in range(8)]
                vgv = [vg_c[i8].rearrange("a (b c) -> (a b) c", c=65).rearrange("(g h p j) c -> h p g j c", g=4, h=2, p=128) for i8 in range(8)]
                KGv = KG[:].rearrange("r g (p j) -> r g j p", j=16)
                pk_ = 0
                for h in range(16):
                    T.dma('sp', QA[0:64, :, :], bass.AP(qT_d.tensor, h * 64 * NP_, [[NP_, 64], [0, 5], [1, NP_]]), writes=['QAq'], semkey='QAq')
                    T.dma('sp', QA[64:70, :, :], qaug_d[:, h, :, :].rearrange("v r t -> r v t"), writes=['QAa'], semkey='QAa')
                    T.dma('sp', KL[:], ktloc_c[h // 2][(h % 2) * 70:(h % 2 + 1) * 70, :], writes=['KL'], semkey='KL')
                    T.dma('sp', VL[:], vlv[h // 2][h % 2], writes=['VL'], semkey='VL')
                    T.dma('sp', KG[:], ktgv[h // 2][h % 2], reads=['d:ktg%d' % (h // 2)], writes=['KG'], semkey='KG')
                    T.dma('sp', VG[:], vgv[h // 2][h % 2], reads=['d:vg%d' % (h // 2)], writes=['VG'], semkey='VG')
                    items = []
                    for qt in range(4):
                        blocks = [('L', kb, 0) for kb in range(4 * qt + 4)] + [('G', j, g) for g in range(4) for j in range(16)]
                        for bi, (kind, kb, g) in enumerate(blocks):
                            items.append((qt, kind, kb, g, bi == 0, bi == len(blocks) - 1))
                    st = {}

                    def stage_a(i):
                        (qt, kind, kb, g, first, last) = items[i]
                        qc = qt * 512
                        _, S, Sk = psum()
                        p = i % 3
                        if kind == 'L':
                            off = max(0, kb * 128 - qc)
                            n = 512 - off
                            dg = kb >= 4 * qt
                            T.op('pe', lambda e: e.matmul(S[:, :n], lhsT=KL[:, kb * 128:(kb + 1) * 128], rhs=QA[:, 0, qc + off:qc + 512],
                                                          start=True, stop=not dg), reads=['KL', 'QAq', 'QAa'], writes=[Sk])
                            if dg:
                                T.op('pe', lambda e: e.matmul(S[:, 0:128], lhsT=identb[:, :], rhs=mbb[:, :], start=False, stop=True),
                                     reads=['identb', 'mbb'], writes=[Sk])
                        else:
                            off = 0
                            n = 512
                            T.op('pe', lambda e: e.matmul(S[:, :n], lhsT=KGv[:, g, kb, :], rhs=QA[:, 1 + g, qc:qc + 512],
                                                          start=True, stop=True), reads=['KG', 'QAq', 'QAa'], writes=[Sk])
                        T.op('act', lambda e: e.activation(out=PT[p][:, :n], in_=S[:, :n], func=AF.Exp), reads=[Sk], writes=['PT%d' % p])
                        st[i] = (p, off, n)

                    def stage_b(i):
                        (qt, kind, kb, g, first, last) = items[i]
                        qc = qt * 512
                        (p, off, n) = st.pop(i)
                        _, O, Ok = psum_r(6 + (qt % 2))
                        lhsV = VL[:, kb, :] if kind == 'L' else VG[:, g, kb, :]
                        vkey = 'VL' if kind == 'L' else 'VG'
                        T.op('pe', lambda e: e.matmul(O[0:65, off:512], lhsT=lhsV, rhs=PT[p][:, :n], start=first, stop=last),
                             reads=[vkey, 'PT%d' % p], writes=[Ok])
                        if last:
                            T.op('dve', lambda e: e.reciprocal(out=rdt[64:65, :], in_=O[64:65, :]), reads=[Ok], writes=['rdt'])
                            _, bc, bck = psum()
                            T.op('pe', lambda e: e.matmul(bc[0:64, :], lhsT=onesf[64:65, 0:64], rhs=rdt[64:65, :], start=True, stop=True),
                                 reads=['rdt', 'onesf'], writes=[bck])
                            T.op('act', lambda e: e.copy(out=bcs[:], in_=bc[0:64, :]), reads=[bck], writes=['bcs'])
                            T.op('dve', lambda e: e.tensor_tensor(out=OT[:, h, qc:qc + 512], in0=O[0:64, :], in1=bcs[:], op=ALU.mult),
                                 reads=[Ok, 'bcs'], writes=['OT'])

                    LAH = 2
                    for i in range(len(items) + LAH):
                        if i < len(items):
                            stage_a(i)
                        if i >= LAH:
                            stage_b(i - LAH)
                        if i % 36 == 35:
                            next(sgen, None)
                for _ in sgen:
                    pass
                T.barrier()
                PSN[0] = 6

        def fox_sample_setup(psb):
            q4 = psb("q4", [4, D]); k4 = psb("k4", [4, D]); p4 = psb("p4", [4, D]); sn = psb("sn", [4, H]); pn = psb("pn", [4, H])
            T.dma('sp', q4[:], srow_d[:, 0:D], reads=['d:srow'], writes=['q4'], semkey='q4')
            T.dma('sp', k4[:], srow_d[:, D:2 * D], reads=['d:srow'], writes=['k4'], semkey='k4')
            T.op('dve', lambda e: e.tensor_tensor(out=p4[:], in0=q4[:], in1=k4[:], op=ALU.mult), reads=['q4', 'k4'], writes=['p4'])
            T.op('dve', lambda e: e.tensor_reduce(out=sn[:], in_=p4[:].rearrange("p (h d) -> p h d", h=16), axis=AX.X, op=ALU.add),
                 reads=['p4'], writes=['sn'])
            T.op('act', lambda e: e.activation(out=pn[:], in_=sn[:], func=AF.Exp, scale=0.125), reads=['sn'], writes=['pn'])
            T.dma('sp', srow_d[:, 3 * D + H:3 * D + 2 * H], pn[:], reads=['pn'], writes=['d:srow2'], semkey='pn_st')

        def fox_sample_gen(OT, lsb):
            RL = 3 * D + 2 * H
            NVB = 8
            rows = lsb("rows", [1, NS, 2 * H]); rowsb = lsb("rowsb", [1, NS, D], BF); pnb = lsb("pnb", [1, NS, H], BF)
            idx = lsb("idx", [128, 1], I32); idxf = lsb("idxf", [128, 1]); idxall = lsb("idxall", [128, 32], I32); idx2 = lsb("idx2", [128, 1], I32)
            LF = lsb("LF", [128, 64, H]); Bi = lsb("Bi", [128, 64, H]); Sc = lsb("Sc", [128, 64, H]); Pm = lsb("Pm", [128, 64, H], BF)
            KB = [lsb("KB%d" % i, [128, 2, D], BF) for i in range(2)]; VB = [lsb("VB%d" % i, [128, 2, D], BF) for i in range(NVB)]
            prod = lsb("prod", [128, 2, D], BF); qB = lsb("qB", [128, 2, D], BF)
            Tt = lsb("Tt", [128, H]); At = lsb("At", [128, H]); dpart = lsb("dpart", [128, H]); rden = lsb("rden", [16, 1])
            On = lsb("On", [16, D], BF)
            T.dma('sp', rows[:], bass.AP(srow_d.tensor, 3 * D, [[0, 1], [RL, NS], [1, 2 * H]]), reads=['d:srow2', 'd:srow'], writes=['rows'], semkey='rows')
            T.dma('pool', rowsb[:], bass.AP(srow_d.tensor, 2 * D, [[0, 1], [RL, NS], [1, D]]), reads=['d:srow'], writes=['rowsb'], semkey='rowsb')
            T.op('dve', lambda e: e.tensor_copy(out=pnb[:], in_=rows[:, :, H:2 * H]), reads=['rows'], writes=['pnb'])

            def vgather(j):
                b = j % NVB
                T.dma('pool', None, None, reads=['idxall'], writes=['VB%d' % b], semkey='VB%d' % b,
                      fn=lambda e: e.indirect_dma_start(out=VB[b][:].rearrange("p t d -> p (t d)"), out_offset=None, in_=cv,
                                                        in_offset=bass.IndirectOffsetOnAxis(ap=idxall[:, j:j + 1], axis=0)))

            for s_ in range(NS):
                with nc.allow_non_contiguous_dma(reason="tiny"):
                    T.dma('sp', idx[:], bass.AP(ptab.tensor, s_ * 64, [[1, 64], [0, 2], [1, 1]]), writes=['idx'], semkey='idx')
                T.op('dve', lambda e: e.tensor_scalar(out=idx2[:], in0=idx[:], scalar1=2.0, scalar2=crank[:, 40:41], op0=ALU.mult, op1=ALU.add),
                     reads=['idx', 'crank'], writes=['idx2'])
                T.op('dve', lambda e: e.tensor_scalar(out=idxf[:], in0=idx2[:], scalar1=32.0, scalar2=None, op0=ALU.mult), reads=['idx2'], writes=['idxf'])
                T.op('dve', lambda e: e.tensor_scalar(out=idxall[:], in0=crank[:, 48:80], scalar1=idxf[:, 0:1], scalar2=None, op0=ALU.add),
                     reads=['idxf', 'crank'], writes=['idxall'])
                T.dma('pool', None, None, reads=['idx2'], writes=['LF'], semkey='LF',
                      fn=lambda e: e.indirect_dma_start(out=LF[:].rearrange("p t h -> p (t h)"), out_offset=None, in_=clf,
                                                        in_offset=bass.IndirectOffsetOnAxis(ap=idx2[:, 0:1], axis=0)))
                for t in range(2):
                    T.dma('pool', qB[:, t, :], bass.AP(srow_d.tensor, s_ * RL, [[0, 128], [1, D]]), reads=['d:srow'], writes=['qB%d' % t], semkey='qB%d' % t)
                for j in range(32):
                    b = j % 2
                    T.dma('pool', None, None, reads=['idxall'], writes=['KB%d' % b], semkey='KB%d' % b,
                          fn=lambda e: e.indirect_dma_start(out=KB[b][:].rearrange("p t d -> p (t d)"), out_offset=None, in_=ck,
                                                            in_offset=bass.IndirectOffsetOnAxis(ap=idxall[:, j:j + 1], axis=0)))
                    T.op('dve', lambda e: e.tensor_tensor(out=prod[:], in0=KB[b][:], in1=qB[:], op=ALU.mult),
                         reads=['KB%d' % b, 'qB0', 'qB1'], writes=['prod'])
                    T.op('dve', lambda e: e.tensor_reduce(out=Sc[:, 2 * j:2 * j + 2, :].rearrange("p t h -> p (t h)"),
                                                          in_=prod[:].rearrange("p t (h d) -> p (t h) d", h=16), axis=AX.X, op=ALU.add),
                         reads=['prod'], writes=['Sc'])
                    if j % 2 == 1:
                        yield
                T.op('dve', lambda e: e.tensor_reduce(out=Tt[:], in_=LF[:].rearrange("p t h -> p h t"), axis=AX.X, op=ALU.add), reads=['LF'], writes=['Tt'])
                yield
                for j in range(NVB):
                    vgather(j)
                yield
                yield
                _, lp, lpk = psum()
                T.op('pe', lambda e: e.matmul(lp[:, 0:H], lhsT=slowf[:], rhs=Tt[:], start=True, stop=False), reads=['slowf', 'Tt'], writes=[lpk])
                T.op('pe', lambda e: e.matmul(lp[:, 0:H], lhsT=onesf[0:1, 0:128], rhs=rows[0:1, s_, 0:H], start=False, stop=True),
                     reads=['onesf', 'rows'], writes=[lpk])
                T.op('dve', lambda e: e.tensor_tensor(out=At[:], in0=lp[:, 0:H], in1=Tt[:], op=ALU.add), reads=[lpk, 'Tt'], writes=['At'])
                for h in range(16):
                    T.op('dve', lambda e: e.tensor_tensor_scan(out=Bi[:, :, h], data0=onesf[:, 0:64], data1=LF[:, :, h], initial=At[:, h:h + 1],
                                                               op0=ALU.mult, op1=ALU.subtract), reads=['LF', 'At', 'onesf'], writes=['Bi'])
                T.op('dve', lambda e: e.scalar_tensor_tensor(out=Sc[:].rearrange("p t h -> p (t h)"), in0=Sc[:].rearrange("p t h -> p (t h)"),
                                                             scalar=0.125, in1=Bi[:].rearrange("p t h -> p (t h)"), op0=ALU.mult, op1=ALU.add),
                     reads=['Sc', 'Bi'], writes=['Sc'])
                T.op('act', lambda e: e.activation(out=Pm[:].rearrange("p t h -> p (t h)"), in_=Sc[:].rearrange("p t h -> p (t h)"), func=AF.Exp),
                     reads=['Sc'], writes=['Pm'])
                T.op('dve', lambda e: e.tensor_reduce(out=dpart[:], in_=Pm[:].rearrange("p t h -> p h t"), axis=AX.X, op=ALU.add), reads=['Pm'], writes=['dpart'])
                yield
                yield
                _, dn, dnk = psum()
                T.op('pe', lambda e: e.matmul(dn[0:16, 0:1], lhsT=dpart[:], rhs=onesf[:, 0:1], start=True, stop=False), reads=['dpart', 'onesf'], writes=[dnk])
                T.op('pe', lambda e: e.matmul(dn[0:16, 0:1], lhsT=rows[0:1, s_, H:2 * H], rhs=onesf[0:1, 0:1], start=False, stop=True),
                     reads=['rows', 'onesf'], writes=[dnk])
                T.op('dve', lambda e: e.reciprocal(out=rden[:], in_=dn[0:16, 0:1]), reads=[dnk], writes=['rden'])
                _, O0, O0k = psum_r(4)
                _, O1, O1k = psum_r(5)
                for q in range(4):
                    for j in range(q * NVB, (q + 1) * NVB):
                        b = j % NVB
                        for t in range(2):
                            pos = 2 * j + t
                            T.op('pe', lambda e: e.matmul(O0[0:16, :], lhsT=Pm[:, pos, :], rhs=VB[b][:, t, 0:512], start=(pos == 0), stop=False),
                                 reads=['Pm', 'VB%d' % b], writes=[O0k])
                            T.op('pe', lambda e: e.matmul(O1[0:16, :], lhsT=Pm[:, pos, :], rhs=VB[b][:, t, 512:1024], start=(pos == 0), stop=False),
                                 reads=['Pm', 'VB%d' % b], writes=[O1k])
                    if q < 3:
                        for j in range((q + 1) * NVB, (q + 2) * NVB):
                            vgather(j)
                        yield
                        yield
                T.op('pe', lambda e: e.matmul(O0[0:16, :], lhsT=pnb[0:1, s_, :], rhs=rowsb[0:1, s_, 0:512], start=False, stop=True),
                     reads=['pnb', 'rowsb'], writes=[O0k])
                T.op('pe', lambda e: e.matmul(O1[0:16, :], lhsT=pnb[0:1, s_, :], rhs=rowsb[0:1, s_, 512:1024], start=False, stop=True),
                     reads=['pnb', 'rowsb'], writes=[O1k])
                T.op('dve', lambda e: e.tensor_scalar(out=On[:, 0:512], in0=O0[0:16, :], scalar1=rden[:, 0:1], scalar2=None, op0=ALU.mult),
                     reads=[O0k, 'rden'], writes=['On'])
                T.op('dve', lambda e: e.tensor_scalar(out=On[:, 512:1024], in0=O1[0:16, :], scalar1=rden[:, 0:1], scalar2=None, op0=ALU.mult),
                     reads=[O1k, 'rden'], writes=['On'])
                _, Z, Zk = psum()
                for h in range(16):
                    T.op('pe', lambda e: e.matmul(Z[0:64, h:h + 1], lhsT=On[:, h * 64:(h + 1) * 64], rhs=identb[0:16, h:h + 1], start=True, stop=True),
                         reads=['On', 'identb'], writes=[Zk])
                T.op('dve', lambda e: e.tensor_copy(out=OT[:, :, NP_ + s_], in_=Z[0:64, 0:16]), reads=[Zk], writes=['OTs'])
                yield

        gqT = dint("gqT", [GH, 128, NT], BF)
        gkT = dint("gkT", [GH, 128, NT], BF)
        glaT = dint("glaT", [GH, 128, NT])
        grT = dint("grT", [8, 128, NT], BF)
        gv = dint("gv", [NP_, D], BF)
        gsv = dint("gsv", [NS, D + 512])

        def gla_proj():
            xT, ld_f, ld_b, Gt, Bt, zt, xo, junk, st_ = WKH.xT, WKH.ld_f, WKH.ld_b, WKH.Gt, WKH.Bt, WKH.zt, WKH.xo, WKH.junk, WKH.st_
            with contextlib.ExitStack() as ls:
                lsb = lambda name, shape, dtype=F32: ls.enter_context(nc.sbuf_tensor(un(name), list(shape), dtype))
                WGQ = lsb("WGQ", [128, 8, 512], BF); WGK = lsb("WGK", [128, 8, 512], BF)
                WGV = lsb("WGV", [128, 8, D], BF); WGR = lsb("WGR", [128, 8, D], BF); WGA = lsb("WGA", [128, 8, 16], BF)
                WA2 = lsb("WA2", [16, 512], BF); nba = lsb("nba", [128, 4]); bac = lsb("bac", [128, 4])
                SQ = lsb("SQ", [128, 4, 512], BF); SK = lsb("SK", [128, 4, 512], BF); SL = lsb("SL", [128, 4, 512]); SR = lsb("SR", [128, 8, 512], BF)
                alr = lsb("alr", [16, 512], BF)
                vt = [lsb("vt%d" % i, [128, D], BF) for i in range(2)]
                vs_ = lsb("vs_", [4, D + 512])
                wv = gla_w_in.rearrange("(c p) f -> p c f", p=128)
                T.dma('pool', WGQ[:], wv[:, :, 0:512], writes=['WGQ'], semkey='WGQ')
                T.dma('pool', WGK[:], wv[:, :, 512:1024], writes=['WGK'], semkey='WGK')
                for hf in range(2):
                    T.dma('pool', WGV[:, :, hf * 512:(hf + 1) * 512], wv[:, :, 1024 + hf * 512:1024 + (hf + 1) * 512], writes=['WGV%d' % hf], semkey='WGV%d' % hf)
                    T.dma('pool', WGR[:, :, hf * 512:(hf + 1) * 512], wv[:, :, 2048 + hf * 512:2048 + (hf + 1) * 512], writes=['WGR%d' % hf], semkey='WGR%d' % hf)
                T.dma('pool', WGA[:], wv[:, :, 3072:3088], writes=['WGA'], semkey='WGA')
                T.dma('pool', WA2[:], gla_w_a2, writes=['WA2'], semkey='WA2')
                with nc.allow_non_contiguous_dma(reason="tiny"):
                    T.dma('sp', bac[:], gla_b_a.rearrange("o (h p) -> p (o h)", p=128), writes=['bac'], semkey='bac')
                T.op('dve', lambda e: e.tensor_scalar(out=nba[:], in0=bac[:], scalar1=-1.0, scalar2=None, op0=ALU.mult), reads=['bac'], writes=['nba'])
                k = 0
                for pas in PASSES:
                    c0 = pas['c0']
                    load_xT(pas, xres)
                    for (gc, gn) in pas['groups']:
                        lc = gc - c0
                        for (W, nm, ST, scale) in [(WGQ, 'WGQ', SQ, 128.0 ** -0.5), (WGK, 'WGK', SK, 1.0)]:
                            for h in range(4):
                                _, pp, ppk = psum()
                                for dc in range(8):
                                    T.op('pe', lambda e: e.matmul(pp[:, :gn], lhsT=W[:, dc, h * 128:(h + 1) * 128], rhs=xT[:, dc, lc:lc + gn],
                                                                  start=(dc == 0), stop=(dc == 7)), reads=[nm, 'xT'], writes=[ppk])
                                T.op('act', lambda e: e.activation(out=ST[:, h, :gn], in_=pp[:, :gn], func=AF.Copy, scale=scale), reads=[ppk], writes=[nm + 'S'])
                        T.dma('sp', gqT.rearrange("h p t -> p h t")[:, :, gc:gc + gn], SQ[:, :, :gn], reads=['WGQS'], writes=['d:gq'], semkey='SQ_st')
                        T.dma('sp', gkT.rearrange("h p t -> p h t")[:, :, gc:gc + gn], SK[:, :, :gn], reads=['WGKS'], writes=['d:gk'], semkey='SK_st')
                        _, pa, pak = psum()
                        for dc in range(8):
                            T.op('pe', lambda e: e.matmul(pa[:16, :gn], lhsT=WGA[:, dc, :], rhs=xT[:, dc, lc:lc + gn], start=(dc == 0), stop=(dc == 7)),
                                 reads=['WGA', 'xT'], writes=[pak])
                        T.op('act', lambda e: e.copy(out=alr[:, :gn], in_=pa[:16, :gn]), reads=[pak], writes=['alr'])
                        for h in range(4):
                            _, pp, ppk = psum()
                            T.op('pe', lambda e: e.matmul(pp[:, :gn], lhsT=WA2[:, h * 128:(h + 1) * 128], rhs=alr[:, :gn], start=True, stop=True),
                                 reads=['WA2', 'alr'], writes=[ppk])
                            T.op('act', lambda e: e.activation(out=SL[:, h, :gn], in_=pp[:, :gn], func=AF.Exp, bias=nba[:, h:h + 1], scale=-1.0),
                                 reads=[ppk, 'nba'], writes=['SL'])
                        T.op('act', lambda e: e.activation(out=SL[:, :, :gn], in_=SL[:, :, :gn], func=AF.Ln, bias=1.0, scale=1.0), reads=['SL'], writes=['SL'])
                        T.op('dve', lambda e: e.tensor_scalar(out=SL[:, :, :gn], in0=SL[:, :, :gn], scalar1=-1.0 / 16.0, scalar2=None, op0=ALU.mult),
                             reads=['SL'], writes=['SL'])
                        T.dma('sp', glaT.rearrange("h p t -> p h t")[:, :, gc:gc + gn], SL[:, :, :gn], reads=['SL'], writes=['d:gla'], semkey='SL_st')
                        for c in range(8):
                            _, pp, ppk = psum()
                            for dc in range(8):
                                T.op('pe', lambda e: e.matmul(pp[:, :gn], lhsT=WGR[:, dc, c * 128:(c + 1) * 128], rhs=xT[:, dc, lc:lc + gn],
                                                              start=(dc == 0), stop=(dc == 7)), reads=['WGR%d' % (c // 4), 'xT'], writes=[ppk])
                            T.op('act', lambda e: e.activation(out=SR[:, c, :gn], in_=pp[:, :gn], func=AF.Silu), reads=[ppk], writes=['SR'])
                        T.dma('sp', grT.rearrange("c p t -> p c t")[:, :, gc:gc + gn], SR[:, :, :gn], reads=['SR'], writes=['d:gr'], semkey='SR_st')
                    for (col, n) in pas['subs']:
                        lc = col - c0
                        s_ = k % 2
                        k += 1
                        dstt = vt[s_] if n == 128 else vs_
                        dk_ = ('vt%d' % s_) if n == 128 else 'vs_'
                        for hf in range(2):
                            _, pp, ppk = psum()
                            for dc in range(8):
                                T.op('pe', lambda e: e.matmul(pp[:n, :], lhsT=xT[:, dc, lc:lc + n], rhs=WGV[:, dc, hf * 512:(hf + 1) * 512],
                                                              start=(dc == 0), stop=(dc == 7)), reads=['WGV%d' % hf, 'xT'], writes=[ppk])
                            T.op('act', lambda e: e.copy(out=dstt[:n, hf * 512:(hf + 1) * 512], in_=pp[:n, :]), reads=[ppk], writes=[dk_])
                        if n == 128:
                            T.dma('sp', gv[col:col + n, :], vt[s_][:n, :], reads=[dk_], writes=['d:gv'], semkey='vt%d_st' % s_)
                        else:
                            _, pp, ppk = psum()
                            for dc in range(8):
                                T.op('pe', lambda e: e.matmul(pp[:n, :], lhsT=xT[:, dc, lc:lc + n], rhs=WGK[:, dc, :], start=(dc == 0), stop=(dc == 7)),
                                     reads=['WGK', 'xT'], writes=[ppk])
                            T.op('act', lambda e: e.copy(out=vs_[:n, D:D + 512], in_=pp[:n, :]), reads=[ppk], writes=['vs_'])
                            T.dma('sp', gsv[:, :], vs_[:, :], reads=['vs_'], writes=['d:gsv'], semkey='vs_st')
                T.barrier()

        def gla_rec(OTg):
            with contextlib.ExitStack() as ls:
                lsb = lambda name, shape, dtype=F32: ls.enter_context(nc.sbuf_tensor(un(name), list(shape), dtype))
                QG = lsb("QG", [128, 4, 512], BF); KGt = lsb("KGt", [128, 4, 512], BF); LA = lsb("LA", [128, 4, 512])
                VT = lsb("VT", [128, 4, D], BF); RG = lsb("RG", [128, 8, 512], BF); OF = lsb("OF", [128, 8, 512])
                Sf = lsb("Sf", [128, 4, 256]); Sb = lsb("Sb", [128, 4, 256], BF); Gs = lsb("Gs", [128, 4])
                Lt = lsb("Lt", [128, 16]); Et = lsb("Et", [128, 16]); tm4 = lsb("tm4", [128, 4])
                BC = [lsb("BC%d" % i, [128, 128]) for i in range(8)]; E1 = [lsb("E1%d" % i, [128, 128]) for i in range(8)]
                E2 = [lsb("E2%d" % i, [128, 128]) for i in range(8)]; eb = [lsb("eb%d" % i, [128, 1]) for i in range(8)]
                qd = [lsb("qd%d" % i, [128, 128], BF) for i in range(8)]; ki = [lsb("ki%d" % i, [128, 128], BF) for i in range(8)]
                ke = [lsb("ke%d" % i, [128, 128], BF) for i in range(8)]; KE = [lsb("KE%d" % i, [128, 128], BF) for i in range(8)]
                at = [lsb("at%d" % i, [128, 128], BF) for i in range(8)]
                sq2 = lsb("sq2", [128, 512]); Mt = lsb("Mt", [128, 512]); Vt = lsb("Vt", [128, 512]); tt = lsb("tt", [128, 512])
                gng = lsb("gng", [128, 8])
                with nc.allow_non_contiguous_dma(reason="tiny"):
                    T.dma('sp', gng[:], gla_norm_g.rearrange("o (c p) -> p (o c)", p=128), writes=['gng'], semkey='gng')

                def ln_gate(n, cols_out, rg_ap, rgkey):
                    for h in range(4):
                        _, ps1, ps1k = psum()
                        _, ps2, ps2k = psum()
                        for c2 in range(2):
                            c = h * 2 + c2
                            T.op('act', lambda e: e.activation(out=sq2[:, :n], in_=OF[:, c, :n], func=AF.Square), reads=['OF'], writes=['sq2'])
                            T.op('pe', lambda e: e.matmul(ps1[:, :n], lhsT=onesf[:, 0:128], rhs=OF[:, c, :n], start=(c2 == 0), stop=(c2 == 1)),
                                 reads=['onesf', 'OF'], writes=[ps1k])
                            T.op('pe', lambda e: e.matmul(ps2[:, :n], lhsT=onesf[:, 0:128], rhs=sq2[:, :n], start=(c2 == 0), stop=(c2 == 1)),
                                 reads=['onesf', 'sq2'], writes=[ps2k])
                        T.op('act', lambda e: e.activation(out=Mt[:, :n], in_=ps1[:, :n], func=AF.Copy, scale=1.0 / GDV), reads=[ps1k], writes=['Mt'])
                        T.op('dve', lambda e: e.tensor_tensor(out=Vt[:, :n], in0=Mt[:, :n], in1=Mt[:, :n], op=ALU.mult), reads=['Mt'], writes=['Vt'])
                        T.op('dve', lambda e: e.scalar_tensor_tensor(out=Vt[:, :n], in0=ps2[:, :n], scalar=1.0 / GDV, in1=Vt[:, :n], op0=ALU.mult, op1=ALU.subtract),
                             reads=[ps2k, 'Vt'], writes=['Vt'])
                        T.op('dve', lambda e: e.tensor_scalar(out=Vt[:, :n], in0=Vt[:, :n], scalar1=EPS, scalar2=None, op0=ALU.add), reads=['Vt'], writes=['Vt'])
                        T.op('act', lambda e: e.activation(out=Vt[:, :n], in_=Vt[:, :n], func=AF.Sqrt), reads=['Vt'], writes=['Vt'])
                        T.op('dve', lambda e: e.reciprocal(out=Vt[:, :n], in_=Vt[:, :n]), reads=['Vt'], writes=['Vt'])
                        for c2 in range(2):
                            c = h * 2 + c2
                            T.op('dve', lambda e: e.tensor_tensor(out=tt[:, :n], in0=OF[:, c, :n], in1=Mt[:, :n], op=ALU.subtract), reads=['OF', 'Mt'], writes=['tt'])
                            T.op('dve', lambda e: e.tensor_tensor(out=tt[:, :n], in0=tt[:, :n], in1=Vt[:, :n], op=ALU.mult), reads=['tt', 'Vt'], writes=['tt'])
                            T.op('dve', lambda e: e.scalar_tensor_tensor(out=OTg[:, c, cols_out:cols_out + n], in0=tt[:, :n], scalar=gng[:, c:c + 1],
                                                                         in1=rg_ap[:, c, :n], op0=ALU.mult, op1=ALU.mult),
                                 reads=['tt', 'gng', rgkey], writes=['OTg'])

                kk = [0]

                def run(with_out):
                    for g4 in range(4):
                        gc = g4 * 512
                        T.dma('sp', KGt[:], gkT.rearrange("h p t -> p h t")[:, :, gc:gc + 512], reads=['d:gk'], writes=['KGt'], semkey='KGt')
                        T.dma('sp', LA[:], glaT.rearrange("h p t -> p h t")[:, :, gc:gc + 512], reads=['d:gla'], writes=['LA'], semkey='LA')
                        T.dma('sp', VT[:], gv[gc:gc + 512, :].rearrange("(c p) d -> p c d", p=128), reads=['d:gv'], writes=['VT'], semkey='VT')
                        if with_out:
                            T.dma('sp', QG[:], gqT.rearrange("h p t -> p h t")[:, :, gc:gc + 512], reads=['d:gq'], writes=['QG'], semkey='QG')
                            T.dma('sp', RG[:], grT.rearrange("c p t -> p c t")[:, :, gc:gc + 512], reads=['d:gr'], writes=['RG'], semkey='RG')
                        def front(ch):
                            cs = slice(ch * 128, ch * 128 + 128)
                            HB = [(h, (ch % 2) * 4 + h) for h in range(4)]
                            for (h, b) in HB:
                                T.op('dve', lambda e: e.tensor_tensor_scan(out=BC[b][:], data0=onesf[:, 0:128], data1=LA[:, h, cs], initial=0.0,
                                                                           op0=ALU.mult, op1=ALU.add), reads=['LA', 'onesf'], writes=['BC%d' % b])
                            for (h, b) in HB:
                                T.op('act', lambda e: e.activation(out=eb[b][:], in_=BC[b][:, 127:128], func=AF.Exp), reads=['BC%d' % b], writes=['eb%d' % b])
                                T.op('act', lambda e: e.activation(out=E2[b][:], in_=BC[b][:], func=AF.Exp, scale=-1.0), reads=['BC%d' % b], writes=['E2%d' % b])
                                if with_out:
                                    T.op('act', lambda e: e.activation(out=E1[b][:], in_=BC[b][:], func=AF.Exp), reads=['BC%d' % b], writes=['E1%d' % b])
                            for (h, b) in HB:
                                T.op('dve', lambda e: e.scalar_tensor_tensor(out=ke[b][:], in0=KGt[:, h, cs], scalar=eb[b][:, 0:1], in1=E2[b][:],
                                                                             op0=ALU.mult, op1=ALU.mult), reads=['KGt', 'eb%d' % b, 'E2%d' % b], writes=['ke%d' % b])
                                if with_out:
                                    T.op('dve', lambda e: e.tensor_tensor(out=qd[b][:], in0=QG[:, h, cs], in1=E1[b][:], op=ALU.mult), reads=['QG', 'E1%d' % b], writes=['qd%d' % b])
                                    T.op('dve', lambda e: e.tensor_tensor(out=ki[b][:], in0=KGt[:, h, cs], in1=E2[b][:], op=ALU.mult), reads=['KGt', 'E2%d' % b], writes=['ki%d' % b])
                            for (h, b) in HB:
                                _, pt_, ptk = psum()
                                ptb = pt_[:].bitcast(BF)
                                T.op('pe', lambda e: e.transpose(out=ptb[0:128, 0:128], in_=ke[b][:, :], identity=identb[:, :]), reads=['ke%d' % b, 'identb'], writes=[ptk])
                                T.op('act', lambda e: e.copy(out=KE[b][:], in_=ptb[0:128, 0:128]), reads=[ptk], writes=['KE%d' % b])
                            if with_out:
                                for (h, b) in HB:
                                    _, pa, pak = psum()
                                    T.op('pe', lambda e: e.matmul(pa[0:128, 0:128], lhsT=ki[b][:, :], rhs=qd[b][:, :], start=True, stop=True),
                                         reads=['ki%d' % b, 'qd%d' % b], writes=[pak])
                                    T.op('dve', lambda e: e.tensor_tensor(out=at[b][:], in0=pa[0:128, 0:128], in1=trif[0:128, 0:128], op=ALU.mult),
                                         reads=[pak, 'trif'], writes=['at%d' % b])
                            if not with_out:
                                for (h, b) in HB:
                                    T.op('dve', lambda e: e.tensor_tensor(out=Gs[:, h:h + 1], in0=Gs[:, h:h + 1], in1=BC[b][:, 127:128], op=ALU.add),
                                         reads=['Gs%d' % h, 'BC%d' % b], writes=['Gs%d' % h])

                        def back(ch):
                            cs = slice(ch * 128, ch * 128 + 128)
                            HB = [(h, (ch % 2) * 4 + h) for h in range(4)]
                            if with_out:
                                for (h, b) in HB:
                                    for c2 in range(2):
                                        _, po, pok = psum()
                                        T.op('pe', lambda e: e.matmul(po[:, 0:128], lhsT=VT[:, ch, h * 256 + c2 * 128:h * 256 + (c2 + 1) * 128], rhs=at[b][:, :],
                                                                      start=True, stop=False), reads=['VT', 'at%d' % b], writes=[pok])
                                        T.op('pe', lambda e: e.matmul(po[:, 0:128], lhsT=Sb[:, h, c2 * 128:(c2 + 1) * 128], rhs=qd[b][:, :],
                                                                      start=False, stop=True), reads=['Sb%d' % h, 'qd%d' % b], writes=[pok])
                                        T.op('act', lambda e: e.copy(out=OF[:, h * 2 + c2, cs], in_=po[:, 0:128]), reads=[pok], writes=['OF'])
                            for (h, b) in HB:
                                _, pst, pstk = psum()
                                T.op('pe', lambda e: e.matmul(pst[:, 0:256], lhsT=KE[b][:, :], rhs=VT[:, ch, h * 256:(h + 1) * 256], start=True, stop=True),
                                     reads=['KE%d' % b, 'VT'], writes=[pstk])
                                T.op('dve', lambda e: e.scalar_tensor_tensor(out=Sf[:, h, :], in0=Sf[:, h, :], scalar=eb[b][:, 0:1], in1=pst[:, 0:256],
                                                                             op0=ALU.mult, op1=ALU.add), reads=['Sf%d' % h, 'eb%d' % b, pstk], writes=['Sf%d' % h])
                                T.op('act', lambda e: e.copy(out=Sb[:, h, :], in_=Sf[:, h, :]), reads=['Sf%d' % h], writes=['Sb%d' % h])

                        front(0)
                        for ch in range(4):
                            if ch + 1 < 4:
                                front(ch + 1)
                            back(ch)
                        if with_out:
                            ln_gate(512, gc, RG, 'RG')

                sgs = contextlib.ExitStack()
                SG = sgs.enter_context(nc.sbuf_tensor(un("SG"), [128, 4, 4, GDV + 64], F32))
                T.op('dve', lambda e: e.memset(Sf[:], 0.0), writes=['Sf0', 'Sf1', 'Sf2', 'Sf3'])
                T.op('dve', lambda e: e.memset(Sb[:], 0.0), writes=['Sb0', 'Sb1', 'Sb2', 'Sb3'])
                T.op('dve', lambda e: e.memset(Gs[:], 0.0), writes=['Gs0', 'Gs1', 'Gs2', 'Gs3'])
                run(False)
                T.dma('sp', sloc.rearrange("(h p) c -> p h c", p=128)[:, :, 0:GDV], Sf[:], reads=['Sf0', 'Sf1', 'Sf2', 'Sf3'], writes=['d:sloc'], semkey='Sf_st')
                with nc.allow_non_contiguous_dma(reason="tiny"):
                    T.dma('sp', sloc.rearrange("(h p) c -> p h c", p=128)[:, :, GDV:GDV + 1], Gs[:].rearrange("p (h o) -> p h o", o=1), reads=['Gs0', 'Gs1', 'Gs2', 'Gs3'], writes=['d:sloc2'], semkey='Gs_st')
                T.barrier()
                allgather(sloc, sg_, 'd:sloc', 'd:sg', 'cc_s')
                T.dma('sp', SG[:], sg_.rearrange("(g h p) c -> p g h c", g=4, h=4), reads=['d:sg'], writes=['SG'], semkey='SG')
                for h in range(4):
                    for rp in range(4):
                        T.op('dve', lambda e: e.tensor_tensor(out=tm4[:], in0=SG[:, :, h, GDV], in1=crank[:, 20 + rp * 4:24 + rp * 4], op=ALU.mult),
                             reads=['SG', 'crank'], writes=['tm4'])
                        T.op('dve', lambda e: e.tensor_reduce(out=Lt[:, h * 4 + rp:h * 4 + rp + 1], in_=tm4[:], axis=AX.X, op=ALU.add), reads=['tm4'], writes=['Lt'])
                T.op('act', lambda e: e.activation(out=Et[:], in_=Lt[:], func=AF.Exp), reads=['Lt'], writes=['Et'])
                for h in range(4):
                    T.op('dve', lambda e: e.tensor_tensor(out=Et[:, h * 4:h * 4 + 4], in0=Et[:, h * 4:h * 4 + 4], in1=crank[:, 36:40], op=ALU.mult),
                         reads=['Et', 'crank'], writes=['Et'])
                for h in range(4):
                    T.op('dve', lambda e: e.tensor_scalar(out=Sf[:, h, :], in0=SG[:, 0, h, 0:GDV], scalar1=Et[:, h * 4:h * 4 + 1], scalar2=None, op0=ALU.mult),
                         reads=['SG', 'Et'], writes=['Sf%d' % h])
                    for rp in range(1, 4):
                        T.op('dve', lambda e: e.scalar_tensor_tensor(out=Sf[:, h, :], in0=SG[:, rp, h, 0:GDV], scalar=Et[:, h * 4 + rp:h * 4 + rp + 1],
                                                                     in1=Sf[:, h, :], op0=ALU.mult, op1=ALU.add), reads=['SG', 'Et', 'Sf%d' % h], writes=['Sf%d' % h])
                    T.op('act', lambda e: e.copy(out=Sb[:, h, :], in_=Sf[:, h, :]), reads=['Sf%d' % h], writes=['Sb%d' % h])
                T.barrier()
                sgs.close()
                run(True)
                T.dma('sp', o_sp.rearrange("(h p) c -> p h c", p=128), Sf[:], reads=['Sf0', 'Sf1', 'Sf2', 'Sf3'], semkey='Sf_st')
                T.barrier()
                QSs = lsb("QSs", [128, 4, NS], BF); LAs = lsb("LAs", [128, 4, NS]); ELs = lsb("ELs", [128, 4, NS]); RGs = lsb("RGs", [128, 8, NS], BF)
                KVb = lsb("KVb", [1, NS, D + 512], BF)
                S0 = [lsb("S0%d" % i, [128, 256]) for i in range(2)]; Snb = [lsb("Snb%d" % i, [128, 256], BF) for i in range(2)]
                with nc.allow_non_contiguous_dma(reason="tiny"):
                    T.dma('sp', QSs[:], gqT.rearrange("h p t -> p h t")[:, :, NP_:NT], reads=['d:gq'], writes=['QSs'], semkey='QSs')
                    T.dma('sp', LAs[:], glaT.rearrange("h p t -> p h t")[:, :, NP_:NT], reads=['d:gla'], writes=['LAs'], semkey='LAs')
                    T.dma('sp', RGs[:], grT.rearrange("c p t -> p c t")[:, :, NP_:NT], reads=['d:gr'], writes=['RGs'], semkey='RGs')
                T.dma('pool', KVb[:], bass.AP(gsv.tensor, 0, [[0, 1], [D + 512, NS], [1, D + 512]]), reads=['d:gsv'], writes=['KVb'], semkey='KVb')
                T.op('act', lambda e: e.activation(out=ELs[:], in_=LAs[:], func=AF.Exp), reads=['LAs'], writes=['ELs'])
                i_ = 0
                for s_ in range(NS):
                    for h in range(4):
                        b = i_ % 2
                        i_ += 1
                        T.dma('sp', S0[b][:], sgla[s_, h], writes=['S0%d' % b], semkey='S0%d' % b)
                        _, pkv, pkvk = psum()
                        T.op('pe', lambda e: e.matmul(pkv[:, 0:256], lhsT=KVb[0:1, s_, D + h * 128:D + (h + 1) * 128], rhs=KVb[0:1, s_, h * 256:(h + 1) * 256],
                                                      start=True, stop=True), reads=['KVb'], writes=[pkvk])
                        T.op('dve', lambda e: e.scalar_tensor_tensor(out=S0[b][:], in0=S0[b][:], scalar=ELs[:, h, s_:s_ + 1], in1=pkv[:, 0:256],
                                                                     op0=ALU.mult, op1=ALU.add), reads=['S0%d' % b, 'ELs', pkvk], writes=['S0%d' % b])
                        T.dma('sp', o_ss[(s_ * 4 + h) * 128:(s_ * 4 + h + 1) * 128, :], S0[b][:], reads=['S0%d' % b], semkey='S0%d_st' % b)
                        T.op('act', lambda e: e.copy(out=Snb[b][:], in_=S0[b][:]), reads=['S0%d' % b], writes=['Snb%d' % b])
                        for c2 in range(2):
                            _, po, pok = psum()
                            T.op('pe', lambda e: e.matmul(po[:, 0:1], lhsT=Snb[b][:, c2 * 128:(c2 + 1) * 128], rhs=QSs[:, h, s_:s_ + 1], start=True, stop=True),
                                 reads=['Snb%d' % b, 'QSs'], writes=[pok])
                            T.op('act', lambda e: e.copy(out=OF[:, h * 2 + c2, s_:s_ + 1], in_=po[:, 0:1]), reads=[pok], writes=['OF'])
                ln_gate(NS, NP_, RGs, 'RGs')
                T.barrier()

        with contextlib.ExitStack() as s1:
            cT = s1.enter_context(nc.sbuf_tensor("cT", [16, NP_], F32))
            with contextlib.ExitStack() as w1:
                alloc_work(w1)
                ffn_stage(0, 0, 0, x_in, xres)
                fox_proj(cT)
            chk(1)
            with contextlib.ExitStack() as s_ot:
                OT = s_ot.enter_context(nc.sbuf_tensor("OT", [64, 16, NT], BF))
                fox_attn(cT, OT)
                chk(5)
                with contextlib.ExitStack() as w2:
                    alloc_work(w2)
                    wo_stage(OT, fox_w_o, 0, 1)
                chk(6)
        with contextlib.ExitStack() as w3:
            alloc_work(w3)
            ffn_stage(0, 1, 2, xres, xres)
            ffn_stage(1, 0, 0, xres, xres)
            chk(7)
            gla_proj()
            chk(8)
            with contextlib.ExitStack() as s2:
                OTg = s2.enter_context(nc.sbuf_tensor("OTg", [128, 8, NT], BF))
                gla_rec(OTg)
                chk(9)
                wo_stage(OTg, gla_w_o, 1, 1)
            ffn_stage(1, 1, 2, xres, o_y)
        T.finish()
    return nc


_NC = None


def kernel(x_prompt, x_sample, cache_fox_k, cache_fox_v, cache_fox_logf, state_gla, page_table,
           ln_g, ln_b, ffn_w_in, ffn_w_out, fox_w_in, fox_b_f, fox_w_o,
           gla_w_in, gla_w_a2, gla_b_a, gla_norm_g, gla_w_o):
    global _NC
    f = lambda a: np.ascontiguousarray(np.asarray(a))
    NPOOL = int(np.asarray(cache_fox_k).shape[1])
    nc = build(8, NPOOL)
    ident = np.eye(128, dtype=np.float32)
    r_ = np.arange(128)
    tri = (r_[:, None] <= r_[None, :]).astype(np.float32)
    slow = (r_[:, None] > r_[None, :]).astype(np.float32)
    ck = f(cache_fox_k).reshape(NPOOL * 64, 2 * D)
    cv = f(cache_fox_v).reshape(NPOOL * 64, 2 * D)
    clf = f(cache_fox_logf).reshape(NPOOL * 2, 64 * H)
    shared = dict(cache_k=ck, cache_v=cv, cache_lf=clf, ln_g=f(ln_g), ln_b=f(ln_b), ffn_w_in=f(ffn_w_in),
                  ffn_w_out=f(ffn_w_out), fox_w_in=f(fox_w_in)[0], fox_b_f=f(fox_b_f), fox_w_o=f(fox_w_o)[0],
                  gla_w_in=f(gla_w_in)[0], gla_w_a2=f(gla_w_a2)[0], gla_b_a=f(gla_b_a), gla_norm_g=f(gla_norm_g),
                  gla_w_o=f(gla_w_o)[0], c_ident=ident, c_tri=tri, c_slow=slow)
    xp = f(x_prompt); xs = f(x_sample); sg = f(state_gla); pt = f(page_table)
    in_maps = []
    for c in range(8):
        b, r = c // 4, c % 4
        x_in = np.concatenate([xp[b, r * NP_:(r + 1) * NP_, :], xs[4 * c:4 * c + 4, 0, :]], axis=0)
        cr = np.zeros((128, 80), np.float32)
        cr[:, 40] = np.arange(128) % 2
        cr[:, 48:80] = np.arange(32)[None, :]
        for a in range(4):
            for bb in range(4):
                cr[:, a * 4 + bb] = 1.0 if (a <= bb < r) else 0.0
                cr[:, 20 + a * 4 + bb] = 1.0 if (a < bb < r) else 0.0
            cr[:, 16 + a] = 0.0 if a < r else PEN
            cr[:, 36 + a] = 1.0 if a < r else 0.0
        m = dict(shared)
        m.update(x_in=np.ascontiguousarray(x_in), state_gla=np.ascontiguousarray(sg[0, 4 * c:4 * c + 4]),
                 page_table=np.ascontiguousarray(pt[4 * c:4 * c + 4]), c_rank=cr)
        in_maps.append(m)
    res = run_bass_kernel_spmd(nc, in_maps, core_ids=list(range(8)))
    R = res.results
    y_p = np.zeros((2, 8192, D), np.float32); y_s = np.zeros((32, 1, D), np.float32)
    k_p = np.zeros((1, 2, 8192, H, HD), np.float32); v_p = np.zeros_like(k_p); lf_p = np.zeros((1, 2, 8192, H), np.float32)
    k_s = np.zeros((1, 32, 1, H, HD), np.float32); v_s = np.zeros_like(k_s); lf_s = np.zeros((1, 32, 1, H), np.float32)
    sg_p = np.zeros((1, 2, GH, GDK, GDV), np.float32); sg_s = np.zeros((1, 32, GH, GDK, GDV), np.float32)
    for c in range(8):
        b, r = c // 4, c % 4
        sl = slice(r * NP_, (r + 1) * NP_)
        y_p[b, sl] = R[c]["o_y"][:NP_]; y_s[4 * c:4 * c + 4, 0] = R[c]["o_y"][NP_:]
        k_p[0, b, sl] = R[c]["o_k"][:NP_].reshape(NP_, H, HD); k_s[0, 4 * c:4 * c + 4, 0] = R[c]["o_k"][NP_:].reshape(4, H, HD)
        v_p[0, b, sl] = R[c]["o_v"][:NP_].reshape(NP_, H, HD); v_s[0, 4 * c:4 * c + 4, 0] = R[c]["o_v"][NP_:].reshape(4, H, HD)
        lf_p[0, b, sl] = R[c]["o_lf"][:NP_]; lf_s[0, 4 * c:4 * c + 4, 0] = R[c]["o_lf"][NP_:]
        if r == 3:
            sg_p[0, b] = R[c]["o_sp"].reshape(GH, GDK, GDV)
        sg_s[0, 4 * c:4 * c + 4] = R[c]["o_ss"].reshape(4, GH, GDK, GDV)
    return (y_p, y_s, k_p, v_p, lf_p, k_s, v_s, lf_s, sg_p, sg_s)
```
